# Optimizing a Trainium2 kernel written in Bass

```python
import math
import jax, jax.numpy as jnp
from jax import lax
import numpy as np

D_MODEL = 1024
BATCH = 4
SEQ = 8192
DEPTH = 1

PLE_DIM = 256
D_FF = 2816
MB_HEADS = 8
MB_HEAD_DIM = 64
MB_WIDTH = MB_HEADS * MB_HEAD_DIM
MB_BLOCK = 256
MB_TOPK = 3
MB_QCHUNK = 64
HG_HEADS = 4
HG_DK = 128
HG_DV = 128
HG_KEY_WIDTH = HG_HEADS * HG_DK
HG_WIDTH = HG_HEADS * HG_DV
HG_CHUNK = 64
REL_BUCKETS = 32
REL_MAX_EXACT = REL_BUCKETS // 2
REL_MAX_DIST = 128
SPLIT_SIZES = (MB_WIDTH, MB_WIDTH, MB_WIDTH, HG_KEY_WIDTH, HG_KEY_WIDTH, HG_WIDTH, HG_WIDTH, D_MODEL, D_MODEL)
IN_COLS = 3 * MB_WIDTH + 2 * HG_KEY_WIDTH + 2 * HG_WIDTH + 2 * D_MODEL
N_NORMS = 7
EPS = 1e-6

kernel_name = "moba_hgrn2_gated_hybrid_macaron"


def rmsnorm(x, w):
    xf = x.astype(jnp.float32)
    y = xf * lax.rsqrt(jnp.mean(xf * xf, axis=-1, keepdims=True) + EPS)
    return y.astype(x.dtype) * w


def swiglu(x, w_gu, w_down):
    g, u = jnp.split(x @ w_gu, 2, axis=-1)
    return (jax.nn.silu(g) * u) @ w_down


def rel_bucket(n):
    n = jnp.maximum(n, 0)
    nf = jnp.maximum(n, 1).astype(jnp.float32)
    large = REL_MAX_EXACT + (jnp.log(nf / REL_MAX_EXACT) / math.log(REL_MAX_DIST / REL_MAX_EXACT)
                             * (REL_BUCKETS - REL_MAX_EXACT)).astype(jnp.int32)
    large = jnp.minimum(large, REL_BUCKETS - 1)
    return jnp.where(n < REL_MAX_EXACT, n, large)


def moba_attention(q, k, v, rel_table):
    B, S, H, Dh = q.shape
    nb = -(-S // MB_BLOCK)
    s_pad = nb * MB_BLOCK
    pad = ((0, 0), (0, s_pad - S), (0, 0), (0, 0))
    q = jnp.pad(q, pad).transpose(0, 2, 1, 3)
    k = jnp.pad(k, pad).transpose(0, 2, 1, 3)
    v = jnp.pad(v, pad).transpose(0, 2, 1, 3)
    kb = k.reshape(B, H, nb, MB_BLOCK, Dh)
    vb = v.reshape(B, H, nb, MB_BLOCK, Dh)
    scale = Dh ** -0.5

    kmean = jnp.mean(kb.astype(jnp.float32), axis=3)
    gate = jnp.einsum('bhtd,bhnd->bhtn', q.astype(jnp.float32), kmean)
    pos = jnp.arange(s_pad)
    own = pos // MB_BLOCK
    past = jnp.arange(nb)[None, :] < own[:, None]
    gate = jnp.where(past[None, None], gate, -jnp.inf)
    kk = min(MB_TOPK, nb)
    _, top_idx = lax.top_k(gate, kk)
    valid = jnp.arange(kk)[None, :] < own[:, None]

    nq = s_pad // MB_QCHUNK
    qc = q.reshape(B, H, nq, MB_QCHUNK, Dh).transpose(2, 0, 1, 3, 4)
    idxc = top_idx.reshape(B, H, nq, MB_QCHUNK, kk).transpose(2, 0, 1, 3, 4)
    validc = valid.reshape(nq, MB_QCHUNK, kk)
    chunk_ids = jnp.arange(nq)
    bidx = jnp.arange(B)[:, None, None, None]
    hidx = jnp.arange(H)[None, :, None, None]
    hsel = jnp.arange(H)[None, :, None, None, None]
    blk_off = jnp.arange(MB_BLOCK)

    def step(args):
        qi, ii, vi, c = args
        qpos = c * MB_QCHUNK + jnp.arange(MB_QCHUNK)
        blk = (c * MB_QCHUNK) // MB_BLOCK
        k_own = lax.dynamic_index_in_dim(kb, blk, axis=2, keepdims=False)
        v_own = lax.dynamic_index_in_dim(vb, blk, axis=2, keepdims=False)
        dist_own = qpos[:, None] - (blk * MB_BLOCK + blk_off)[None, :]
        bias_own = jnp.moveaxis(rel_table[rel_bucket(dist_own)], -1, 0)[None]
        s_own = jnp.einsum('bhqd,bhkd->bhqk', qi, k_own).astype(jnp.float32) * scale + bias_own
        s_own = jnp.where((dist_own >= 0)[None, None], s_own, -jnp.inf)
        k_sel = kb[bidx, hidx, ii]
        v_sel = vb[bidx, hidx, ii]
        kpos_sel = ii[..., None] * MB_BLOCK + blk_off
        dist_sel = qpos[None, None, :, None, None] - kpos_sel
        bias_sel = rel_table[rel_bucket(dist_sel), hsel]
        s_sel = jnp.einsum('bhqd,bhqjkd->bhqjk', qi, k_sel).astype(jnp.float32) * scale + bias_sel
        s_sel = jnp.where(vi[None, None, :, :, None], s_sel, -jnp.inf)
        s = jnp.concatenate([s_own, s_sel.reshape(B, H, MB_QCHUNK, kk * MB_BLOCK)], axis=-1)
        w = jax.nn.softmax(s, axis=-1)
        w_own = w[..., :MB_BLOCK].astype(v.dtype)
        w_sel = w[..., MB_BLOCK:].reshape(B, H, MB_QCHUNK, kk, MB_BLOCK).astype(v.dtype)
        return (jnp.einsum('bhqk,bhkd->bhqd', w_own, v_own)
                + jnp.einsum('bhqjk,bhqjkd->bhqd', w_sel, v_sel))

    out = lax.map(step, (qc, idxc, validc, chunk_ids))
    out = out.transpose(1, 0, 3, 2, 4).reshape(B, s_pad, H, Dh)[:, :S]
    return out.reshape(B, S, H * Dh)


def hgrn2_scan(q, k, v, logf):
    B, S, H, DK = q.shape
    DV = v.shape[-1]
    n = S // HG_CHUNK

    def to_chunks(a):
        return a.astype(jnp.float32).reshape(B, n, HG_CHUNK, H, a.shape[-1]).transpose(1, 0, 3, 2, 4)

    qc, kc, vc, fc = to_chunks(q), to_chunks(k), to_chunks(v), to_chunks(logf)
    causal = jnp.tril(jnp.ones((HG_CHUNK, HG_CHUNK), dtype=bool))

    def step(state, xs):
        qi, ki, vi, fi = xs
        b = jnp.cumsum(fi, axis=2)
        o_inter = jnp.einsum('bhck,bhkv->bhcv', qi * jnp.exp(b), state)
        diff = b[:, :, :, None, :] - b[:, :, None, :, :]
        decay = jnp.exp(jnp.where(causal[None, None, :, :, None], diff, -jnp.inf))
        attn = jnp.einsum('bhtk,bhsk,bhtsk->bhts', qi, ki, decay)
        o_intra = jnp.einsum('bhts,bhsv->bhtv', attn, vi)
        b_last = b[:, :, -1:, :]
        new_state = (state * jnp.exp(b_last[:, :, 0, :, None])
                     + jnp.einsum('bhsk,bhsv->bhkv', ki * jnp.exp(b_last - b), vi))
        return new_state, o_inter + o_intra

    state0 = jnp.zeros((B, H, DK, DV), jnp.float32)
    _, ys = lax.scan(step, state0, (qc, kc, vc, fc))
    return ys.transpose(1, 0, 3, 2, 4).reshape(B, S, H, DV).astype(q.dtype)


def setup_inputs(seed: int = 0) -> dict:
    key = jax.random.key(seed)
    ks = jax.random.split(key, 16)
    f32 = jnp.float32

    def nrm(k, shape, scale):
        return jax.random.normal(k, shape, f32) * scale

    return {
        "x": nrm(ks[0], (BATCH, SEQ, D_MODEL), 1.0),
        "p": nrm(ks[1], (DEPTH, BATCH, SEQ, PLE_DIM), 1.0),
        "w_ffn1_gu": nrm(ks[2], (DEPTH, D_MODEL, 2 * D_FF), D_MODEL ** -0.5),
        "w_ffn1_down": nrm(ks[3], (DEPTH, D_FF, D_MODEL), D_FF ** -0.5),
        "w_in": nrm(ks[4], (DEPTH, D_MODEL, IN_COLS), D_MODEL ** -0.5),
        "w_branch_a": nrm(ks[5], (DEPTH, MB_WIDTH, D_MODEL), MB_WIDTH ** -0.5),
        "w_branch_b": nrm(ks[6], (DEPTH, HG_WIDTH, D_MODEL), HG_WIDTH ** -0.5),
        "w_out": nrm(ks[7], (DEPTH, D_MODEL, D_MODEL), D_MODEL ** -0.5),
        "w_ffn2_gu": nrm(ks[8], (DEPTH, D_MODEL, 2 * D_FF), D_MODEL ** -0.5),
        "w_ffn2_down": nrm(ks[9], (DEPTH, D_FF, D_MODEL), D_FF ** -0.5),
        "w_ple": nrm(ks[10], (DEPTH, PLE_DIM, D_MODEL), PLE_DIM ** -0.5),
        "w_ple_gate": nrm(ks[11], (DEPTH, D_MODEL, D_MODEL), D_MODEL ** -0.5),
        "norm_gains": 1.0 + nrm(ks[12], (DEPTH, N_NORMS, D_MODEL), 0.1),
        "hg_norm_w": 1.0 + nrm(ks[13], (DEPTH, HG_DV), 0.1),
        "lb_param": nrm(ks[14], (DEPTH + 1, HG_KEY_WIDTH), 0.5),
        "rel_table": nrm(ks[15], (REL_BUCKETS, MB_HEADS), 0.5),
    }


def reference(x, p, w_ffn1_gu, w_ffn1_down, w_in, w_branch_a, w_branch_b, w_out,
              w_ffn2_gu, w_ffn2_down, w_ple, w_ple_gate, norm_gains, hg_norm_w,
              lb_param, rel_table):
    B, S, _ = x.shape
    offsets = np.cumsum(SPLIT_SIZES)[:-1].tolist()
    lbs = jnp.cumsum(jax.nn.softmax(lb_param.astype(jnp.float32), axis=0), axis=0)
    h = x
    for i in range(DEPTH):
        g = norm_gains[i]
        h = h + 0.5 * rmsnorm(swiglu(rmsnorm(h, g[0]), w_ffn1_gu[i], w_ffn1_down[i]), g[1])

        u = rmsnorm(h, g[2])
        z = u @ w_in[i]
        mq, mk, mv, hq, hf, hi, hg, ga, gb = jnp.split(z, offsets, axis=-1)

        o_a = moba_attention(mq.reshape(B, S, MB_HEADS, MB_HEAD_DIM),
                             mk.reshape(B, S, MB_HEADS, MB_HEAD_DIM),
                             mv.reshape(B, S, MB_HEADS, MB_HEAD_DIM), rel_table)

        lb = lbs[i].reshape(HG_HEADS, HG_DK)
        f = lb + (1.0 - lb) * jax.nn.sigmoid(hf.astype(jnp.float32).reshape(B, S, HG_HEADS, HG_DK))
        logf = jnp.log(f)
        k_in = (1.0 - f).astype(x.dtype)
        o_b = hgrn2_scan(jax.nn.silu(hq).reshape(B, S, HG_HEADS, HG_DK), k_in,
                         hi.reshape(B, S, HG_HEADS, HG_DV), logf)
        o_b = rmsnorm(o_b, hg_norm_w[i]).reshape(B, S, HG_WIDTH) * jax.nn.silu(hg)

        merged = (jax.nn.sigmoid(ga) * (o_a @ w_branch_a[i])
                  + jax.nn.sigmoid(gb) * (o_b @ w_branch_b[i]))
        h = h + rmsnorm(merged @ w_out[i], g[3])

        h = h + 0.5 * rmsnorm(swiglu(rmsnorm(h, g[4]), w_ffn2_gu[i], w_ffn2_down[i]), g[5])

        e = p[i] @ w_ple[i]
        h = h + rmsnorm(jax.nn.sigmoid(h @ w_ple_gate[i]) * e, g[6])
    return h
```

```python
import numpy as np
import concourse.bass as bass
import concourse.mybir as mybir
from concourse.bass_utils import run_bass_kernel_spmd
from contextlib import ExitStack

F32 = mybir.dt.float32
BF16 = mybir.dt.bfloat16
ALU = mybir.AluOpType
AF = mybir.ActivationFunctionType
AX = mybir.AxisListType

COMPUTE = ("pe", "act", "dve", "pool")
ENGS = ("pe", "act", "dve", "pool", "sp")
NDMASEM = 6

D = 1024
DFF = 2816
T = 512
NCTX = 4096
NOWN = 4096
NTOK = NCTX + NOWN
EPS = 1e-6
NEG = -30000.0


class _Op:
    __slots__ = ("eng", "fn", "deps", "dma", "sig", "ticket", "semi", "semval")

    def __init__(self, eng, fn, dma):
        self.eng = eng
        self.fn = fn
        self.dma = dma
        self.deps = []
        self.sig = False
        self.ticket = None
        self.semi = None
        self.semval = None


class Sched:
    def __init__(self, nc, stack):
        self.nc = nc
        self.pending = {e: [] for e in ENGS}
        self.lastw = {}
        self.readers = {}
        self.tick = {e: 0 for e in COMPUTE}
        self.sem = {e: stack.enter_context(nc.semaphore("s_" + e)) for e in COMPUTE}
        self.dsem = {}
        self.dcnt = {}
        self.dlast = {}
        for q in ("sp", "act", "pool"):
            self.dsem[q] = [stack.enter_context(nc.semaphore("d_%s%d" % (q, i))) for i in range(NDMASEM)]
            self.dcnt[q] = 0
            self.dlast[q] = [None] * NDMASEM
        self.waited = {}
        self.lastreal = {e: None for e in COMPUTE}

    def add(self, eng, fn, reads=(), writes=(), dma=False):
        op = _Op(eng, fn, dma)
        deps = []
        for k in reads:
            w = self.lastw.get(k)
            if w is not None:
                deps.append((w, "raw"))
        for k in writes:
            w = self.lastw.get(k)
            if w is not None:
                deps.append((w, "waw"))
            for r in self.readers.get(k, ()):
                deps.append((r, "war"))
        seen = set()
        for p, kind in deps:
            if p is op or id(p) in seen:
                continue
            if not p.dma and p.eng == eng and not dma:
                if eng == "pe" or kind != "raw":
                    continue
            seen.add(id(p))
            op.deps.append(p)
            if not p.dma:
                p.sig = True
        if dma:
            q = eng
            i = self.dcnt[q] % NDMASEM
            self.dcnt[q] += 1
            prev = self.dlast[q][i]
            if prev is not None and id(prev) not in seen:
                op.deps.append(prev)
            op.semi = (q, i)
            op.semval = 16 * ((self.dcnt[q] - 1) // NDMASEM + 1)
            self.dlast[q][i] = op
        else:
            self.lastreal[eng] = op
        for k in reads:
            self.readers.setdefault(k, []).append(op)
        for k in writes:
            self.lastw[k] = op
            self.readers[k] = []
        self.pending[eng].append(op)
        return op

    def pe(self, fn, r=(), w=()):
        return self.add("pe", fn, r, w)

    def act(self, fn, r=(), w=()):
        return self.add("act", fn, r, w)

    def dve(self, fn, r=(), w=()):
        return self.add("dve", fn, r, w)

    def pool(self, fn, r=(), w=()):
        return self.add("pool", fn, r, w)

    def dma(self, out, in_, r=(), w=(), q="sp", **kw):
        return self.add(q, lambda e: e.dma_start(out=out, in_=in_, **kw), r, w, dma=True)

    def barrier(self):
        deps = []
        for e in COMPUTE:
            p = self.lastreal[e]
            if p is not None:
                p.sig = True
                deps.append(p)
        for q in self.dlast:
            for p in self.dlast[q]:
                if p is not None:
                    deps.append(p)
        for e in ENGS:
            w = _Op(e, None, False)
            w.deps = [p for p in deps if (p.dma or p.eng != e)]
            self.pending[e].append(w)
        self.lastw = {}
        self.readers = {}

    def flush(self):
        nc = self.nc
        for e in COMPUTE:
            for op in self.pending[e]:
                if op.sig and op.ticket is None:
                    self.tick[e] += 1
                    op.ticket = self.tick[e]
        pend = self.pending
        self.pending = {e: [] for e in ENGS}
        sched = self

        def emit(engname, engine):
            for op in pend[engname]:
                for p in op.deps:
                    if p.dma:
                        s = sched.dsem[p.semi[0]][p.semi[1]]
                        key = (engname, "d", p.semi)
                        val = p.semval
                    else:
                        assert p.ticket is not None
                        s = sched.sem[p.eng]
                        key = (engname, "c", p.eng)
                        val = p.ticket
                    if sched.waited.get(key, 0) >= val:
                        continue
                    sched.waited[key] = val
                    engine.wait_ge(s, val)
                if op.fn is None:
                    continue
                ins = op.fn(engine)
                if op.dma:
                    ins.then_inc(sched.dsem[op.semi[0]][op.semi[1]], 16)
                elif op.sig:
                    ins.then_inc(sched.sem[op.eng], 1)

        with nc.Block() as block:
            @block.tensor
            def _(e):
                emit("pe", e)

            @block.scalar
            def _(e):
                emit("act", e)

            @block.vector
            def _(e):
                emit("dve", e)

            @block.gpsimd
            def _(e):
                emit("pool", e)

            @block.sync
            def _(e):
                emit("sp", e)


WSPEC = [
    ("w_ffn1_gu", D, 2 * DFF), ("w_ffn1_down", DFF, D), ("w_in", D, 5632),
    ("w_branch_a", 512, D), ("w_branch_b", 512, D), ("w_out", D, D),
    ("w_ffn2_gu", D, 2 * DFF), ("w_ffn2_down", DFF, D), ("w_ple", 256, D), ("w_ple_gate", D, D),
]


class K:
    pass


def build_nc(debug=(), phases="ABCD"):
    nc = bass.Bass("TRN2", target_bir_lowering=False)
    k = K()
    k.nc = nc
    din = lambda n, s, d=F32: nc.dram_tensor(n, list(s), d, kind="ExternalInput").ap()

    def dscr(n, s, d):
        kind = "ExternalOutput" if n in debug else "Internal"
        return nc.dram_tensor(n, list(s), d, kind=kind).ap()

    k.xc = din("xc", [NTOK, D])
    k.pc = din("pc", [NOWN, 256])
    k.w = {n: din(n, [a, b]) for n, a, b in WSPEC}
    k.gainsT = din("gainsT", [128, 56])
    k.hgw = din("hg_norm_w", [1, 128])
    k.lbp = din("lb_param", [2, 512])
    k.rel = din("rel_table", [32, 8])
    k.ident = din("ident", [128, 128])
    k.out = nc.dram_tensor("out", [NOWN, D], F32, kind="ExternalOutput").ap()

    k.wb = {n: dscr("wb_" + n, [a, b], BF16) for n, a, b in WSPEC}
    k.H1T = dscr("H1T", [128, 8, NOWN], F32)
    k.QT = dscr("QT", [4, 128, NOWN], BF16)
    k.KT = dscr("KT", [4, 128, NTOK], BF16)
    k.V = dscr("V", [NTOK, 520], BF16)
    k.HF = dscr("HF", [NTOK, 512], F32)
    k.HI = dscr("HI", [NTOK, 512], BF16)
    k.HQ = dscr("HQ", [NOWN, 512], F32)
    k.HG = dscr("HG", [NOWN, 512], F32)
    k.SGA = dscr("SGA", [8, 128, NOWN], BF16)
    k.SGB = dscr("SGB", [8, 128, NOWN], BF16)
    k.OAT = dscr("OAT", [4, 128, NOWN], BF16)
    k.OBT = dscr("OBT", [4, 128, NOWN], BF16)
    k.TVH = dscr("TVH", [8, 768], BF16)
    k.TVL = dscr("TVL", [8, 768], BF16)
    k.cJ = din("cJ", [128, 128])
    k.cOhp = din("cOhp", [32, 768])
    k.cCm = din("cCm", [8, 768])
    k.cKaug = din("cKaug", [32, NTOK])
    k.gmask = din("gmask", [128, 16, 32])
    k.cOwnoh = din("cOwnoh", [128, 16, 32])
    k.cM2 = din("cM2", [128, 128])
    k.cSel = din("cSel", [128, 4])
    k.cTri = din("cTri", [128, 64])

    with ExitStack() as st0:
        S = Sched(nc, st0)
        k.S = S
        k.ps = [st0.enter_context(nc.psum_tensor("ps%d" % i, [128, 512], F32)) for i in range(8)]
        k.bankctr = 0
        k.bank_excl = set()
        sb0 = lambda n, s, d: st0.enter_context(nc.sbuf_tensor(n, list(s), d))
        k.identf = sb0("identf", [128, 128], F32)
        k.identb = sb0("identb", [128, 128], BF16)
        k.onesb = sb0("onesb", [128, 128], BF16)
        k.gT = sb0("gT", [128, 56], F32)
        k.gH = sb0("gH", [128, 56], F32)
        S.dma(k.identf[:], k.ident, w=["identf"])
        S.dma(k.gT[:], k.gainsT, w=["gT"])
        S.dve(lambda e: e.tensor_copy(k.identb[:], k.identf[:]), r=["identf"], w=["identb"])
        S.pool(lambda e: e.memset(k.onesb[:], 1.0), w=["onesb"])
        S.dve(lambda e: e.tensor_scalar_mul(k.gH[:], k.gT[:], 0.5), r=["gT"], w=["gH"])
        for n, a, b in WSPEC:
            if n == "w_ffn1_gu":
                for (n0, nc_) in [(c * 512, 512) for c in range(5)] + [(2560, 256)]:
                    for off in (0, DFF):
                        S.dma(k.wb[n][:, off + n0:off + n0 + nc_], k.w[n][:, off + n0:off + n0 + nc_], w=[("wbc", n, off + n0)], q="pool")
                continue
            step = 256
            for r0 in range(0, a, step):
                r1 = min(a, r0 + step)
                S.dma(k.wb[n][r0:r1, :], k.w[n][r0:r1, :], w=[("wb", n, r0)], q="pool")
        k.wrows = {n: a for n, a, b in WSPEC}
        if "A" in phases:
            phase_a(k)
            S.barrier()
            S.flush()
        if "B" in phases:
            phase_b(k)
            S.barrier()
            S.flush()
        if "C" in phases:
            phase_c(k)
            S.barrier()
            S.flush()
        if "D" in phases:
            phase_d(k)
        S.barrier()
        S.flush()
    return nc


def wkeys(k, n, kc0, nkc, n0=None):
    if n == "w_ffn1_gu":
        return [("wbc", n, n0)]
    r0 = kc0 * 128
    r1 = (kc0 + nkc) * 128
    return [("wb", n, r) for r in range(0, k.wrows[n], 256) if r < r1 and r + 256 > r0]


def load_w(k, n, kc0, nkc, n0, ncols):
    S = k.S
    slot = k.wctr % k.NW
    k.wctr += 1
    wt = k.wring[slot]
    src = k.wb[n][kc0 * 128:(kc0 + nkc) * 128, n0:n0 + ncols].rearrange("(kc p) n -> p kc n", p=128)
    S.dma(wt[:, 0:nkc, 0:ncols], src, r=wkeys(k, n, kc0, nkc, n0), w=[("w", slot)])
    return wt, ("w", slot)


def rms_stats(k, src, srckey, rs, rskey):
    S = k.S
    for kc in range(8):
        S.act(lambda e, kc=kc: e.activation(k.sq[:, kc, :], src[:, kc, :], AF.Square), r=[srckey], w=[("sq", kc)])
    b = nbank(k)
    for kc in range(8):
        S.pe(lambda e, kc=kc, b=b: e.matmul(k.ps[b][:], lhsT=k.onesb[:], rhs=k.sq[:, kc, :], start=(kc == 0), stop=(kc == 7)),
             r=[("sq", kc), "onesb"], w=[("ps", b)])
    S.act(lambda e, b=b: e.activation(rs[:], k.ps[b][:], AF.Sqrt, scale=1.0 / D, bias=EPS), r=[("ps", b)], w=[rskey])
    S.dve(lambda e: e.reciprocal(rs[:], rs[:]), r=[rskey], w=[rskey])


def norm_apply(k, src, srckey, rs, rskey, gcol, dst, dstkey):
    S = k.S
    for kc in range(8):
        S.dve(lambda e, kc=kc: e.scalar_tensor_tensor(out=dst[:, kc, :], in0=src[:, kc, :], scalar=k.gT[:, gcol * 8 + kc:gcol * 8 + kc + 1],
                                                       in1=rs[:], op0=ALU.mult, op1=ALU.mult),
              r=[srckey, rskey, "gT"], w=[(dstkey, kc)])


def nbank(k):
    while True:
        b = k.bankctr % 8
        k.bankctr += 1
        if b not in k.bank_excl:
            return b


def run_hooks(hooks, n):
    for _ in range(n):
        if hooks:
            hooks.pop(0)()


def stats_thunks(k, src, srckey, rs, rskey):
    S = k.S
    th = []
    for kc in range(8):
        th.append(lambda kc=kc: S.act(lambda e: e.activation(k.sq[:, kc, :], src[:, kc, :], AF.Square), r=[(srckey, kc)], w=[("sq", kc)]))

    def mm():
        b = nbank(k)
        for kc in range(8):
            S.pe(lambda e, kc=kc: e.matmul(k.ps[b][:], lhsT=k.onesb[:], rhs=k.sq[:, kc, :], start=(kc == 0), stop=(kc == 7)),
                 r=[("sq", kc), "onesb"], w=[("ps", b)])
        S.act(lambda e: e.activation(rs[:], k.ps[b][:], AF.Sqrt, scale=1.0 / D, bias=EPS), r=[("ps", b)], w=[rskey])
        S.dve(lambda e: e.reciprocal(rs[:], rs[:]), r=[rskey], w=[rskey])
    th.append(mm)
    return th


def ffn_gu(k, wgu, xn, xnkey, hooks=None):
    S = k.S
    sched_hooks = hooks if isinstance(hooks, dict) else None
    hooks = [] if (hooks is None or sched_hooks is not None) else hooks
    per = -(-len(hooks) // 18) if hooks else 0
    chunks = [(c * 512, 512) for c in range(5)] + [(2560, 256)]
    for (n0, nc_) in chunks:
        wg, wgk = load_w(k, wgu, 0, 8, n0, nc_)
        wu, wuk = load_w(k, wgu, 0, 8, DFF + n0, nc_)
        for j in range(nc_ // 128):
            jb = (n0 // 128) + j
            bg = nbank(k)
            bu = nbank(k)
            for kc in range(8):
                S.pe(lambda e, kc=kc, j=j, bg=bg, wg=wg: e.matmul(k.ps[bg][:], lhsT=wg[:, kc, j * 128:(j + 1) * 128], rhs=xn[:, kc, :], start=(kc == 0), stop=(kc == 7)),
                     r=[wgk, (xnkey, kc)], w=[("ps", bg)])
            for kc in range(8):
                S.pe(lambda e, kc=kc, j=j, bu=bu, wu=wu: e.matmul(k.ps[bu][:], lhsT=wu[:, kc, j * 128:(j + 1) * 128], rhs=xn[:, kc, :], start=(kc == 0), stop=(kc == 7)),
                     r=[wuk, (xnkey, kc)], w=[("ps", bu)])
            sl = jb % 2
            S.act(lambda e, bg=bg, sl=sl: e.activation(k.silu[sl][:], k.ps[bg][:], AF.Silu), r=[("ps", bg)], w=[("silu", sl)])
            S.dve(lambda e, bu=bu, sl=sl, jb=jb: e.tensor_tensor(out=k.aT[:, jb, :], in0=k.silu[sl][:], in1=k.ps[bu][:], op=ALU.mult),
                  r=[("ps", bu), ("silu", sl)], w=[("aT", jb)])
            run_hooks(hooks, per)
            if sched_hooks is not None:
                for fn in sched_hooks.pop(jb, ()):
                    fn()
    run_hooks(hooks, len(hooks))
    if sched_hooks is not None:
        for jb in sorted(sched_hooks):
            for fn in sched_hooks[jb]:
                fn()


def ffn_down(k, wdown, yT, ykey, hooks=None, coarse=False):
    S = k.S
    hooks = hooks if hooks is not None else []
    per = -(-len(hooks) // 5) if hooks else 0
    for half in range(2):
        banks = [nbank(k) for _ in range(4)]
        k.bank_excl = set(banks)
        groups = [(0, 8), (8, 8), (16, 6)]
        for (kc0, nkc) in groups:
            wd, wdk = load_w(k, wdown, kc0, nkc, half * 512, 512)
            for j in range(4):
                for kk in range(nkc):
                    kc = kc0 + kk
                    S.pe(lambda e, j=j, kk=kk, kc=kc, wd=wd, b=banks[j]: e.matmul(k.ps[b][:], lhsT=wd[:, kk, j * 128:(j + 1) * 128], rhs=k.aT[:, kc, :],
                                                                                 start=(kc == 0), stop=(kc == 21)),
                         r=[wdk, ("aT", kc)], w=[("ps", banks[j])])
            if not (half == 1 and kc0 == 16):
                run_hooks(hooks, per)
        k.bank_excl = set()
        for j in range(4):
            S.act(lambda e, j=j, b=banks[j], half=half: e.copy(yT[:, half * 4 + j, :], k.ps[b][:]), r=[("ps", banks[j])], w=[ykey if coarse else (ykey, half * 4 + j)])
    run_hooks(hooks, len(hooks))


def ffn(k, wgu, wdown, xn, xnkey, yT, ykey):
    ffn_gu(k, wgu, xn, xnkey)
    ffn_down(k, wdown, yT, ykey, coarse=True)


def phase_a(k):
    nc = k.nc
    S = k.S
    with ExitStack() as st:
        sb = lambda n, s, d: st.enter_context(nc.sbuf_tensor(n, list(s), d))
        k.NW = 5
        k.wctr = 0
        k.wring = [sb("wring%d" % i, [128, 8, 512], BF16) for i in range(k.NW)]
        xtok = sb("xtok", [128, 4, D], F32)
        xTs = [sb("xT%d" % i, [128, 8, T], F32) for i in range(2)]
        xnA = sb("xnA", [128, 8, T], BF16)
        xnB = sb("xnB", [128, 8, T], BF16)
        k.sq = sb("sq", [128, 8, T], BF16)
        k.aT = sb("aT", [128, 22, T], BF16)
        yT = sb("yT", [128, 8, T], F32)
        k.silu = [sb("silu%d" % i, [128, T], F32) for i in range(2)]
        rs = sb("rs", [128, T], F32)
        fmst = [sb("fmst%d" % i, [128, 4, T], BF16) for i in range(1)]
        tmst_b = [sb("tmstb%d" % i, [128, 4, 512], BF16) for i in range(2)]
        tmst_f = [sb("tmstf%d" % i, [128, 4, 512], F32) for i in range(1)]
        vst = [sb("vst%d" % i, [128, 4, 520], BF16) for i in range(2)]
        for i in range(2):
            S.pool(lambda e, i=i: e.memset(vst[i][:], 1.0), w=[("vst", i)])
        cn = {"fm": 0, "tmb": 0, "tmf": 0, "v": 0}
        NT = NTOK // T

        def front_thunks(ti):
            xT = xTs[ti % 2]
            xk = "xT%d" % (ti % 2)
            t0 = ti * T
            th = []
            th.append(lambda: S.dma(xtok[:], k.xc[t0:t0 + T, :].rearrange("(s p) d -> p s d", p=128), w=["xtok"]))

            def tr(kc):
                b = nbank(k)
                for s in range(4):
                    S.pe(lambda e, s=s: e.transpose(k.ps[b][:, s * 128:(s + 1) * 128], xtok[:, s, kc * 128:(kc + 1) * 128], k.identf[:]),
                         r=["xtok", "identf"], w=[("ps", b)])
                S.act(lambda e: e.copy(xT[:, kc, :], k.ps[b][:]), r=[("ps", b)], w=[(xk, kc)])
            for kc in range(8):
                th.append(lambda kc=kc: tr(kc))
            th += stats_thunks(k, xT, xk, rs, "rs")
            for kc in range(8):
                th.append(lambda kc=kc: S.dve(lambda e: e.scalar_tensor_tensor(out=xnA[:, kc, :], in0=xT[:, kc, :], scalar=k.gT[:, kc:kc + 1], in1=rs[:],
                                                                                op0=ALU.mult, op1=ALU.mult), r=[(xk, kc), "rs", "gT"], w=[("xnA", kc)]))
            return th

        def mid_thunks(ti):
            xT = xTs[ti % 2]
            xk = "xT%d" % (ti % 2)
            own = ti >= NCTX // T
            to = ti * T - NCTX
            th = stats_thunks(k, yT, "yT", rs, "rs")

            def upd(kc):
                S.dve(lambda e: e.tensor_tensor(out=yT[:, kc, :], in0=yT[:, kc, :], in1=rs[:], op=ALU.mult), r=[("yT", kc), "rs"], w=[("yT", kc)])
                S.dve(lambda e: e.scalar_tensor_tensor(out=xT[:, kc, :], in0=yT[:, kc, :], scalar=k.gH[:, 8 + kc:8 + kc + 1], in1=xT[:, kc, :],
                                                       op0=ALU.mult, op1=ALU.add), r=[("yT", kc), (xk, kc), "gH"], w=[(xk, kc)])
            for kc in range(8):
                th.append(lambda kc=kc: upd(kc))
            if own:
                th.append(lambda: S.dma(k.H1T[:, :, to:to + T], xT[:], r=[(xk, kc) for kc in range(8)], w=[("H1T", ti)], q="pool"))
            th += stats_thunks(k, xT, xk, rs, "rs")
            for kc in range(8):
                th.append(lambda kc=kc: S.dve(lambda e: e.scalar_tensor_tensor(out=xnB[:, kc, :], in0=xT[:, kc, :], scalar=k.gT[:, 16 + kc:16 + kc + 1], in1=rs[:],
                                                                                op0=ALU.mult, op1=ALU.mult), r=[(xk, kc), "rs", "gT"], w=[("xnB", kc)]))
            return th

        def win(ti):
            t0 = ti * T
            own = ti >= NCTX // T
            to = t0 - NCTX
            xn = xnB
            xnk = [("xnB", kc) for kc in range(8)]
            fm_groups = [("k", 512, k.KT, t0)]
            if own:
                fm_groups = [("q", 0, k.QT, to), ("k", 512, k.KT, t0), ("ga", 3584, k.SGA, to), ("ga", 4096, k.SGA, to),
                             ("gb", 4608, k.SGB, to), ("gb", 5120, k.SGB, to)]
            for (kind, n0, dst, tt) in fm_groups:
                wt, wk = load_w(k, "w_in", 0, 8, n0, 512)
                st_ = fmst[cn["fm"] % len(fmst)]
                stk = ("fmst", cn["fm"] % len(fmst))
                cn["fm"] += 1
                for j in range(4):
                    b = nbank(k)
                    for kc in range(8):
                        S.pe(lambda e, kc=kc, j=j, b=b, wt=wt: e.matmul(k.ps[b][:], lhsT=wt[:, kc, j * 128:(j + 1) * 128], rhs=xn[:, kc, :], start=(kc == 0), stop=(kc == 7)),
                             r=[wk] + xnk, w=[("ps", b)])
                    if kind == "q":
                        S.act(lambda e, j=j, b=b, st_=st_: e.mul(st_[:, j, :], k.ps[b][:], 0.125), r=[("ps", b)], w=[stk])
                    elif kind == "k":
                        S.act(lambda e, j=j, b=b, st_=st_: e.copy(st_[:, j, :], k.ps[b][:]), r=[("ps", b)], w=[stk])
                    else:
                        S.act(lambda e, j=j, b=b, st_=st_: e.activation(st_[:, j, :], k.ps[b][:], AF.Sigmoid), r=[("ps", b)], w=[stk])
                blk0 = {0: 0, 512: 0, 3584: 0, 4096: 4, 4608: 0, 5120: 4}[n0]
                S.dma(dst[blk0:blk0 + 4, :, tt:tt + T].rearrange("j p t -> p j t"), st_[:], r=[stk], w=[(kind, n0, ti)], q="pool")
            tm_groups = [("v", 1024, k.V, t0, True), ("hf", 2048, k.HF, t0, False), ("hi", 2560, k.HI, t0, True)]
            if own:
                tm_groups += [("hq", 1536, k.HQ, to, False), ("hg", 3072, k.HG, to, False)]
            for (kind, n0, dst, tt, isb) in tm_groups:
                wt, wk = load_w(k, "w_in", 0, 8, n0, 512)
                if kind == "v":
                    st_ = vst[cn["v"] % 2]
                    stk = ("vst", cn["v"] % 2)
                    cn["v"] += 1
                elif isb:
                    st_ = tmst_b[cn["tmb"] % len(tmst_b)]
                    stk = ("tmstb", cn["tmb"] % len(tmst_b))
                    cn["tmb"] += 1
                else:
                    st_ = tmst_f[cn["tmf"] % len(tmst_f)]
                    stk = ("tmstf", cn["tmf"] % len(tmst_f))
                    cn["tmf"] += 1
                for s in range(4):
                    b = nbank(k)
                    for kc in range(8):
                        S.pe(lambda e, kc=kc, s=s, b=b, wt=wt: e.matmul(k.ps[b][:], lhsT=xn[:, kc, s * 128:(s + 1) * 128], rhs=wt[:, kc, :], start=(kc == 0), stop=(kc == 7)),
                             r=[wk] + xnk, w=[("ps", b)])
                    if kind == "v":
                        S.act(lambda e, s=s, b=b, st_=st_: e.copy(st_[:, s, :].rearrange("p (h c) -> p h c", c=65)[:, :, 0:64],
                                                                  k.ps[b][:].rearrange("p (h c) -> p h c", c=64)), r=[("ps", b)], w=[stk])
                    else:
                        S.act(lambda e, s=s, b=b, st_=st_: e.copy(st_[:, s, :], k.ps[b][:]), r=[("ps", b)], w=[stk])
                S.dma(dst[tt:tt + T, :].rearrange("(s p) c -> p s c", p=128), st_[:], r=[stk], w=[(kind, ti)], q="pool")

        for f in front_thunks(0):
            f()
        ffn_gu(k, "w_ffn1_gu", xnA, "xnA")
        for ti in range(NT):
            fr = front_thunks(ti + 1) if ti + 1 < NT else []
            ffn_down(k, "w_ffn1_down", yT, "yT", hooks=fr)
            md = mid_thunks(ti)
            if ti + 1 < NT:
                ffn_gu(k, "w_ffn1_gu", xnA, "xnA", hooks=md)
            else:
                for f in md:
                    f()
            win(ti)


def phase_b(k):
    nc = k.nc
    S = k.S
    with ExitStack() as st:
        sb = lambda n, s, d: st.enter_context(nc.sbuf_tensor(n, list(s), d))
        mctr = [0]

        def mb():
            mctr[0] += 1
            return 6 + (mctr[0] % 2)

        VA = sb("b_VA", [128, 64, 520], BF16)
        KA = [sb("b_KA%d" % i, [96, NTOK], BF16) for i in range(2)]
        QA = [sb("b_QA%d" % i, [96, NOWN], BF16) for i in range(2)]
        sqb = sb("b_sqb", [64, NTOK], BF16)
        qsq = sb("b_qsq", [64, NOWN], BF16)
        Jb = sb("b_Jb", [128, 128], BF16)
        Jf = sb("b_Jf", [128, 128], F32)
        onesf = sb("b_onesf", [128, 64], F32)
        relt = sb("b_relt", [32, 8], F32)
        ohp = sb("b_ohp", [32, 768], F32)
        cmt = sb("b_cm", [8, 768], F32)
        tv = sb("b_tv", [8, 768], F32)
        tvh = sb("b_tvh", [8, 768], BF16)
        tvhf = sb("b_tvhf", [8, 768], F32)
        tvl = sb("b_tvl", [8, 768], BF16)
        gmask = sb("b_gmask", [128, 16, 32], F32)
        ownoh = sb("b_ownoh", [128, 16, 32], F32)
        Bt = [[sb("b_Bt%d_%d" % (i, j), [128, 256], BF16) for j in range(8)] for i in range(2)]
        kmTs = [sb("b_kmT%d" % i, [64, 32], BF16) for i in range(2)]
        kmf = sb("b_kmf", [64, 32], F32)
        kmx = sb("b_kmx", [128, 16], F32)
        kmax2 = sb("b_kmax2", [128, 1], F32)
        gm = sb("b_gm", [128, 32], F32)
        mx8 = sb("b_mx8", [128, 8], F32)
        vv = sb("b_vv", [128, 32], F32)
        sel = sb("b_sel", [128, 32], F32)
        mmall = sb("b_mmall", [128, 32], F32)
        mmball = sb("b_mmball", [128, 32], BF16)
        a01alls = [sb("b_a01all%d" % i, [128, 32], F32) for i in range(2)]
        mm = sb("b_mm", [128, 1], F32)
        mmb = sb("b_mmb", [128, 1], BF16)
        a01 = sb("b_a01", [128, 1], F32)
        rbb = sb("b_rbb", [128, 32], BF16)
        rbbs = [sb("b_rbbs%d" % i, [128, 32], BF16) for i in range(2)]
        PT = [sb("b_PT%d" % i, [128, 512], BF16) for i in range(4)]
        osb = [sb("b_osb%d" % i, [65, 256], F32) for i in range(2)]
        rden = [sb("b_rden%d" % i, [65, 256], F32) for i in range(2)]
        ost = [sb("b_ost%d" % i, [64, 256], BF16) for i in range(2)]

        Vr = k.V.rearrange("(a p) c -> p a c", p=128)
        for g8 in range(8):
            S.dma(VA[:, g8 * 8:(g8 + 1) * 8, :], Vr[:, g8 * 8:(g8 + 1) * 8, :], w=[("VA", g8)])
        S.dma(Jf[:], k.cJ, w=["Jf"])
        S.dve(lambda e: e.tensor_copy(Jb[:], Jf[:]), r=["Jf"], w=["Jb"])
        S.pool(lambda e: e.memset(onesf[:], 1.0), w=["onesf"])
        S.dma(relt[:], k.rel, w=["relt"])
        S.dma(ohp[:], k.cOhp, w=["ohp"])
        S.dma(cmt[:], k.cCm, w=["cmt"])
        S.dma(gmask[:], k.gmask, w=["gmask"])
        S.dma(ownoh[:], k.cOwnoh, w=["ownoh"])
        for i in range(2):
            S.dma(KA[i][64:96, :], k.cKaug, w=[("KAaug", i)], q="pool")
        for (c0, c1) in ((0, 512), (512, 768)):
            b = mb()
            S.pe(lambda e, b=b, c0=c0, c1=c1: e.matmul(k.ps[b][0:8, 0:c1 - c0], lhsT=relt[:], rhs=ohp[:, c0:c1], start=True, stop=True), r=["relt", "ohp"], w=[("ps", b)])
            S.dve(lambda e, b=b, c0=c0, c1=c1: e.tensor_tensor(out=tv[:, c0:c1], in0=k.ps[b][0:8, 0:c1 - c0], in1=cmt[:, c0:c1], op=ALU.add), r=[("ps", b), "cmt"], w=["tv"])
        S.dve(lambda e: e.tensor_copy(tvh[:], tv[:]), r=["tv"], w=["tvh"])
        S.dve(lambda e: e.tensor_copy(tvhf[:], tvh[:]), r=["tvh"], w=["tvhf"])
        S.dve(lambda e: e.tensor_tensor(out=tvl[:], in0=tv[:], in1=tvhf[:], op=ALU.subtract), r=["tv", "tvhf"], w=["tvl"])
        S.dma(k.TVH, tvh[:], r=["tvh"], w=["TVH"], q="pool")
        S.dma(k.TVL, tvl[:], r=["tvl"], w=["TVL"], q="pool")

        ctr = {"s": 0, "o": 0, "p": 0}

        def hv(h):
            hb = h % 2
            return h // 2, (h % 2) * 64, hb, KA[hb], QA[hb], ("KA", hb), ("QA", hb), Bt[hb]

        def setup_parts(h):
            hp, po, hb, ka, qa, kak, qak, bt = hv(h)

            def t0():
                S.dma(ka[0:64, :], k.KT[hp, po:po + 64, :], w=[kak])
                S.dma(qa[0:64, :], k.QT[hp, po:po + 64, :], w=[qak])
                for a in range(2):
                    for (idx, off) in ((0, 128 - 128 * a), (1, 384 - 128 * a)):
                        for (lo, src) in ((0, k.TVH), (1, k.TVL)):
                            ap_ = bass.AP(tensor=src.tensor, offset=src[h:h + 1, off:off + 1].offset, ap=[[1, 128], [1, 256]])
                            S.dma(bt[idx * 4 + a * 2 + lo][:], ap_, r=["TVH", "TVL"], w=[("Bt", hb, idx * 4 + a * 2 + lo)])
                S.pool(lambda e: e.tensor_tensor(out=sqb[:], in0=ka[0:64, :], in1=ka[0:64, :], op=ALU.mult), r=[kak], w=["sqb"])
                S.pool(lambda e: e.tensor_tensor(out=qsq[:], in0=qa[0:64, :], in1=qa[0:64, :], op=ALU.mult), r=[qak], w=["qsq"])

            def tr_(j):
                S.dve(lambda e: e.tensor_reduce(out=kmf[:, 4 * j:4 * j + 4], in_=ka[0:64, j * 1024:(j + 1) * 1024].rearrange("p (n c) -> p n c", c=256), axis=AX.X, op=ALU.add),
                      r=[kak], w=[("kmf", j)])

            def tm():
                S.dve(lambda e: e.tensor_scalar_mul(kmTs[hb][:], kmf[:], 1.0 / 256), r=[("kmf", j) for j in range(8)], w=[("kmT", hb)])

            def tk(c2):
                for c in (2 * c2, 2 * c2 + 1):
                    b = 3
                    S.pe(lambda e, b=b, c=c: e.matmul(k.ps[b][:], lhsT=k.onesb[0:64, :], rhs=sqb[:, c * 512:(c + 1) * 512], start=True, stop=True), r=["sqb", "onesb"], w=[("ps", b)])
                    S.dve(lambda e, b=b, c=c: e.tensor_reduce(out=kmx[:, c:c + 1], in_=k.ps[b][:], axis=AX.X, op=ALU.max), r=[("ps", b)], w=["kmx"])

            def tq():
                S.dve(lambda e: e.tensor_reduce(out=kmax2[:], in_=kmx[:], axis=AX.X, op=ALU.max), r=["kmx"], w=["kmax2"])
                bq = 3
                for qt_ in range(32):
                    S.pe(lambda e, qt_=qt_: e.matmul(k.ps[bq][:, qt_:qt_ + 1], lhsT=qsq[:, qt_ * 128:(qt_ + 1) * 128], rhs=k.onesb[0:64, 0:1], start=True, stop=True),
                         r=["qsq", "onesb"], w=[("ps", bq)])
                S.dve(lambda e: e.tensor_scalar_mul(mmall[:], k.ps[bq][:, 0:32], kmax2[:, 0:1]), r=[("ps", bq), "kmax2"], w=["mmall"])

            def ta():
                S.act(lambda e: e.activation(mmall[:], mmall[:], AF.Ln), r=["mmall"], w=["mmall"])
                S.act(lambda e: e.activation(mmball[:], mmall[:], AF.Exp, scale=0.5), r=["mmall"], w=["mmball"])
                S.dve(lambda e: e.tensor_scalar(a01alls[hb][:], mmball[:], -1.0, -NEG, op0=ALU.mult, op1=ALU.add), r=["mmball"], w=[("a01all", hb)])
            return [t0] + [lambda j=j: tr_(j) for j in range(8)] + [tm] + [lambda c2=c2: tk(c2) for c2 in range(8)] + [tq, ta]

        def setup(h):
            for p in setup_parts(h):
                p()

        def gate_parts(h, i):
            hp, po, hb, ka, qa, kak, qak, bt = hv(h)
            qc = slice(i * 256, (i + 1) * 256)
            bT = 6 + (i % 2)
            pbT = k.ps[bT][:].bitcast(BF16)
            bg = 7 - (i % 2)

            def g_mm(qt_):
                qcols = slice(i * 256 + qt_ * 128, i * 256 + (qt_ + 1) * 128)
                S.pe(lambda e: e.matmul(k.ps[bg][:, 0:32], lhsT=qa[0:64, qcols], rhs=kmTs[hb][:], start=True, stop=True), r=[qak, ("kmT", hb)], w=[("ps", bg)])
                S.dve(lambda e: e.tensor_tensor(out=gm[:], in0=k.ps[bg][:, 0:32], in1=gmask[:, i, :], op=ALU.add), r=[("ps", bg), "gmask"], w=["gm"])
                S.dve(lambda e: e.max(out=mx8[:], in_=gm[:]), r=["gm"], w=["mx8"])
                S.dve(lambda e: e.tensor_single_scalar(vv[:], gm[:], -1e29, op=ALU.is_gt), r=["gm"], w=["vv"])
                S.dve(lambda e: e.scalar_tensor_tensor(out=sel[:], in0=gm[:], scalar=mx8[:, 2:3], in1=vv[:], op0=ALU.is_ge, op1=ALU.mult), r=["gm", "mx8", "vv"], w=["sel"])
                S.dve(lambda e: e.tensor_tensor(out=sel[:], in0=sel[:], in1=ownoh[:, i, :], op=ALU.add), r=["sel", "ownoh"], w=["sel"])
                qti = i * 2 + qt_
                S.dve(lambda e: e.tensor_scalar(rbbs[qt_][:], sel[:], a01alls[hb][:, qti:qti + 1], NEG, op0=ALU.mult, op1=ALU.add), r=["sel", ("a01all", hb)], w=[("rbb", qt_)])

            def g_tr(qt_):
                S.pe(lambda e: e.transpose(pbT[0:32, qt_ * 128:(qt_ + 1) * 128], rbbs[qt_][:], k.identb[:]), r=[("rbb", qt_), "identb"], w=[("ps", bT)])

            def g_copy():
                S.dve(lambda e: e.tensor_copy(qa[64:96, qc], pbT[0:32, 0:256]), r=[("ps", bT)], w=[("QAaug", hb, i)])

            def p0():
                g_mm(0)

            def p1():
                g_tr(0)
                g_mm(1)

            def p2():
                g_tr(1)
                g_copy()
            return [p0, p1, p2]

        def main(h, i, hooks):
            hp, po, hb, ka, qa, kak, qak, bt = hv(h)
            I = 16 + i
            qc = slice(i * 256, (i + 1) * 256)
            bO = 4 + (ctr["o"] % 2)
            ctr["o"] += 1

            def rec_S(n):
                bS = ctr["s"] % 3
                ctr["s"] += 1
                special = n >= I - 1
                for a in range(2):
                    kcols = slice(n * 256 + a * 128, n * 256 + (a + 1) * 128)
                    S.pe(lambda e, bS=bS, a=a, kcols=kcols, special=special: e.matmul(
                        k.ps[bS][:, a * 256:(a + 1) * 256], lhsT=ka[0:96, kcols], rhs=qa[0:96, qc], start=True, stop=(not special)),
                        r=[kak, qak, ("KAaug", hb), ("QAaug", hb, i)], w=[("ps", bS)])
                    if special:
                        idx = 0 if n == I else 1
                        for lo in range(2):
                            S.pe(lambda e, bS=bS, a=a, idx=idx, lo=lo: e.matmul(
                                k.ps[bS][:, a * 256:(a + 1) * 256], lhsT=Jb[:], rhs=bt[idx * 4 + a * 2 + lo][:], start=False, stop=(lo == 1)),
                                r=["Jb", ("Bt", hb, idx * 4 + a * 2 + lo)], w=[("ps", bS)])
                return bS

            def rec_PV(n, bS):
                pi = ctr["p"] % 4
                ctr["p"] += 1
                S.act(lambda e, bS=bS, pi=pi: e.activation(PT[pi][:], k.ps[bS][:], AF.Exp), r=[("ps", bS)], w=[("PT", pi)])
                for a in range(2):
                    ktile = n * 2 + a
                    S.pe(lambda e, a=a, ktile=ktile, pi=pi, first=(n == 0 and a == 0), last=(n == I and a == 1): e.matmul(
                        k.ps[bO][0:65, 0:256], lhsT=VA[:, ktile, h * 65:(h + 1) * 65], rhs=PT[pi][:, a * 256:(a + 1) * 256], start=first, stop=last),
                        r=[("VA", ktile // 8), ("PT", pi)], w=[("ps", bO)])

            DEPTH = 2
            banks = {}
            for n in range(min(DEPTH, I + 1)):
                banks[n] = rec_S(n)
            for n in range(I + 1):
                if n + DEPTH <= I:
                    banks[n + DEPTH] = rec_S(n + DEPTH)
                rec_PV(n, banks.pop(n))
                for fn in hooks.get(n, ()):
                    fn()
            oi = ctr["o"] % 2
            S.act(lambda e: e.copy(osb[oi][:], k.ps[bO][0:65, 0:256]), r=[("ps", bO)], w=[("osb", oi)])
            S.dve(lambda e: e.reciprocal(rden[oi][64:65, :], osb[oi][64:65, :]), r=[("osb", oi)], w=[("rden", oi)])

            def fin():
                bB = 3
                S.pe(lambda e: e.matmul(k.ps[bB][0:64, 0:256], lhsT=onesf[64:65, :], rhs=rden[oi][64:65, :], start=True, stop=True), r=["onesf", ("rden", oi)], w=[("ps", bB)])
                S.dve(lambda e: e.tensor_tensor(out=ost[oi][:], in0=osb[oi][0:64, :], in1=k.ps[bB][0:64, 0:256], op=ALU.mult), r=[("osb", oi), ("ps", bB)], w=[("ost", oi)])
                S.dma(k.OAT[hp, po:po + 64, qc], ost[oi][:], r=[("ost", oi)], w=["OAT"], q="pool")
            return fin

        blocks = [(h, i) for h in range(8) for i in range(16)]
        setup(0)
        for p in gate_parts(0, 0):
            p()
        pending_fin = None
        for bi, (h, i) in enumerate(blocks):
            hooks = {}
            if pending_fin is not None:
                hooks.setdefault(3, []).append(pending_fin)
            if h + 1 < 8 and i in (10, 11, 12):
                sp = setup_parts(h + 1)
                if i == 10:
                    hooks.setdefault(1, []).append(sp[0])
                    for j in range(8):
                        hooks.setdefault(3 + 2 * j, []).append(sp[1 + j])
                    hooks.setdefault(20, []).append(sp[9])
                elif i == 11:
                    for j in range(8):
                        hooks.setdefault(2 + 3 * j, []).append(sp[10 + j])
                else:
                    hooks.setdefault(2, []).append(sp[18])
                    hooks.setdefault(12, []).append(sp[19])
            if bi + 1 < len(blocks):
                h2, i2 = blocks[bi + 1]
                parts = gate_parts(h2, i2)
                if h2 != h:
                    steps = (14, 19, 24)
                else:
                    steps = (1, 6, 11)
                for st_, p in zip(steps, parts):
                    hooks.setdefault(st_, []).append(p)
            pending_fin = main(h, i, hooks)
        pending_fin()


def phase_c(k):
    nc = k.nc
    S = k.S
    LN_HALF = -0.6931471805599453
    with ExitStack() as st:
        sb = lambda n, s, d: st.enter_context(nc.sbuf_tensor(n, list(s), d))

        class Ring:
            def __init__(self, name, n, shape, dt):
                self.name = name
                self.t = [sb("c_%s%d" % (name, i), shape, dt) for i in range(n)]

            def at(self, s):
                i = s % len(self.t)
                return self.t[i], (self.name, i)

        lbp2 = sb("c_lbp2", [128, 2, 512], F32)
        lb = sb("c_lb", [128, 512], F32)
        c0 = sb("c_c0", [128, 512], F32)
        c1 = sb("c_c1", [128, 512], F32)
        hgwb = sb("c_hgwb", [128, 512], F32)
        M2 = sb("c_M2", [128, 128], F32)
        Sel = sb("c_Sel", [128, 4], F32)
        tri = sb("c_tri", [128, 64], F32)
        state = sb("c_state", [128, 4, 128], F32)
        stfs = [sb("c_stf%d" % i, [128, 4, 128], F32) for i in range(2)]
        tmp = sb("c_tmp", [128, 4, 128], F32)
        R_hf = Ring("hf", 2, [128, 512], F32)
        R_hi = Ring("hi", 6, [128, 512], BF16)
        R_hq = Ring("hq", 4, [128, 512], F32)
        R_hg = Ring("hg", 4, [128, 512], F32)
        R_thq = Ring("thq", 3, [128, 512], F32)
        R_thg = Ring("thg", 3, [128, 512], F32)
        R_f = Ring("f", 2, [128, 512], F32)
        R_logf = Ring("logf", 2, [128, 512], F32)
        R_kin = Ring("kin", 2, [128, 512], F32)
        R_kt = Ring("kt", 2, [128, 512], BF16)
        R_qTs = Ring("qTs", 3, [128, 4, 128], BF16)
        R_kTs = Ring("kTs", 3, [128, 4, 128], BF16)
        R_dec = Ring("dec", 2, [128, 16], F32)
        R_t2 = Ring("t2", 4, [128, 512], F32)
        R_AT = Ring("AT", 2, [128, 4, 64], BF16)
        R_spb = [Ring("spb%d" % c, 2, [128, 4, 128], BF16) for c in range(2)]
        R_obuf = Ring("obuf", 2, [128, 4, 128], F32)
        thf = sb("c_thf", [128, 512], F32)
        Em = sb("c_Em", [128, 512], F32)
        Ep = sb("c_Ep", [128, 512], F32)
        q1 = sb("c_q1", [128, 512], F32)
        qt = sb("c_qt", [128, 512], BF16)
        junk = sb("c_junk", [128, 128], F32)
        ss = sb("c_ss", [128, 4], F32)
        ob = sb("c_ob", [128, 4, 128], F32)
        obb = sb("c_obb", [128, 512], BF16)
        obst = [sb("c_obst%d" % i, [128, 4, 512], BF16) for i in range(2)]

        S.dma(lbp2[:, 0, :], k.lbp[0:1, :].partition_broadcast(128), w=[("lbp2", 0)])
        S.dma(lbp2[:, 1, :], k.lbp[1:2, :].partition_broadcast(128), w=[("lbp2", 1)])
        for h in range(4):
            S.dma(hgwb[:, h * 128:(h + 1) * 128], k.hgw[0:1, :].partition_broadcast(128), w=[("hgwb", h)])
        S.dma(M2[:], k.cM2, w=["M2"])
        S.dma(Sel[:], k.cSel, w=["Sel"])
        S.dma(tri[:], k.cTri, w=["tri"])
        S.dve(lambda e: e.tensor_tensor(out=lb[:], in0=lbp2[:, 1, :], in1=lbp2[:, 0, :], op=ALU.subtract), r=[("lbp2", 0), ("lbp2", 1)], w=["lb"])
        S.act(lambda e: e.activation(lb[:], lb[:], AF.Exp), r=["lb"], w=["lb"])
        S.dve(lambda e: e.tensor_scalar_add(lb[:], lb[:], 1.0), r=["lb"], w=["lb"])
        S.dve(lambda e: e.reciprocal(lb[:], lb[:]), r=["lb"], w=["lb"])
        S.dve(lambda e: e.tensor_scalar(c1[:], lb[:], -0.5, 0.5, op0=ALU.mult, op1=ALU.add), r=["lb"], w=["c1"])
        S.dve(lambda e: e.tensor_tensor(out=c0[:], in0=lb[:], in1=c1[:], op=ALU.add), r=["lb", "c1"], w=["c0"])
        S.dve(lambda e: e.tensor_scalar_mul(hgwb[:], hgwb[:], 0.5), r=[("hgwb", h) for h in range(4)], w=["hgwb"])
        S.pool(lambda e: e.memset(state[:], 0.0), w=["state"])

        nsteps = NTOK // 128
        first_own = NCTX // 128
        prs = [slice(0, 64), slice(64, 128)]
        k.bank_excl = {0, 1, 2}
        ctxs = {}
        chs = {}

        def E_a(s):
            own = s >= first_own
            t0 = s * 128
            to = t0 - NCTX
            hf, hfk = R_hf.at(s)
            hi, hik = R_hi.at(s)
            f_, fk = R_f.at(s)
            S.dma(hf[:], k.HF[t0:t0 + 128, :], w=[hfk])
            S.dma(hi[:], k.HI[t0:t0 + 128, :], w=[hik])
            S.act(lambda e: e.activation(thf[:], hf[:], AF.Tanh, scale=0.5), r=[hfk], w=["thf"])
            if own:
                hq, hqk = R_hq.at(s)
                hg, hgk = R_hg.at(s)
                thq, thqk = R_thq.at(s)
                thg, thgk = R_thg.at(s)
                S.dma(hq[:], k.HQ[to:to + 128, :], w=[hqk])
                S.dma(hg[:], k.HG[to:to + 128, :], w=[hgk])
                S.act(lambda e: e.activation(thq[:], hq[:], AF.Tanh, scale=0.5), r=[hqk], w=[thqk])
                S.act(lambda e: e.activation(thg[:], hg[:], AF.Tanh, scale=0.5), r=[hgk], w=[thgk])
            S.dve(lambda e: e.tensor_tensor(out=f_[:], in0=thf[:], in1=c1[:], op=ALU.mult), r=["thf", "c1"], w=[fk])
            S.dve(lambda e: e.tensor_tensor(out=f_[:], in0=f_[:], in1=c0[:], op=ALU.add), r=[fk, "c0"], w=[fk])

        def E_b(s):
            f_, fk = R_f.at(s)
            logf, lk = R_logf.at(s)
            kin, kk = R_kin.at(s)
            S.act(lambda e: e.activation(logf[:], f_[:], AF.Ln), r=[fk], w=[lk])
            S.act(lambda e: e.activation(kin[:], f_[:], AF.Identity, scale=-1.0, bias=1.0), r=[fk], w=[kk])
            bbp = s % 2
            S.pe(lambda e: e.matmul(k.ps[bbp][:], lhsT=M2[:], rhs=logf[:], start=True, stop=True), r=["M2", lk], w=[("ps", bbp)])
            bdec = 2
            co = (s % 2) * 16
            for h in range(4):
                S.pe(lambda e, h=h: e.matmul(k.ps[bdec][:, co + h * 4:co + (h + 1) * 4], lhsT=logf[:, h * 128:(h + 1) * 128], rhs=Sel[:], start=True, stop=True),
                     r=[lk, "Sel"], w=[("psdec", s % 2)])
            ctxs[s] = (bbp, bdec, co)

        def E_c(s):
            own = s >= first_own
            bbp, bdec, co = ctxs.pop(s)
            kin, kk = R_kin.at(s)
            dec, dk = R_dec.at(s)
            kt, ktk = R_kt.at(s)
            S.act(lambda e: e.activation(Em[:], k.ps[bbp][:], AF.Exp, scale=-1.0), r=[("ps", bbp)], w=["Em"])
            if own:
                S.act(lambda e: e.activation(Ep[:], k.ps[bbp][:], AF.Exp, bias=LN_HALF), r=[("ps", bbp)], w=["Ep"])
            S.act(lambda e: e.activation(dec[:], k.ps[bdec][:, co:co + 16], AF.Exp), r=[("psdec", s % 2)], w=[dk])
            S.dve(lambda e: e.tensor_tensor(out=kt[:], in0=kin[:], in1=Em[:], op=ALU.mult), r=[kk, "Em"], w=[ktk])
            if own:
                hq, hqk = R_hq.at(s)
                hg, hgk = R_hg.at(s)
                thq, thqk = R_thq.at(s)
                thg, thgk = R_thg.at(s)
                t2, t2k = R_t2.at(s)
                qTs, qTk = R_qTs.at(s)
                kTs, kTk = R_kTs.at(s)
                S.dve(lambda e: e.scalar_tensor_tensor(out=q1[:], in0=thq[:], scalar=1.0, in1=hq[:], op0=ALU.add, op1=ALU.mult), r=[thqk, hqk], w=["q1"])
                S.dve(lambda e: e.tensor_tensor(out=qt[:], in0=q1[:], in1=Ep[:], op=ALU.mult), r=["q1", "Ep"], w=["qt"])
                S.dve(lambda e: e.scalar_tensor_tensor(out=t2[:], in0=thg[:], scalar=1.0, in1=hg[:], op0=ALU.add, op1=ALU.mult), r=[thgk, hgk], w=[t2k])
                S.dve(lambda e: e.tensor_tensor(out=t2[:], in0=t2[:], in1=hgwb[:], op=ALU.mult), r=[t2k, "hgwb"], w=[t2k])
                for (src, srck, dst, dstk) in ((qt, "qt", qTs, qTk), (kt, ktk, kTs, kTk)):
                    b = nbank(k)
                    pb = k.ps[b][:].bitcast(BF16)
                    for h in range(4):
                        S.pe(lambda e, h=h, pb=pb, src=src: e.transpose(pb[:, h * 128:(h + 1) * 128], src[:, h * 128:(h + 1) * 128], k.identb[:]),
                             r=[srck, "identb"], w=[("ps", b)])
                    S.act(lambda e, pb=pb, dst=dst: e.copy(dst[:].rearrange("p h t -> p (h t)"), pb[:, 0:512]), r=[("ps", b)], w=[dstk])

        def Ch_pe1(s):
            own = s >= first_own
            kt, ktk = R_kt.at(s)
            hi, hik = R_hi.at(s)
            bKs = []
            for c in range(2):
                pr = prs[c]
                bK = nbank(k)
                bKs.append(bK)
                for h in range(4):
                    S.pe(lambda e, h=h, bK=bK, pr=pr: e.matmul(k.ps[bK][:, h * 128:(h + 1) * 128], lhsT=kt[pr, h * 128:(h + 1) * 128], rhs=hi[pr, h * 128:(h + 1) * 128], start=True, stop=True),
                         r=[ktk, hik], w=[("ps", bK)])
            chs[s] = bKs
            if own:
                qTs, qTk = R_qTs.at(s)
                kTs, kTk = R_kTs.at(s)
                AT, ATk = R_AT.at(s)
                for c in range(2):
                    pr = prs[c]
                    bA = nbank(k)
                    for h in range(4):
                        S.pe(lambda e, h=h, bA=bA, pr=pr: e.matmul(k.ps[bA][0:64, h * 64:(h + 1) * 64], lhsT=kTs[:, h, pr], rhs=qTs[:, h, pr], start=True, stop=True),
                             r=[kTk, qTk], w=[("ps", bA)])
                    S.dve(lambda e, bA=bA, pr=pr: e.tensor_tensor(out=AT[pr], in0=k.ps[bA][0:64, 0:256].rearrange("p (h t) -> p h t", t=64),
                                                                  in1=tri[0:64, :].unsqueeze(1).to_broadcast([64, 4, 64]), op=ALU.mult),
                          r=[("ps", bA), "tri"], w=[(ATk, c)])

        def Ch_rec(s):
            own = s >= first_own
            dec, dk = R_dec.at(s)
            bKs = chs.pop(s)
            for c in range(2):
                dmid = dec[:, 2 * c:16:4].unsqueeze(2).to_broadcast([128, 4, 128])
                dlast = dec[:, 2 * c + 1:16:4].unsqueeze(2).to_broadcast([128, 4, 128])
                bK = bKs[c]
                S.dve(lambda e, dmid=dmid, c=c: e.tensor_tensor(out=stfs[c][:], in0=state[:], in1=dmid, op=ALU.mult), r=["state", dk], w=[("stf", c)])
                if own:
                    spb, spk = R_spb[c].at(s)
                    S.act(lambda e, c=c, spb=spb: e.copy(spb[:], stfs[c][:]), r=[("stf", c)], w=[spk])
                S.dve(lambda e, bK=bK, c=c: e.tensor_tensor(out=tmp[:], in0=k.ps[bK][:].rearrange("p (h v) -> p h v", v=128), in1=stfs[c][:], op=ALU.add), r=[("ps", bK), ("stf", c)], w=["tmp"])
                S.dve(lambda e, dlast=dlast: e.tensor_tensor(out=state[:], in0=tmp[:], in1=dlast, op=ALU.mult), r=["tmp", dk], w=["state"])

        def Ch_o(s):
            qTs, qTk = R_qTs.at(s)
            hi, hik = R_hi.at(s)
            AT, ATk = R_AT.at(s)
            obuf, obk = R_obuf.at(s)
            for c in range(2):
                pr = prs[c]
                spb, spk = R_spb[c].at(s)
                bO = nbank(k)
                for h in range(4):
                    S.pe(lambda e, h=h, bO=bO, pr=pr, spb=spb: e.matmul(k.ps[bO][0:64, h * 128:(h + 1) * 128], lhsT=qTs[:, h, pr], rhs=spb[:, h, :], start=True, stop=False),
                         r=[qTk, spk], w=[("ps", bO)])
                    S.pe(lambda e, h=h, bO=bO, pr=pr: e.matmul(k.ps[bO][0:64, h * 128:(h + 1) * 128], lhsT=AT[pr, h, :], rhs=hi[pr, h * 128:(h + 1) * 128], start=False, stop=True),
                         r=[(ATk, c), hik], w=[("ps", bO)])
                S.act(lambda e, bO=bO, pr=pr: e.copy(obuf[pr].rearrange("p h v -> p (h v)"), k.ps[bO][0:64, :]), r=[("ps", bO)], w=[(obk, c)])

        def Post(s):
            obuf, obk = R_obuf.at(s)
            t2, t2k = R_t2.at(s)
            S.dve(lambda e: e.memset(ss[:], 0.0), w=["ss"])
            for h in range(4):
                S.act(lambda e, h=h: e.activation(junk[:], obuf[:, h, :], AF.Square, accum_out=ss[:, h:h + 1]), r=[(obk, 0), (obk, 1), "junk"], w=["ss", "junk"])
            S.act(lambda e: e.activation(ss[:], ss[:], AF.Ln, scale=1.0 / 128, bias=EPS), r=["ss"], w=["ss"])
            S.act(lambda e: e.activation(ss[:], ss[:], AF.Exp, scale=-0.5), r=["ss"], w=["ss"])
            S.dve(lambda e: e.tensor_tensor(out=ob[:], in0=obuf[:], in1=ss[:, 0:4].unsqueeze(2).to_broadcast([128, 4, 128]), op=ALU.mult), r=[(obk, 0), (obk, 1), "ss"], w=["ob"])
            S.dve(lambda e: e.tensor_tensor(out=obb[:], in0=ob[:].rearrange("p h v -> p (h v)"), in1=t2[:], op=ALU.mult), r=["ob", t2k], w=["obb"])
            so = s - first_own
            sti = (so // 4) % 2
            b = nbank(k)
            pb = k.ps[b][:].bitcast(BF16)
            for h in range(4):
                S.pe(lambda e, h=h, pb=pb: e.transpose(pb[:, h * 128:(h + 1) * 128], obb[:, h * 128:(h + 1) * 128], k.identb[:]),
                     r=["obb", "identb"], w=[("ps", b)])
            sub = so % 4
            S.act(lambda e, pb=pb: e.copy(obst[sti][:, :, sub * 128:(sub + 1) * 128], pb[:, 0:512].rearrange("p (h t) -> p h t", t=128)),
                  r=[("ps", b)], w=[("obst", sti)])
            if sub == 3:
                tb = (so // 4) * 512
                S.dma(k.OBT[:, :, tb:tb + 512].rearrange("j p t -> p j t"), obst[sti][:], r=[("obst", sti)], w=["OBT"], q="pool")

        ok = lambda s: 0 <= s < nsteps
        own_ = lambda s: first_own <= s < nsteps
        for i in range(-3, nsteps + 2):
            if ok(i):
                Ch_pe1(i)
                Ch_rec(i)
            if own_(i - 1):
                Ch_o(i - 1)
            if own_(i - 2):
                Post(i - 2)
            if ok(i + 1):
                E_c(i + 1)
            if ok(i + 2):
                E_b(i + 2)
            if ok(i + 3):
                E_a(i + 3)
        k.bank_excl = set()


def phase_d(k):
    nc = k.nc
    S = k.S
    with ExitStack() as st:
        sb = lambda n, s, d: st.enter_context(nc.sbuf_tensor(n, list(s), d))
        k.NW = 4
        k.wctr = 0
        k.wring = [sb("dwring%d" % i, [128, 8, 512], BF16) for i in range(k.NW)]
        xtok = sb("dxtok", [128, 4, D], F32)
        xTs = [sb("dxT%d" % i, [128, 8, T], F32) for i in range(2)]
        xnA = sb("dxnA", [128, 8, T], BF16)
        xnB = sb("dxnB", [128, 8, T], BF16)
        k.sq = sb("dsq", [128, 8, T], BF16)
        k.aT = sb("daT", [128, 22, T], BF16)
        yT = sb("dyT", [128, 8, T], F32)
        k.silu = [sb("dsilu%d" % i, [128, T], F32) for i in range(2)]
        sgt = [sb("dsgt%d" % i, [128, T], F32) for i in range(2)]
        rs = sb("drs", [128, T], F32)
        oaT = sb("doaT", [128, 4, T], BF16)
        obT = sb("dobT", [128, 4, T], BF16)
        sga = sb("dsga", [128, 8, T], BF16)
        sgb = sb("dsgb", [128, 8, T], BF16)
        ptoks = [sb("dptok%d" % i, [128, 4, 256], F32) for i in range(2)]
        pT = sb("dpT", [128, 2, T], BF16)
        pw = sb("dpw", [128, 2, 512], BF16)
        pg = sb("dpg", [128, 8, 512], BF16)
        NT = NOWN // T

        def chain_thunks(xT, xk, gtile, gcol, nxt):
            th = stats_thunks(k, yT, "yT", rs, "rs")

            def upd(kc):
                S.dve(lambda e: e.tensor_tensor(out=yT[:, kc, :], in0=yT[:, kc, :], in1=rs[:], op=ALU.mult), r=[("yT", kc), "rs"], w=[("yT", kc)])
                S.dve(lambda e: e.scalar_tensor_tensor(out=xT[:, kc, :], in0=yT[:, kc, :], scalar=gtile[:, gcol * 8 + kc:gcol * 8 + kc + 1], in1=xT[:, kc, :],
                                                       op0=ALU.mult, op1=ALU.add), r=[("yT", kc), (xk, kc), "gT", "gH"], w=[(xk, kc)])
            for kc in range(8):
                th.append(lambda kc=kc: upd(kc))
            if nxt is not None and nxt[0] == "norm":
                _, gcol2, dst, dstkey = nxt
                th += stats_thunks(k, xT, xk, rs, "rs")
                for kc in range(8):
                    th.append(lambda kc=kc: S.dve(lambda e: e.scalar_tensor_tensor(out=dst[:, kc, :], in0=xT[:, kc, :], scalar=k.gT[:, gcol2 * 8 + kc:gcol2 * 8 + kc + 1], in1=rs[:],
                                                                                    op0=ALU.mult, op1=ALU.mult), r=[(xk, kc), "rs", "gT"], w=[(dstkey, kc)]))
            elif nxt is not None and nxt[0] == "cast":
                _, dst, dstkey = nxt
                for kc in range(8):
                    th.append(lambda kc=kc: S.act(lambda e: e.copy(dst[:, kc, :], xT[:, kc, :]), r=[(xk, kc)], w=[(dstkey, kc)]))
            return th

        def P1_loads(ti):
            xT = xTs[ti % 2]
            xk = "dxT%d" % (ti % 2)
            to = ti * T
            S.dma(xT[:], k.H1T[:, :, to:to + T], w=[(xk, kc) for kc in range(8)])
            S.dma(oaT[:], k.OAT[:, :, to:to + T].rearrange("j p t -> p j t"), w=["oaT"])
            S.dma(obT[:], k.OBT[:, :, to:to + T].rearrange("j p t -> p j t"), w=["obT"])
            S.dma(sga[:], k.SGA[:, :, to:to + T].rearrange("j p t -> p j t"), w=["sga"])
            S.dma(sgb[:], k.SGB[:, :, to:to + T].rearrange("j p t -> p j t"), w=["sgb"])
            S.dma(ptoks[ti % 2][:], k.pc[to:to + T, :].rearrange("(s p) d -> p s d", p=128), w=[("ptok", ti % 2)])

        def P1(ti):
            for half in range(2):
                wa, wak = load_w(k, "w_branch_a", 0, 4, half * 512, 512)
                wb_, wbk = load_w(k, "w_branch_b", 0, 4, half * 512, 512)
                for jj in range(4):
                    j = half * 4 + jj
                    ba = nbank(k)
                    bb = nbank(k)
                    for kc in range(4):
                        S.pe(lambda e, kc=kc, jj=jj, ba=ba, wa=wa: e.matmul(k.ps[ba][:], lhsT=wa[:, kc, jj * 128:(jj + 1) * 128], rhs=oaT[:, kc, :], start=(kc == 0), stop=(kc == 3)),
                             r=[wak, "oaT"], w=[("ps", ba)])
                    for kc in range(4):
                        S.pe(lambda e, kc=kc, jj=jj, bb=bb, wb_=wb_: e.matmul(k.ps[bb][:], lhsT=wb_[:, kc, jj * 128:(jj + 1) * 128], rhs=obT[:, kc, :], start=(kc == 0), stop=(kc == 3)),
                             r=[wbk, "obT"], w=[("ps", bb)])
                    S.dve(lambda e, j=j, ba=ba: e.tensor_tensor(out=sgt[0][:], in0=k.ps[ba][:], in1=sga[:, j, :], op=ALU.mult), r=[("ps", ba), "sga"], w=[("sgt", 0)])
                    S.dve(lambda e, j=j, bb=bb: e.tensor_tensor(out=sgt[1][:], in0=k.ps[bb][:], in1=sgb[:, j, :], op=ALU.mult), r=[("ps", bb), "sgb"], w=[("sgt", 1)])
                    S.pool(lambda e, j=j: e.tensor_tensor(out=xnB[:, j, :], in0=sgt[0][:], in1=sgt[1][:], op=ALU.add), r=[("sgt", 0), ("sgt", 1)], w=[("xnB", j)])
            xnk = [("xnB", kc) for kc in range(8)]
            for half in range(2):
                wo, wok = load_w(k, "w_out", 0, 8, half * 512, 512)
                for jj in range(4):
                    j = half * 4 + jj
                    b = nbank(k)
                    for kc in range(8):
                        S.pe(lambda e, kc=kc, jj=jj, b=b, wo=wo: e.matmul(k.ps[b][:], lhsT=wo[:, kc, jj * 128:(jj + 1) * 128], rhs=xnB[:, kc, :], start=(kc == 0), stop=(kc == 7)),
                             r=[wok] + xnk, w=[("ps", b)])
                    S.act(lambda e, j=j, b=b: e.copy(yT[:, j, :], k.ps[b][:]), r=[("ps", b)], w=[("yT", j)])

        def P2(ti, halves=(0, 1)):
            for kc in (range(2) if 0 in halves else ()):
                b = nbank(k)
                for s in range(4):
                    S.pe(lambda e, kc=kc, s=s, b=b: e.transpose(k.ps[b][:, s * 128:(s + 1) * 128], ptoks[ti % 2][:, s, kc * 128:(kc + 1) * 128], k.identf[:]),
                         r=[("ptok", ti % 2), "identf"], w=[("ps", b)])
                S.act(lambda e, kc=kc, b=b: e.copy(pT[:, kc, :], k.ps[b][:]), r=[("ps", b)], w=["pT"])
            xnk = [("xnB", kc) for kc in range(8)]
            for half in halves:
                S.dma(pw[:], k.wb["w_ple"][0:256, half * 512:(half + 1) * 512].rearrange("(kc p) n -> p kc n", p=128), r=wkeys(k, "w_ple", 0, 2), w=["pw"])
                S.dma(pg[:], k.wb["w_ple_gate"][:, half * 512:(half + 1) * 512].rearrange("(kc p) n -> p kc n", p=128), r=wkeys(k, "w_ple_gate", 0, 8), w=["pg"])
                wp, wpk, wg, wgk = pw, "pw", pg, "pg"
                for jj in range(4):
                    j = half * 4 + jj
                    be = nbank(k)
                    bg = nbank(k)
                    for kc in range(2):
                        S.pe(lambda e, kc=kc, jj=jj, be=be, wp=wp: e.matmul(k.ps[be][:], lhsT=wp[:, kc, jj * 128:(jj + 1) * 128], rhs=pT[:, kc, :], start=(kc == 0), stop=(kc == 1)),
                             r=[wpk, "pT"], w=[("ps", be)])
                    for kc in range(8):
                        S.pe(lambda e, kc=kc, jj=jj, bg=bg, wg=wg: e.matmul(k.ps[bg][:], lhsT=wg[:, kc, jj * 128:(jj + 1) * 128], rhs=xnB[:, kc, :], start=(kc == 0), stop=(kc == 7)),
                             r=[wgk] + xnk, w=[("ps", bg)])
                    sl = j % 2
                    S.act(lambda e, bg=bg, sl=sl: e.activation(sgt[sl][:], k.ps[bg][:], AF.Sigmoid), r=[("ps", bg)], w=[("sgt", sl)])
                    S.dve(lambda e, be=be, sl=sl, j=j: e.tensor_tensor(out=yT[:, j, :], in0=sgt[sl][:], in1=k.ps[be][:], op=ALU.mult),
                          r=[("ps", be), ("sgt", sl)], w=[("yT", j)])

        def O_thunks(ti):
            xT = xTs[ti % 2]
            xk = "dxT%d" % (ti % 2)
            to = ti * T
            th = []

            def tr(s, g2):
                b = nbank(k)
                for kk in range(4):
                    kc = g2 * 4 + kk
                    S.pe(lambda e, kc=kc, kk=kk: e.transpose(k.ps[b][:, kk * 128:(kk + 1) * 128], xT[:, kc, s * 128:(s + 1) * 128], k.identf[:]),
                         r=[(xk, kc), "identf"], w=[("ps", b)])
                S.act(lambda e: e.copy(xtok[:, s, g2 * 512:(g2 + 1) * 512], k.ps[b][:]), r=[("ps", b)], w=["xtok"])
            for s in range(4):
                for g2 in range(2):
                    th.append(lambda s=s, g2=g2: tr(s, g2))
            th.append(lambda: S.dma(k.out[to:to + T, :].rearrange("(s p) d -> p s d", p=128), xtok[:], r=["xtok"], w=[("out", ti)], q="pool"))
            return th

        def tail_thunks(ti):
            xT = xTs[ti % 2]
            xk = "dxT%d" % (ti % 2)
            th = chain_thunks(xT, xk, k.gH, 5, ("cast", xnB, "xnB"))
            th.append(lambda: P2(ti, (0,)))
            th.append(lambda: P2(ti, (1,)))
            th += chain_thunks(xT, xk, k.gT, 6, None)
            th += O_thunks(ti)
            if ti + 2 < NT:
                th.append(lambda: P1_loads(ti + 2))
            return th

        def tail_schedule(ti):
            th = tail_thunks(ti)
            place = {0: th[0:8], 2: th[8:25], 7: th[25:26], 9: th[26:30], 11: th[30:34], 13: th[34:43],
                     17: th[43:45], 18: th[45:47], 19: th[47:49], 20: th[49:51], 21: th[51:]}
            assert sum(len(v) for v in place.values()) == len(th)
            return place

        def c1_thunks(ti):
            xT = xTs[ti % 2]
            xk = "dxT%d" % (ti % 2)
            return chain_thunks(xT, xk, k.gT, 3, ("norm", 4, xnA, "xnA"))

        P1_loads(0)
        P1(0)
        for f in c1_thunks(0):
            f()
        ffn_gu(k, "w_ffn2_gu", xnA, "xnA")
        for ti in range(NT):
            if ti + 1 < NT:
                if ti == 0:
                    P1_loads(1)
                P1(ti + 1)
                ffn_down(k, "w_ffn2_down", yT, "yT", hooks=c1_thunks(ti + 1))
                ffn_gu(k, "w_ffn2_gu", xnA, "xnA", hooks=tail_thunks(ti))
            else:
                ffn_down(k, "w_ffn2_down", yT, "yT")
                for f in tail_thunks(ti):
                    f()


def _rel_bucket_np(n):
    n = np.maximum(n, 0)
    nf = np.maximum(n, 1).astype(np.float32)
    large = 16 + (np.log(nf / np.float32(16)) / np.float32(np.log(128 / 16)) * np.float32(16)).astype(np.int32)
    large = np.minimum(large, 31)
    return np.where(n < 16, n, large)


def _consts():
    c = {"ident": np.eye(128, dtype=np.float32)}
    c["cJ"] = np.ascontiguousarray(np.eye(128, dtype=np.float32)[::-1])
    j = np.arange(768)
    d = j - 255
    ohp = np.zeros((32, 768), np.float32)
    pos = d >= 0
    bk = _rel_bucket_np(np.maximum(d, 0))
    ohp[bk[pos], j[pos]] += 1.0
    ohp[31, j[pos]] -= 1.0
    c["cOhp"] = ohp
    cm = np.zeros((8, 768), np.float32)
    cm[:, ~pos] = NEG
    c["cCm"] = cm
    ka = np.zeros((32, NTOK), np.float32)
    ka[np.arange(NTOK) // 256, np.arange(NTOK)] = 1.0
    c["cKaug"] = ka
    oo = np.zeros((128, 16, 32), np.float32)
    for i in range(16):
        oo[:, i, 16 + i] = 1.0
    c["cOwnoh"] = oo
    s_ = np.arange(128)
    t_ = np.arange(128)
    same = (s_[:, None] // 64) == (t_[None, :] // 64)
    m2 = (same & (s_[:, None] <= t_[None, :])).astype(np.float32) - (same & ((s_[:, None] % 64) <= 31)).astype(np.float32)
    c["cM2"] = m2.astype(np.float32)
    sel = np.zeros((128, 4), np.float32)
    sel[(s_ < 64) & (s_ % 64 <= 31), 0] = 1
    sel[(s_ < 64) & (s_ % 64 > 31), 1] = 1
    sel[(s_ >= 64) & (s_ % 64 <= 31), 2] = 1
    sel[(s_ >= 64) & (s_ % 64 > 31), 3] = 1
    c["cSel"] = sel
    c["cTri"] = ((s_[:, None] % 64) <= np.arange(64)[None, :]).astype(np.float32)
    return c


def _gmask(half):
    g = np.zeros((128, 16, 32), np.float32)
    for i in range(16):
        g[:, i, 16 + i:] = -1e30
    if half == 0:
        g[:, :, 0:16] = -1e30
    return g


def make_in_maps(inputs):
    x = np.asarray(inputs["x"], np.float32)
    p = np.asarray(inputs["p"], np.float32)
    common = {n: np.ascontiguousarray(np.asarray(inputs[n], np.float32)[0]) for n, a, b in WSPEC}
    g = np.asarray(inputs["norm_gains"], np.float32)[0]
    common["gainsT"] = np.ascontiguousarray(g.reshape(7, 8, 128).transpose(2, 0, 1).reshape(128, 56))
    common["hg_norm_w"] = np.ascontiguousarray(np.asarray(inputs["hg_norm_w"], np.float32).reshape(1, 128))
    common["lb_param"] = np.ascontiguousarray(np.asarray(inputs["lb_param"], np.float32))
    common["rel_table"] = np.ascontiguousarray(np.asarray(inputs["rel_table"], np.float32))
    common.update(_consts())
    maps = []
    for c in range(8):
        b, half = c // 2, c % 2
        m = dict(common)
        xc = np.zeros((NTOK, D), np.float32)
        if half == 1:
            xc[:NCTX] = x[b, :NCTX]
        xc[NCTX:] = x[b, half * NOWN:(half + 1) * NOWN]
        m["xc"] = xc
        m["gmask"] = _gmask(half)
        m["pc"] = np.ascontiguousarray(p[0, b, half * NOWN:(half + 1) * NOWN])
        maps.append(m)
    return maps


def kernel(**inputs):
    nc = build_nc(debug=("OAT", "OBT", "TVH"))
    maps = make_in_maps(inputs)
    res = run_bass_kernel_spmd(nc, maps, core_ids=list(range(8)))
    out = np.zeros((4, 8192, D), np.float32)
    for c in range(8):
        b, half = c // 2, c % 2
        out[b, half * NOWN:(half + 1) * NOWN] = res.results[c]["out"]
    return out
```

```python
import numpy as np
import concourse.bass as bass
import concourse.mybir as mybir
from concourse.bass_utils import run_bass_kernel_spmd
from contextlib import ExitStack

F32 = mybir.dt.float32
BF16 = mybir.dt.bfloat16
ALU = mybir.AluOpType
AF = mybir.ActivationFunctionType
AX = mybir.AxisListType

COMPUTE = ("pe", "act", "dve", "pool")
ENGS = ("pe", "act", "dve", "pool", "sp")
NDMASEM = 6

D = 1024
DFF = 2816
T = 512
NCTX = 4096
NOWN = 4096
NTOK = NCTX + NOWN
EPS = 1e-6
NEG = -30000.0


class _Op:
    __slots__ = ("eng", "fn", "deps", "dma", "sig", "ticket", "semi", "semval")

    def __init__(self, eng, fn, dma):
        self.eng = eng
        self.fn = fn
        self.dma = dma
        self.deps = []
        self.sig = False
        self.ticket = None
        self.semi = None
        self.semval = None


class Sched:
    def __init__(self, nc, stack):
        self.nc = nc
        self.pending = {e: [] for e in ENGS}
        self.lastw = {}
        self.readers = {}
        self.tick = {e: 0 for e in COMPUTE}
        self.sem = {e: stack.enter_context(nc.semaphore("s_" + e)) for e in COMPUTE}
        self.dsem = {}
        self.dcnt = {}
        self.dlast = {}
        for q in ("sp", "act", "pool"):
            self.dsem[q] = [stack.enter_context(nc.semaphore("d_%s%d" % (q, i))) for i in range(NDMASEM)]
            self.dcnt[q] = 0
            self.dlast[q] = [None] * NDMASEM
        self.waited = {}
        self.lastreal = {e: None for e in COMPUTE}

    def add(self, eng, fn, reads=(), writes=(), dma=False):
        op = _Op(eng, fn, dma)
        deps = []
        for k in reads:
            w = self.lastw.get(k)
            if w is not None:
                deps.append((w, "raw"))
        for k in writes:
            w = self.lastw.get(k)
            if w is not None:
                deps.append((w, "waw"))
            for r in self.readers.get(k, ()):
                deps.append((r, "war"))
        seen = set()
        for p, kind in deps:
            if p is op or id(p) in seen:
                continue
            if not p.dma and p.eng == eng and not dma:
                if eng == "pe" or kind != "raw":
                    continue
            seen.add(id(p))
            op.deps.append(p)
            if not p.dma:
                p.sig = True
        if dma:
            q = eng
            i = self.dcnt[q] % NDMASEM
            self.dcnt[q] += 1
            prev = self.dlast[q][i]
            if prev is not None and id(prev) not in seen:
                op.deps.append(prev)
            op.semi = (q, i)
            op.semval = 16 * ((self.dcnt[q] - 1) // NDMASEM + 1)
            self.dlast[q][i] = op
        else:
            self.lastreal[eng] = op
        for k in reads:
            self.readers.setdefault(k, []).append(op)
        for k in writes:
            self.lastw[k] = op
            self.readers[k] = []
        self.pending[eng].append(op)
        return op

    def pe(self, fn, r=(), w=()):
        return self.add("pe", fn, r, w)

    def act(self, fn, r=(), w=()):
        return self.add("act", fn, r, w)

    def dve(self, fn, r=(), w=()):
        return self.add("dve", fn, r, w)

    def pool(self, fn, r=(), w=()):
        return self.add("pool", fn, r, w)

    def dma(self, out, in_, r=(), w=(), q="sp", **kw):
        return self.add(q, lambda e: e.dma_start(out=out, in_=in_, **kw), r, w, dma=True)

    def barrier(self):
        deps = []
        for e in COMPUTE:
            p = self.lastreal[e]
            if p is not None:
                p.sig = True
                deps.append(p)
        for q in self.dlast:
            for p in self.dlast[q]:
                if p is not None:
                    deps.append(p)
        for e in ENGS:
            w = _Op(e, None, False)
            w.deps = [p for p in deps if (p.dma or p.eng != e)]
            self.pending[e].append(w)
        self.lastw = {}
        self.readers = {}

    def flush(self):
        nc = self.nc
        for e in COMPUTE:
            for op in self.pending[e]:
                if op.sig and op.ticket is None:
                    self.tick[e] += 1
                    op.ticket = self.tick[e]
        pend = self.pending
        self.pending = {e: [] for e in ENGS}
        sched = self

        def emit(engname, engine):
            for op in pend[engname]:
                for p in op.deps:
                    if p.dma:
                        s = sched.dsem[p.semi[0]][p.semi[1]]
                        key = (engname, "d", p.semi)
                        val = p.semval
                    else:
                        assert p.ticket is not None
                        s = sched.sem[p.eng]
                        key = (engname, "c", p.eng)
                        val = p.ticket
                    if sched.waited.get(key, 0) >= val:
                        continue
                    sched.waited[key] = val
                    engine.wait_ge(s, val)
                if op.fn is None:
                    continue
                ins = op.fn(engine)
                if op.dma:
                    ins.then_inc(sched.dsem[op.semi[0]][op.semi[1]], 16)
                elif op.sig:
                    ins.then_inc(sched.sem[op.eng], 1)

        with nc.Block() as block:
            @block.tensor
            def _(e):
                emit("pe", e)

            @block.scalar
            def _(e):
                emit("act", e)

            @block.vector
            def _(e):
                emit("dve", e)

            @block.gpsimd
            def _(e):
                emit("pool", e)

            @block.sync
            def _(e):
                emit("sp", e)


WSPEC = [
    ("w_ffn1_gu", D, 2 * DFF), ("w_ffn1_down", DFF, D), ("w_in", D, 5632),
    ("w_branch_a", 512, D), ("w_branch_b", 512, D), ("w_out", D, D),
    ("w_ffn2_gu", D, 2 * DFF), ("w_ffn2_down", DFF, D), ("w_ple", 256, D), ("w_ple_gate", D, D),
]


class K:
    pass


def build_nc(debug=(), phases="ABCD"):
    nc = bass.Bass("TRN2", target_bir_lowering=False)
    k = K()
    k.nc = nc
    din = lambda n, s, d=F32: nc.dram_tensor(n, list(s), d, kind="ExternalInput").ap()

    def dscr(n, s, d):
        kind = "ExternalOutput" if n in debug else "Internal"
        return nc.dram_tensor(n, list(s), d, kind=kind).ap()

    k.xc = din("xc", [NTOK, D])
    k.pc = din("pc", [NOWN, 256])
    k.w = {n: din(n, [a, b]) for n, a, b in WSPEC}
    k.gainsT = din("gainsT", [128, 56])
    k.hgw = din("hg_norm_w", [1, 128])
    k.lbp = din("lb_param", [2, 512])
    k.rel = din("rel_table", [32, 8])
    k.ident = din("ident", [128, 128])
    k.out = nc.dram_tensor("out", [NOWN, D], F32, kind="ExternalOutput").ap()

    k.wb = {n: dscr("wb_" + n, [a, b], BF16) for n, a, b in WSPEC}
    k.H1T = dscr("H1T", [128, 8, NOWN], F32)
    k.QT = dscr("QT", [4, 128, NOWN], BF16)
    k.KT = dscr("KT", [4, 128, NTOK], BF16)
    k.V = dscr("V", [NTOK, 520], BF16)
    k.HF = dscr("HF", [NTOK, 512], F32)
    k.HI = dscr("HI", [NTOK, 512], BF16)
    k.HQ = dscr("HQ", [NOWN, 512], F32)
    k.HG = dscr("HG", [NOWN, 512], F32)
    k.SGA = dscr("SGA", [8, 128, NOWN], BF16)
    k.SGB = dscr("SGB", [8, 128, NOWN], BF16)
    k.OAT = dscr("OAT", [4, 128, NOWN], BF16)
    k.OBT = dscr("OBT", [4, 128, NOWN], BF16)
    k.TVH = dscr("TVH", [8, 768], BF16)
    k.TVL = dscr("TVL", [8, 768], BF16)
    k.cJ = din("cJ", [128, 128])
    k.cOhp = din("cOhp", [32, 768])
    k.cCm = din("cCm", [8, 768])
    k.cKaug = din("cKaug", [32, NTOK])
    k.gmask = din("gmask", [128, 16, 32])
    k.cOwnoh = din("cOwnoh", [128, 16, 32])
    k.cM2 = din("cM2", [128, 128])
    k.cSel = din("cSel", [128, 4])
    k.cTri = din("cTri", [128, 64])

    with ExitStack() as st0:
        S = Sched(nc, st0)
        k.S = S
        k.ps = [st0.enter_context(nc.psum_tensor("ps%d" % i, [128, 512], F32)) for i in range(8)]
        k.bankctr = 0
        k.bank_excl = set()
        sb0 = lambda n, s, d: st0.enter_context(nc.sbuf_tensor(n, list(s), d))
        k.identf = sb0("identf", [128, 128], F32)
        k.identb = sb0("identb", [128, 128], BF16)
        k.onesb = sb0("onesb", [128, 128], BF16)
        k.gT = sb0("gT", [128, 56], F32)
        k.gH = sb0("gH", [128, 56], F32)
        S.dma(k.identf[:], k.ident, w=["identf"])
        S.dma(k.gT[:], k.gainsT, w=["gT"])
        S.dve(lambda e: e.tensor_copy(k.identb[:], k.identf[:]), r=["identf"], w=["identb"])
        S.pool(lambda e: e.memset(k.onesb[:], 1.0), w=["onesb"])
        S.dve(lambda e: e.tensor_scalar_mul(k.gH[:], k.gT[:], 0.5), r=["gT"], w=["gH"])
        for n, a, b in WSPEC:
            if n == "w_ffn1_gu":
                for (n0, nc_) in [(c * 512, 512) for c in range(5)] + [(2560, 256)]:
                    for off in (0, DFF):
                        S.dma(k.wb[n][:, off + n0:off + n0 + nc_], k.w[n][:, off + n0:off + n0 + nc_], w=[("wbc", n, off + n0)], q="pool")
                continue
            step = 256
            for r0 in range(0, a, step):
                r1 = min(a, r0 + step)
                S.dma(k.wb[n][r0:r1, :], k.w[n][r0:r1, :], w=[("wb", n, r0)], q="pool")
        k.wrows = {n: a for n, a, b in WSPEC}
        if "A" in phases:
            phase_a(k)
            S.barrier()
            S.flush()
        if "B" in phases:
            phase_b(k)
            S.barrier()
            S.flush()
        if "C" in phases:
            phase_c(k)
            S.barrier()
            S.flush()
        if "D" in phases:
            phase_d(k)
        S.barrier()
        S.flush()
    return nc


def wkeys(k, n, kc0, nkc, n0=None):
    if n == "w_ffn1_gu":
        return [("wbc", n, n0)]
    r0 = kc0 * 128
    r1 = (kc0 + nkc) * 128
    return [("wb", n, r) for r in range(0, k.wrows[n], 256) if r < r1 and r + 256 > r0]


def load_w(k, n, kc0, nkc, n0, ncols):
    S = k.S
    slot = k.wctr % k.NW
    k.wctr += 1
    wt = k.wring[slot]
    src = k.wb[n][kc0 * 128:(kc0 + nkc) * 128, n0:n0 + ncols].rearrange("(kc p) n -> p kc n", p=128)
    S.dma(wt[:, 0:nkc, 0:ncols], src, r=wkeys(k, n, kc0, nkc, n0), w=[("w", slot)])
    return wt, ("w", slot)


def rms_stats(k, src, srckey, rs, rskey):
    S = k.S
    for kc in range(8):
        S.act(lambda e, kc=kc: e.activation(k.sq[:, kc, :], src[:, kc, :], AF.Square), r=[srckey], w=[("sq", kc)])
    b = nbank(k)
    for kc in range(8):
        S.pe(lambda e, kc=kc, b=b: e.matmul(k.ps[b][:], lhsT=k.onesb[:], rhs=k.sq[:, kc, :], start=(kc == 0), stop=(kc == 7)),
             r=[("sq", kc), "onesb"], w=[("ps", b)])
    S.act(lambda e, b=b: e.activation(rs[:], k.ps[b][:], AF.Sqrt, scale=1.0 / D, bias=EPS), r=[("ps", b)], w=[rskey])
    S.dve(lambda e: e.reciprocal(rs[:], rs[:]), r=[rskey], w=[rskey])


def norm_apply(k, src, srckey, rs, rskey, gcol, dst, dstkey):
    S = k.S
    for kc in range(8):
        S.dve(lambda e, kc=kc: e.scalar_tensor_tensor(out=dst[:, kc, :], in0=src[:, kc, :], scalar=k.gT[:, gcol * 8 + kc:gcol * 8 + kc + 1],
                                                       in1=rs[:], op0=ALU.mult, op1=ALU.mult),
              r=[srckey, rskey, "gT"], w=[(dstkey, kc)])


def nbank(k):
    while True:
        b = k.bankctr % 8
        k.bankctr += 1
        if b not in k.bank_excl:
            return b


def run_hooks(hooks, n):
    for _ in range(n):
        if hooks:
            hooks.pop(0)()


def stats_thunks(k, src, srckey, rs, rskey):
    S = k.S
    th = []
    for kc in range(8):
        th.append(lambda kc=kc: S.act(lambda e: e.activation(k.sq[:, kc, :], src[:, kc, :], AF.Square), r=[(srckey, kc)], w=[("sq", kc)]))

    def mm():
        b = nbank(k)
        for kc in range(8):
            S.pe(lambda e, kc=kc: e.matmul(k.ps[b][:], lhsT=k.onesb[:], rhs=k.sq[:, kc, :], start=(kc == 0), stop=(kc == 7)),
                 r=[("sq", kc), "onesb"], w=[("ps", b)])
        S.act(lambda e: e.activation(rs[:], k.ps[b][:], AF.Sqrt, scale=1.0 / D, bias=EPS), r=[("ps", b)], w=[rskey])
        S.dve(lambda e: e.reciprocal(rs[:], rs[:]), r=[rskey], w=[rskey])
    th.append(mm)
    return th


def ffn_gu(k, wgu, xn, xnkey, hooks=None):
    S = k.S
    hooks = hooks if hooks is not None else []
    per = -(-len(hooks) // 18) if hooks else 0
    chunks = [(c * 512, 512) for c in range(5)] + [(2560, 256)]
    for (n0, nc_) in chunks:
        wg, wgk = load_w(k, wgu, 0, 8, n0, nc_)
        wu, wuk = load_w(k, wgu, 0, 8, DFF + n0, nc_)
        for j in range(nc_ // 128):
            jb = (n0 // 128) + j
            bg = nbank(k)
            bu = nbank(k)
            for kc in range(8):
                S.pe(lambda e, kc=kc, j=j, bg=bg, wg=wg: e.matmul(k.ps[bg][:], lhsT=wg[:, kc, j * 128:(j + 1) * 128], rhs=xn[:, kc, :], start=(kc == 0), stop=(kc == 7)),
                     r=[wgk, (xnkey, kc)], w=[("ps", bg)])
            for kc in range(8):
                S.pe(lambda e, kc=kc, j=j, bu=bu, wu=wu: e.matmul(k.ps[bu][:], lhsT=wu[:, kc, j * 128:(j + 1) * 128], rhs=xn[:, kc, :], start=(kc == 0), stop=(kc == 7)),
                     r=[wuk, (xnkey, kc)], w=[("ps", bu)])
            sl = jb % 2
            S.act(lambda e, bg=bg, sl=sl: e.activation(k.silu[sl][:], k.ps[bg][:], AF.Silu), r=[("ps", bg)], w=[("silu", sl)])
            S.dve(lambda e, bu=bu, sl=sl, jb=jb: e.tensor_tensor(out=k.aT[:, jb, :], in0=k.silu[sl][:], in1=k.ps[bu][:], op=ALU.mult),
                  r=[("ps", bu), ("silu", sl)], w=[("aT", jb)])
            run_hooks(hooks, per)
    run_hooks(hooks, len(hooks))


def ffn_down(k, wdown, yT, ykey, hooks=None, coarse=False):
    S = k.S
    hooks = hooks if hooks is not None else []
    per = -(-len(hooks) // 5) if hooks else 0
    for half in range(2):
        banks = [nbank(k) for _ in range(4)]
        k.bank_excl = set(banks)
        groups = [(0, 8), (8, 8), (16, 6)]
        for (kc0, nkc) in groups:
            wd, wdk = load_w(k, wdown, kc0, nkc, half * 512, 512)
            for j in range(4):
                for kk in range(nkc):
                    kc = kc0 + kk
                    S.pe(lambda e, j=j, kk=kk, kc=kc, wd=wd, b=banks[j]: e.matmul(k.ps[b][:], lhsT=wd[:, kk, j * 128:(j + 1) * 128], rhs=k.aT[:, kc, :],
                                                                                 start=(kc == 0), stop=(kc == 21)),
                         r=[wdk, ("aT", kc)], w=[("ps", banks[j])])
            if not (half == 1 and kc0 == 16):
                run_hooks(hooks, per)
        k.bank_excl = set()
        for j in range(4):
            S.act(lambda e, j=j, b=banks[j], half=half: e.copy(yT[:, half * 4 + j, :], k.ps[b][:]), r=[("ps", banks[j])], w=[ykey if coarse else (ykey, half * 4 + j)])
    run_hooks(hooks, len(hooks))


def ffn(k, wgu, wdown, xn, xnkey, yT, ykey):
    ffn_gu(k, wgu, xn, xnkey)
    ffn_down(k, wdown, yT, ykey, coarse=True)


def phase_a(k):
    nc = k.nc
    S = k.S
    with ExitStack() as st:
        sb = lambda n, s, d: st.enter_context(nc.sbuf_tensor(n, list(s), d))
        k.NW = 5
        k.wctr = 0
        k.wring = [sb("wring%d" % i, [128, 8, 512], BF16) for i in range(k.NW)]
        xtok = sb("xtok", [128, 4, D], F32)
        xTs = [sb("xT%d" % i, [128, 8, T], F32) for i in range(2)]
        xnA = sb("xnA", [128, 8, T], BF16)
        xnB = sb("xnB", [128, 8, T], BF16)
        k.sq = sb("sq", [128, 8, T], BF16)
        k.aT = sb("aT", [128, 22, T], BF16)
        yT = sb("yT", [128, 8, T], F32)
        k.silu = [sb("silu%d" % i, [128, T], F32) for i in range(2)]
        rs = sb("rs", [128, T], F32)
        fmst = [sb("fmst%d" % i, [128, 4, T], BF16) for i in range(1)]
        tmst_b = [sb("tmstb%d" % i, [128, 4, 512], BF16) for i in range(2)]
        tmst_f = [sb("tmstf%d" % i, [128, 4, 512], F32) for i in range(1)]
        vst = [sb("vst%d" % i, [128, 4, 520], BF16) for i in range(2)]
        for i in range(2):
            S.pool(lambda e, i=i: e.memset(vst[i][:], 1.0), w=[("vst", i)])
        cn = {"fm": 0, "tmb": 0, "tmf": 0, "v": 0}
        NT = NTOK // T

        def front_thunks(ti):
            xT = xTs[ti % 2]
            xk = "xT%d" % (ti % 2)
            t0 = ti * T
            th = []
            th.append(lambda: S.dma(xtok[:], k.xc[t0:t0 + T, :].rearrange("(s p) d -> p s d", p=128), w=["xtok"]))

            def tr(kc):
                b = nbank(k)
                for s in range(4):
                    S.pe(lambda e, s=s: e.transpose(k.ps[b][:, s * 128:(s + 1) * 128], xtok[:, s, kc * 128:(kc + 1) * 128], k.identf[:]),
                         r=["xtok", "identf"], w=[("ps", b)])
                S.act(lambda e: e.copy(xT[:, kc, :], k.ps[b][:]), r=[("ps", b)], w=[(xk, kc)])
            for kc in range(8):
                th.append(lambda kc=kc: tr(kc))
            th += stats_thunks(k, xT, xk, rs, "rs")
            for kc in range(8):
                th.append(lambda kc=kc: S.dve(lambda e: e.scalar_tensor_tensor(out=xnA[:, kc, :], in0=xT[:, kc, :], scalar=k.gT[:, kc:kc + 1], in1=rs[:],
                                                                                op0=ALU.mult, op1=ALU.mult), r=[(xk, kc), "rs", "gT"], w=[("xnA", kc)]))
            return th

        def mid_thunks(ti):
            xT = xTs[ti % 2]
            xk = "xT%d" % (ti % 2)
            own = ti >= NCTX // T
            to = ti * T - NCTX
            th = stats_thunks(k, yT, "yT", rs, "rs")

            def upd(kc):
                S.dve(lambda e: e.tensor_tensor(out=yT[:, kc, :], in0=yT[:, kc, :], in1=rs[:], op=ALU.mult), r=[("yT", kc), "rs"], w=[("yT", kc)])
                S.dve(lambda e: e.scalar_tensor_tensor(out=xT[:, kc, :], in0=yT[:, kc, :], scalar=k.gH[:, 8 + kc:8 + kc + 1], in1=xT[:, kc, :],
                                                       op0=ALU.mult, op1=ALU.add), r=[("yT", kc), (xk, kc), "gH"], w=[(xk, kc)])
            for kc in range(8):
                th.append(lambda kc=kc: upd(kc))
            if own:
                th.append(lambda: S.dma(k.H1T[:, :, to:to + T], xT[:], r=[(xk, kc) for kc in range(8)], w=[("H1T", ti)], q="pool"))
            th += stats_thunks(k, xT, xk, rs, "rs")
            for kc in range(8):
                th.append(lambda kc=kc: S.dve(lambda e: e.scalar_tensor_tensor(out=xnB[:, kc, :], in0=xT[:, kc, :], scalar=k.gT[:, 16 + kc:16 + kc + 1], in1=rs[:],
                                                                                op0=ALU.mult, op1=ALU.mult), r=[(xk, kc), "rs", "gT"], w=[("xnB", kc)]))
            return th

        def win(ti):
            t0 = ti * T
            own = ti >= NCTX // T
            to = t0 - NCTX
            xn = xnB
            xnk = [("xnB", kc) for kc in range(8)]
            fm_groups = [("k", 512, k.KT, t0)]
            if own:
                fm_groups = [("q", 0, k.QT, to), ("k", 512, k.KT, t0), ("ga", 3584, k.SGA, to), ("ga", 4096, k.SGA, to),
                             ("gb", 4608, k.SGB, to), ("gb", 5120, k.SGB, to)]
            for (kind, n0, dst, tt) in fm_groups:
                wt, wk = load_w(k, "w_in", 0, 8, n0, 512)
                st_ = fmst[cn["fm"] % len(fmst)]
                stk = ("fmst", cn["fm"] % len(fmst))
                cn["fm"] += 1
                for j in range(4):
                    b = nbank(k)
                    for kc in range(8):
                        S.pe(lambda e, kc=kc, j=j, b=b, wt=wt: e.matmul(k.ps[b][:], lhsT=wt[:, kc, j * 128:(j + 1) * 128], rhs=xn[:, kc, :], start=(kc == 0), stop=(kc == 7)),
                             r=[wk] + xnk, w=[("ps", b)])
                    if kind == "q":
                        S.act(lambda e, j=j, b=b, st_=st_: e.mul(st_[:, j, :], k.ps[b][:], 0.125), r=[("ps", b)], w=[stk])
                    elif kind == "k":
                        S.act(lambda e, j=j, b=b, st_=st_: e.copy(st_[:, j, :], k.ps[b][:]), r=[("ps", b)], w=[stk])
                    else:
                        S.act(lambda e, j=j, b=b, st_=st_: e.activation(st_[:, j, :], k.ps[b][:], AF.Sigmoid), r=[("ps", b)], w=[stk])
                blk0 = {0: 0, 512: 0, 3584: 0, 4096: 4, 4608: 0, 5120: 4}[n0]
                S.dma(dst[blk0:blk0 + 4, :, tt:tt + T].rearrange("j p t -> p j t"), st_[:], r=[stk], w=[(kind, n0, ti)], q="pool")
            tm_groups = [("v", 1024, k.V, t0, True), ("hf", 2048, k.HF, t0, False), ("hi", 2560, k.HI, t0, True)]
            if own:
                tm_groups += [("hq", 1536, k.HQ, to, False), ("hg", 3072, k.HG, to, False)]
            for (kind, n0, dst, tt, isb) in tm_groups:
                wt, wk = load_w(k, "w_in", 0, 8, n0, 512)
                if kind == "v":
                    st_ = vst[cn["v"] % 2]
                    stk = ("vst", cn["v"] % 2)
                    cn["v"] += 1
                elif isb:
                    st_ = tmst_b[cn["tmb"] % len(tmst_b)]
                    stk = ("tmstb", cn["tmb"] % len(tmst_b))
                    cn["tmb"] += 1
                else:
                    st_ = tmst_f[cn["tmf"] % len(tmst_f)]
                    stk = ("tmstf", cn["tmf"] % len(tmst_f))
                    cn["tmf"] += 1
                for s in range(4):
                    b = nbank(k)
                    for kc in range(8):
                        S.pe(lambda e, kc=kc, s=s, b=b, wt=wt: e.matmul(k.ps[b][:], lhsT=xn[:, kc, s * 128:(s + 1) * 128], rhs=wt[:, kc, :], start=(kc == 0), stop=(kc == 7)),
                             r=[wk] + xnk, w=[("ps", b)])
                    if kind == "v":
                        S.act(lambda e, s=s, b=b, st_=st_: e.copy(st_[:, s, :].rearrange("p (h c) -> p h c", c=65)[:, :, 0:64],
                                                                  k.ps[b][:].rearrange("p (h c) -> p h c", c=64)), r=[("ps", b)], w=[stk])
                    else:
                        S.act(lambda e, s=s, b=b, st_=st_: e.copy(st_[:, s, :], k.ps[b][:]), r=[("ps", b)], w=[stk])
                S.dma(dst[tt:tt + T, :].rearrange("(s p) c -> p s c", p=128), st_[:], r=[stk], w=[(kind, ti)], q="pool")

        for f in front_thunks(0):
            f()
        ffn_gu(k, "w_ffn1_gu", xnA, "xnA")
        for ti in range(NT):
            fr = front_thunks(ti + 1) if ti + 1 < NT else []
            ffn_down(k, "w_ffn1_down", yT, "yT", hooks=fr)
            md = mid_thunks(ti)
            if ti + 1 < NT:
                ffn_gu(k, "w_ffn1_gu", xnA, "xnA", hooks=md)
            else:
                for f in md:
                    f()
            win(ti)


def phase_b(k):
    nc = k.nc
    S = k.S
    with ExitStack() as st:
        sb = lambda n, s, d: st.enter_context(nc.sbuf_tensor(n, list(s), d))
        mctr = [0]

        def mb():
            mctr[0] += 1
            return 6 + (mctr[0] % 2)

        VA = sb("b_VA", [128, 64, 520], BF16)
        KA = [sb("b_KA%d" % i, [96, NTOK], BF16) for i in range(2)]
        QA = [sb("b_QA%d" % i, [96, NOWN], BF16) for i in range(2)]
        sqb = sb("b_sqb", [64, NTOK], BF16)
        qsq = sb("b_qsq", [64, NOWN], BF16)
        Jb = sb("b_Jb", [128, 128], BF16)
        Jf = sb("b_Jf", [128, 128], F32)
        onesf = sb("b_onesf", [128, 64], F32)
        relt = sb("b_relt", [32, 8], F32)
        ohp = sb("b_ohp", [32, 768], F32)
        cmt = sb("b_cm", [8, 768], F32)
        tv = sb("b_tv", [8, 768], F32)
        tvh = sb("b_tvh", [8, 768], BF16)
        tvhf = sb("b_tvhf", [8, 768], F32)
        tvl = sb("b_tvl", [8, 768], BF16)
        gmask = sb("b_gmask", [128, 16, 32], F32)
        ownoh = sb("b_ownoh", [128, 16, 32], F32)
        Bt = [[sb("b_Bt%d_%d" % (i, j), [128, 256], BF16) for j in range(8)] for i in range(2)]
        kmTs = [sb("b_kmT%d" % i, [64, 32], BF16) for i in range(2)]
        kmf = sb("b_kmf", [64, 32], F32)
        kmx = sb("b_kmx", [128, 16], F32)
        kmax2 = sb("b_kmax2", [128, 1], F32)
        gm = sb("b_gm", [128, 32], F32)
        mx8 = sb("b_mx8", [128, 8], F32)
        vv = sb("b_vv", [128, 32], F32)
        sel = sb("b_sel", [128, 32], F32)
        mmall = sb("b_mmall", [128, 32], F32)
        mmball = sb("b_mmball", [128, 32], BF16)
        a01alls = [sb("b_a01all%d" % i, [128, 32], F32) for i in range(2)]
        mm = sb("b_mm", [128, 1], F32)
        mmb = sb("b_mmb", [128, 1], BF16)
        a01 = sb("b_a01", [128, 1], F32)
        rbb = sb("b_rbb", [128, 32], BF16)
        rbbs = [sb("b_rbbs%d" % i, [128, 32], BF16) for i in range(2)]
        PT = [sb("b_PT%d" % i, [128, 512], BF16) for i in range(4)]
        osb = [sb("b_osb%d" % i, [65, 256], F32) for i in range(2)]
        rden = [sb("b_rden%d" % i, [65, 256], F32) for i in range(2)]
        ost = [sb("b_ost%d" % i, [64, 256], BF16) for i in range(2)]

        Vr = k.V.rearrange("(a p) c -> p a c", p=128)
        for g8 in range(8):
            S.dma(VA[:, g8 * 8:(g8 + 1) * 8, :], Vr[:, g8 * 8:(g8 + 1) * 8, :], w=[("VA", g8)])
        S.dma(Jf[:], k.cJ, w=["Jf"])
        S.dve(lambda e: e.tensor_copy(Jb[:], Jf[:]), r=["Jf"], w=["Jb"])
        S.pool(lambda e: e.memset(onesf[:], 1.0), w=["onesf"])
        S.dma(relt[:], k.rel, w=["relt"])
        S.dma(ohp[:], k.cOhp, w=["ohp"])
        S.dma(cmt[:], k.cCm, w=["cmt"])
        S.dma(gmask[:], k.gmask, w=["gmask"])
        S.dma(ownoh[:], k.cOwnoh, w=["ownoh"])
        for i in range(2):
            S.dma(KA[i][64:96, :], k.cKaug, w=[("KAaug", i)], q="pool")
        for (c0, c1) in ((0, 512), (512, 768)):
            b = mb()
            S.pe(lambda e, b=b, c0=c0, c1=c1: e.matmul(k.ps[b][0:8, 0:c1 - c0], lhsT=relt[:], rhs=ohp[:, c0:c1], start=True, stop=True), r=["relt", "ohp"], w=[("ps", b)])
            S.dve(lambda e, b=b, c0=c0, c1=c1: e.tensor_tensor(out=tv[:, c0:c1], in0=k.ps[b][0:8, 0:c1 - c0], in1=cmt[:, c0:c1], op=ALU.add), r=[("ps", b), "cmt"], w=["tv"])
        S.dve(lambda e: e.tensor_copy(tvh[:], tv[:]), r=["tv"], w=["tvh"])
        S.dve(lambda e: e.tensor_copy(tvhf[:], tvh[:]), r=["tvh"], w=["tvhf"])
        S.dve(lambda e: e.tensor_tensor(out=tvl[:], in0=tv[:], in1=tvhf[:], op=ALU.subtract), r=["tv", "tvhf"], w=["tvl"])
        S.dma(k.TVH, tvh[:], r=["tvh"], w=["TVH"], q="pool")
        S.dma(k.TVL, tvl[:], r=["tvl"], w=["TVL"], q="pool")

        ctr = {"s": 0, "o": 0, "p": 0}

        def hv(h):
            hb = h % 2
            return h // 2, (h % 2) * 64, hb, KA[hb], QA[hb], ("KA", hb), ("QA", hb), Bt[hb]

        def setup(h):
            hp, po, hb, ka, qa, kak, qak, bt = hv(h)
            S.dma(ka[0:64, :], k.KT[hp, po:po + 64, :], w=[kak])
            S.dma(qa[0:64, :], k.QT[hp, po:po + 64, :], w=[qak])
            for a in range(2):
                for (idx, off) in ((0, 128 - 128 * a), (1, 384 - 128 * a)):
                    for (lo, src) in ((0, k.TVH), (1, k.TVL)):
                        ap_ = bass.AP(tensor=src.tensor, offset=src[h:h + 1, off:off + 1].offset, ap=[[1, 128], [1, 256]])
                        S.dma(bt[idx * 4 + a * 2 + lo][:], ap_, r=["TVH", "TVL"], w=[("Bt", hb, idx * 4 + a * 2 + lo)])
            S.dve(lambda e, ka=ka: e.tensor_reduce(out=kmf[:], in_=ka[0:64, :].rearrange("p (n c) -> p n c", c=256), axis=AX.X, op=ALU.add), r=[kak], w=["kmf"])
            S.dve(lambda e: e.tensor_scalar_mul(kmTs[hb][:], kmf[:], 1.0 / 256), r=["kmf"], w=[("kmT", hb)])
            S.pool(lambda e, ka=ka: e.tensor_tensor(out=sqb[:], in0=ka[0:64, :], in1=ka[0:64, :], op=ALU.mult), r=[kak], w=["sqb"])
            S.pool(lambda e, qa=qa: e.tensor_tensor(out=qsq[:], in0=qa[0:64, :], in1=qa[0:64, :], op=ALU.mult), r=[qak], w=["qsq"])
            for c in range(16):
                b = mb()
                S.pe(lambda e, b=b, c=c: e.matmul(k.ps[b][:], lhsT=k.onesb[0:64, :], rhs=sqb[:, c * 512:(c + 1) * 512], start=True, stop=True), r=["sqb", "onesb"], w=[("ps", b)])
                S.dve(lambda e, b=b, c=c: e.tensor_reduce(out=kmx[:, c:c + 1], in_=k.ps[b][:], axis=AX.X, op=ALU.max), r=[("ps", b)], w=["kmx"])
            S.dve(lambda e: e.tensor_reduce(out=kmax2[:], in_=kmx[:], axis=AX.X, op=ALU.max), r=["kmx"], w=["kmax2"])
            bq = mb()
            for qt_ in range(32):
                S.pe(lambda e, bq=bq, qt_=qt_: e.matmul(k.ps[bq][:, qt_:qt_ + 1], lhsT=qsq[:, qt_ * 128:(qt_ + 1) * 128], rhs=k.onesb[0:64, 0:1], start=True, stop=True),
                     r=["qsq", "onesb"], w=[("ps", bq)])
            S.dve(lambda e, bq=bq: e.tensor_scalar_mul(mmall[:], k.ps[bq][:, 0:32], kmax2[:, 0:1]), r=[("ps", bq), "kmax2"], w=["mmall"])
            S.act(lambda e: e.activation(mmall[:], mmall[:], AF.Ln), r=["mmall"], w=["mmall"])
            S.act(lambda e: e.activation(mmball[:], mmall[:], AF.Exp, scale=0.5), r=["mmall"], w=["mmball"])
            S.dve(lambda e: e.tensor_scalar(a01alls[hb][:], mmball[:], -1.0, -NEG, op0=ALU.mult, op1=ALU.add), r=["mmball"], w=[("a01all", hb)])

        def gate_parts(h, i):
            hp, po, hb, ka, qa, kak, qak, bt = hv(h)
            qc = slice(i * 256, (i + 1) * 256)
            bT = 6 + (i % 2)
            pbT = k.ps[bT][:].bitcast(BF16)
            bg = 7 - (i % 2)

            def g_mm(qt_):
                qcols = slice(i * 256 + qt_ * 128, i * 256 + (qt_ + 1) * 128)
                S.pe(lambda e: e.matmul(k.ps[bg][:, 0:32], lhsT=qa[0:64, qcols], rhs=kmTs[hb][:], start=True, stop=True), r=[qak, ("kmT", hb)], w=[("ps", bg)])
                S.dve(lambda e: e.tensor_tensor(out=gm[:], in0=k.ps[bg][:, 0:32], in1=gmask[:, i, :], op=ALU.add), r=[("ps", bg), "gmask"], w=["gm"])
                S.dve(lambda e: e.max(out=mx8[:], in_=gm[:]), r=["gm"], w=["mx8"])
                S.dve(lambda e: e.tensor_single_scalar(vv[:], gm[:], -1e29, op=ALU.is_gt), r=["gm"], w=["vv"])
                S.dve(lambda e: e.scalar_tensor_tensor(out=sel[:], in0=gm[:], scalar=mx8[:, 2:3], in1=vv[:], op0=ALU.is_ge, op1=ALU.mult), r=["gm", "mx8", "vv"], w=["sel"])
                S.dve(lambda e: e.tensor_tensor(out=sel[:], in0=sel[:], in1=ownoh[:, i, :], op=ALU.add), r=["sel", "ownoh"], w=["sel"])
                qti = i * 2 + qt_
                S.dve(lambda e: e.tensor_scalar(rbbs[qt_][:], sel[:], a01alls[hb][:, qti:qti + 1], NEG, op0=ALU.mult, op1=ALU.add), r=["sel", ("a01all", hb)], w=[("rbb", qt_)])

            def g_tr(qt_):
                S.pe(lambda e: e.transpose(pbT[0:32, qt_ * 128:(qt_ + 1) * 128], rbbs[qt_][:], k.identb[:]), r=[("rbb", qt_), "identb"], w=[("ps", bT)])

            def g_copy():
                S.dve(lambda e: e.tensor_copy(qa[64:96, qc], pbT[0:32, 0:256]), r=[("ps", bT)], w=[("QAaug", hb, i)])

            def p0():
                g_mm(0)

            def p1():
                g_tr(0)
                g_mm(1)

            def p2():
                g_tr(1)
                g_copy()
            return [p0, p1, p2]

        def main(h, i, hooks):
            hp, po, hb, ka, qa, kak, qak, bt = hv(h)
            I = 16 + i
            qc = slice(i * 256, (i + 1) * 256)
            bO = 4 + (ctr["o"] % 2)
            ctr["o"] += 1

            def rec_S(n):
                bS = ctr["s"] % 3
                ctr["s"] += 1
                special = n >= I - 1
                for a in range(2):
                    kcols = slice(n * 256 + a * 128, n * 256 + (a + 1) * 128)
                    S.pe(lambda e, bS=bS, a=a, kcols=kcols, special=special: e.matmul(
                        k.ps[bS][:, a * 256:(a + 1) * 256], lhsT=ka[0:96, kcols], rhs=qa[0:96, qc], start=True, stop=(not special)),
                        r=[kak, qak, ("KAaug", hb), ("QAaug", hb, i)], w=[("ps", bS)])
                    if special:
                        idx = 0 if n == I else 1
                        for lo in range(2):
                            S.pe(lambda e, bS=bS, a=a, idx=idx, lo=lo: e.matmul(
                                k.ps[bS][:, a * 256:(a + 1) * 256], lhsT=Jb[:], rhs=bt[idx * 4 + a * 2 + lo][:], start=False, stop=(lo == 1)),
                                r=["Jb", ("Bt", hb, idx * 4 + a * 2 + lo)], w=[("ps", bS)])
                return bS

            def rec_PV(n, bS):
                pi = ctr["p"] % 4
                ctr["p"] += 1
                S.act(lambda e, bS=bS, pi=pi: e.activation(PT[pi][:], k.ps[bS][:], AF.Exp), r=[("ps", bS)], w=[("PT", pi)])
                for a in range(2):
                    ktile = n * 2 + a
                    S.pe(lambda e, a=a, ktile=ktile, pi=pi, first=(n == 0 and a == 0), last=(n == I and a == 1): e.matmul(
                        k.ps[bO][0:65, 0:256], lhsT=VA[:, ktile, h * 65:(h + 1) * 65], rhs=PT[pi][:, a * 256:(a + 1) * 256], start=first, stop=last),
                        r=[("VA", ktile // 8), ("PT", pi)], w=[("ps", bO)])

            DEPTH = 2
            banks = {}
            for n in range(min(DEPTH, I + 1)):
                banks[n] = rec_S(n)
            for n in range(I + 1):
                if n + DEPTH <= I:
                    banks[n + DEPTH] = rec_S(n + DEPTH)
                rec_PV(n, banks.pop(n))
                for fn in hooks.get(n, ()):
                    fn()
            oi = ctr["o"] % 2
            S.dve(lambda e: e.tensor_copy(osb[oi][:], k.ps[bO][0:65, 0:256]), r=[("ps", bO)], w=[("osb", oi)])
            S.dve(lambda e: e.reciprocal(rden[oi][64:65, :], osb[oi][64:65, :]), r=[("osb", oi)], w=[("rden", oi)])

            def fin():
                bB = 3
                S.pe(lambda e: e.matmul(k.ps[bB][0:64, 0:256], lhsT=onesf[64:65, :], rhs=rden[oi][64:65, :], start=True, stop=True), r=["onesf", ("rden", oi)], w=[("ps", bB)])
                S.dve(lambda e: e.tensor_tensor(out=ost[oi][:], in0=osb[oi][0:64, :], in1=k.ps[bB][0:64, 0:256], op=ALU.mult), r=[("osb", oi), ("ps", bB)], w=[("ost", oi)])
                S.dma(k.OAT[hp, po:po + 64, qc], ost[oi][:], r=[("ost", oi)], w=["OAT"], q="pool")
            return fin

        blocks = [(h, i) for h in range(8) for i in range(16)]
        setup(0)
        for p in gate_parts(0, 0):
            p()
        pending_fin = None
        for bi, (h, i) in enumerate(blocks):
            hooks = {}
            if pending_fin is not None:
                hooks.setdefault(3, []).append(pending_fin)
            if i == 10 and h + 1 < 8:
                hooks.setdefault(1, []).append(lambda h2=h + 1: setup(h2))
            if bi + 1 < len(blocks):
                h2, i2 = blocks[bi + 1]
                parts = gate_parts(h2, i2)
                if h2 != h:
                    steps = (14, 19, 24)
                else:
                    steps = (1, 6, 11)
                for st_, p in zip(steps, parts):
                    hooks.setdefault(st_, []).append(p)
            pending_fin = main(h, i, hooks)
        pending_fin()


def phase_c(k):
    nc = k.nc
    S = k.S
    LN_HALF = -0.6931471805599453
    with ExitStack() as st:
        sb = lambda n, s, d: st.enter_context(nc.sbuf_tensor(n, list(s), d))

        class Ring:
            def __init__(self, name, n, shape, dt):
                self.name = name
                self.t = [sb("c_%s%d" % (name, i), shape, dt) for i in range(n)]

            def at(self, s):
                i = s % len(self.t)
                return self.t[i], (self.name, i)

        lbp2 = sb("c_lbp2", [128, 2, 512], F32)
        lb = sb("c_lb", [128, 512], F32)
        c0 = sb("c_c0", [128, 512], F32)
        c1 = sb("c_c1", [128, 512], F32)
        hgwb = sb("c_hgwb", [128, 512], F32)
        M2 = sb("c_M2", [128, 128], F32)
        Sel = sb("c_Sel", [128, 4], F32)
        tri = sb("c_tri", [128, 64], F32)
        state = sb("c_state", [128, 4, 128], F32)
        stfs = [sb("c_stf%d" % i, [128, 4, 128], F32) for i in range(2)]
        tmp = sb("c_tmp", [128, 4, 128], F32)
        R_hf = Ring("hf", 2, [128, 512], F32)
        R_hi = Ring("hi", 6, [128, 512], BF16)
        R_hq = Ring("hq", 4, [128, 512], F32)
        R_hg = Ring("hg", 4, [128, 512], F32)
        R_thq = Ring("thq", 3, [128, 512], F32)
        R_thg = Ring("thg", 3, [128, 512], F32)
        R_f = Ring("f", 2, [128, 512], F32)
        R_logf = Ring("logf", 2, [128, 512], F32)
        R_kin = Ring("kin", 2, [128, 512], F32)
        R_kt = Ring("kt", 2, [128, 512], BF16)
        R_qTs = Ring("qTs", 3, [128, 4, 128], BF16)
        R_kTs = Ring("kTs", 3, [128, 4, 128], BF16)
        R_dec = Ring("dec", 2, [128, 16], F32)
        R_t2 = Ring("t2", 4, [128, 512], F32)
        R_AT = Ring("AT", 2, [128, 4, 64], BF16)
        R_spb = [Ring("spb%d" % c, 2, [128, 4, 128], BF16) for c in range(2)]
        R_obuf = Ring("obuf", 2, [128, 4, 128], F32)
        thf = sb("c_thf", [128, 512], F32)
        Em = sb("c_Em", [128, 512], F32)
        Ep = sb("c_Ep", [128, 512], F32)
        q1 = sb("c_q1", [128, 512], F32)
        qt = sb("c_qt", [128, 512], BF16)
        junk = sb("c_junk", [128, 128], F32)
        ss = sb("c_ss", [128, 4], F32)
        ob = sb("c_ob", [128, 4, 128], F32)
        obb = sb("c_obb", [128, 512], BF16)
        obst = [sb("c_obst%d" % i, [128, 4, 512], BF16) for i in range(2)]

        S.dma(lbp2[:, 0, :], k.lbp[0:1, :].partition_broadcast(128), w=[("lbp2", 0)])
        S.dma(lbp2[:, 1, :], k.lbp[1:2, :].partition_broadcast(128), w=[("lbp2", 1)])
        for h in range(4):
            S.dma(hgwb[:, h * 128:(h + 1) * 128], k.hgw[0:1, :].partition_broadcast(128), w=[("hgwb", h)])
        S.dma(M2[:], k.cM2, w=["M2"])
        S.dma(Sel[:], k.cSel, w=["Sel"])
        S.dma(tri[:], k.cTri, w=["tri"])
        S.dve(lambda e: e.tensor_tensor(out=lb[:], in0=lbp2[:, 1, :], in1=lbp2[:, 0, :], op=ALU.subtract), r=[("lbp2", 0), ("lbp2", 1)], w=["lb"])
        S.act(lambda e: e.activation(lb[:], lb[:], AF.Exp), r=["lb"], w=["lb"])
        S.dve(lambda e: e.tensor_scalar_add(lb[:], lb[:], 1.0), r=["lb"], w=["lb"])
        S.dve(lambda e: e.reciprocal(lb[:], lb[:]), r=["lb"], w=["lb"])
        S.dve(lambda e: e.tensor_scalar(c1[:], lb[:], -0.5, 0.5, op0=ALU.mult, op1=ALU.add), r=["lb"], w=["c1"])
        S.dve(lambda e: e.tensor_tensor(out=c0[:], in0=lb[:], in1=c1[:], op=ALU.add), r=["lb", "c1"], w=["c0"])
        S.dve(lambda e: e.tensor_scalar_mul(hgwb[:], hgwb[:], 0.5), r=[("hgwb", h) for h in range(4)], w=["hgwb"])
        S.pool(lambda e: e.memset(state[:], 0.0), w=["state"])

        nsteps = NTOK // 128
        first_own = NCTX // 128
        prs = [slice(0, 64), slice(64, 128)]
        k.bank_excl = {0, 1, 2}
        ctxs = {}
        chs = {}

        def E_a(s):
            own = s >= first_own
            t0 = s * 128
            to = t0 - NCTX
            hf, hfk = R_hf.at(s)
            hi, hik = R_hi.at(s)
            f_, fk = R_f.at(s)
            S.dma(hf[:], k.HF[t0:t0 + 128, :], w=[hfk])
            S.dma(hi[:], k.HI[t0:t0 + 128, :], w=[hik])
            S.act(lambda e: e.activation(thf[:], hf[:], AF.Tanh, scale=0.5), r=[hfk], w=["thf"])
            if own:
                hq, hqk = R_hq.at(s)
                hg, hgk = R_hg.at(s)
                thq, thqk = R_thq.at(s)
                thg, thgk = R_thg.at(s)
                S.dma(hq[:], k.HQ[to:to + 128, :], w=[hqk])
                S.dma(hg[:], k.HG[to:to + 128, :], w=[hgk])
                S.act(lambda e: e.activation(thq[:], hq[:], AF.Tanh, scale=0.5), r=[hqk], w=[thqk])
                S.act(lambda e: e.activation(thg[:], hg[:], AF.Tanh, scale=0.5), r=[hgk], w=[thgk])
            S.dve(lambda e: e.tensor_tensor(out=f_[:], in0=thf[:], in1=c1[:], op=ALU.mult), r=["thf", "c1"], w=[fk])
            S.dve(lambda e: e.tensor_tensor(out=f_[:], in0=f_[:], in1=c0[:], op=ALU.add), r=[fk, "c0"], w=[fk])

        def E_b(s):
            f_, fk = R_f.at(s)
            logf, lk = R_logf.at(s)
            kin, kk = R_kin.at(s)
            S.act(lambda e: e.activation(logf[:], f_[:], AF.Ln), r=[fk], w=[lk])
            S.act(lambda e: e.activation(kin[:], f_[:], AF.Identity, scale=-1.0, bias=1.0), r=[fk], w=[kk])
            bbp = s % 2
            S.pe(lambda e: e.matmul(k.ps[bbp][:], lhsT=M2[:], rhs=logf[:], start=True, stop=True), r=["M2", lk], w=[("ps", bbp)])
            bdec = 2
            co = (s % 2) * 16
            for h in range(4):
                S.pe(lambda e, h=h: e.matmul(k.ps[bdec][:, co + h * 4:co + (h + 1) * 4], lhsT=logf[:, h * 128:(h + 1) * 128], rhs=Sel[:], start=True, stop=True),
                     r=[lk, "Sel"], w=[("psdec", s % 2)])
            ctxs[s] = (bbp, bdec, co)

        def E_c(s):
            own = s >= first_own
            bbp, bdec, co = ctxs.pop(s)
            kin, kk = R_kin.at(s)
            dec, dk = R_dec.at(s)
            kt, ktk = R_kt.at(s)
            S.act(lambda e: e.activation(Em[:], k.ps[bbp][:], AF.Exp, scale=-1.0), r=[("ps", bbp)], w=["Em"])
            if own:
                S.act(lambda e: e.activation(Ep[:], k.ps[bbp][:], AF.Exp, bias=LN_HALF), r=[("ps", bbp)], w=["Ep"])
            S.act(lambda e: e.activation(dec[:], k.ps[bdec][:, co:co + 16], AF.Exp), r=[("psdec", s % 2)], w=[dk])
            S.dve(lambda e: e.tensor_tensor(out=kt[:], in0=kin[:], in1=Em[:], op=ALU.mult), r=[kk, "Em"], w=[ktk])
            if own:
                hq, hqk = R_hq.at(s)
                hg, hgk = R_hg.at(s)
                thq, thqk = R_thq.at(s)
                thg, thgk = R_thg.at(s)
                t2, t2k = R_t2.at(s)
                qTs, qTk = R_qTs.at(s)
                kTs, kTk = R_kTs.at(s)
                S.dve(lambda e: e.scalar_tensor_tensor(out=q1[:], in0=thq[:], scalar=1.0, in1=hq[:], op0=ALU.add, op1=ALU.mult), r=[thqk, hqk], w=["q1"])
                S.dve(lambda e: e.tensor_tensor(out=qt[:], in0=q1[:], in1=Ep[:], op=ALU.mult), r=["q1", "Ep"], w=["qt"])
                S.dve(lambda e: e.scalar_tensor_tensor(out=t2[:], in0=thg[:], scalar=1.0, in1=hg[:], op0=ALU.add, op1=ALU.mult), r=[thgk, hgk], w=[t2k])
                S.dve(lambda e: e.tensor_tensor(out=t2[:], in0=t2[:], in1=hgwb[:], op=ALU.mult), r=[t2k, "hgwb"], w=[t2k])
                for (src, srck, dst, dstk) in ((qt, "qt", qTs, qTk), (kt, ktk, kTs, kTk)):
                    b = nbank(k)
                    pb = k.ps[b][:].bitcast(BF16)
                    for h in range(4):
                        S.pe(lambda e, h=h, pb=pb, src=src: e.transpose(pb[:, h * 128:(h + 1) * 128], src[:, h * 128:(h + 1) * 128], k.identb[:]),
                             r=[srck, "identb"], w=[("ps", b)])
                    S.act(lambda e, pb=pb, dst=dst: e.copy(dst[:].rearrange("p h t -> p (h t)"), pb[:, 0:512]), r=[("ps", b)], w=[dstk])

        def Ch_pe1(s):
            own = s >= first_own
            kt, ktk = R_kt.at(s)
            hi, hik = R_hi.at(s)
            bKs = []
            for c in range(2):
                pr = prs[c]
                bK = nbank(k)
                bKs.append(bK)
                for h in range(4):
                    S.pe(lambda e, h=h, bK=bK, pr=pr: e.matmul(k.ps[bK][:, h * 128:(h + 1) * 128], lhsT=kt[pr, h * 128:(h + 1) * 128], rhs=hi[pr, h * 128:(h + 1) * 128], start=True, stop=True),
                         r=[ktk, hik], w=[("ps", bK)])
            chs[s] = bKs
            if own:
                qTs, qTk = R_qTs.at(s)
                kTs, kTk = R_kTs.at(s)
                AT, ATk = R_AT.at(s)
                for c in range(2):
                    pr = prs[c]
                    bA = nbank(k)
                    for h in range(4):
                        S.pe(lambda e, h=h, bA=bA, pr=pr: e.matmul(k.ps[bA][0:64, h * 64:(h + 1) * 64], lhsT=kTs[:, h, pr], rhs=qTs[:, h, pr], start=True, stop=True),
                             r=[kTk, qTk], w=[("ps", bA)])
                    S.dve(lambda e, bA=bA, pr=pr: e.tensor_tensor(out=AT[pr], in0=k.ps[bA][0:64, 0:256].rearrange("p (h t) -> p h t", t=64),
                                                                  in1=tri[0:64, :].unsqueeze(1).to_broadcast([64, 4, 64]), op=ALU.mult),
                          r=[("ps", bA), "tri"], w=[(ATk, c)])

        def Ch_rec(s):
            own = s >= first_own
            dec, dk = R_dec.at(s)
            bKs = chs.pop(s)
            for c in range(2):
                dmid = dec[:, 2 * c:16:4].unsqueeze(2).to_broadcast([128, 4, 128])
                dlast = dec[:, 2 * c + 1:16:4].unsqueeze(2).to_broadcast([128, 4, 128])
                bK = bKs[c]
                S.dve(lambda e, dmid=dmid, c=c: e.tensor_tensor(out=stfs[c][:], in0=state[:], in1=dmid, op=ALU.mult), r=["state", dk], w=[("stf", c)])
                if own:
                    spb, spk = R_spb[c].at(s)
                    S.act(lambda e, c=c, spb=spb: e.copy(spb[:], stfs[c][:]), r=[("stf", c)], w=[spk])
                S.dve(lambda e, bK=bK, c=c: e.tensor_tensor(out=tmp[:], in0=k.ps[bK][:].rearrange("p (h v) -> p h v", v=128), in1=stfs[c][:], op=ALU.add), r=[("ps", bK), ("stf", c)], w=["tmp"])
                S.dve(lambda e, dlast=dlast: e.tensor_tensor(out=state[:], in0=tmp[:], in1=dlast, op=ALU.mult), r=["tmp", dk], w=["state"])

        def Ch_o(s):
            qTs, qTk = R_qTs.at(s)
            hi, hik = R_hi.at(s)
            AT, ATk = R_AT.at(s)
            obuf, obk = R_obuf.at(s)
            for c in range(2):
                pr = prs[c]
                spb, spk = R_spb[c].at(s)
                bO = nbank(k)
                for h in range(4):
                    S.pe(lambda e, h=h, bO=bO, pr=pr, spb=spb: e.matmul(k.ps[bO][0:64, h * 128:(h + 1) * 128], lhsT=qTs[:, h, pr], rhs=spb[:, h, :], start=True, stop=False),
                         r=[qTk, spk], w=[("ps", bO)])
                    S.pe(lambda e, h=h, bO=bO, pr=pr: e.matmul(k.ps[bO][0:64, h * 128:(h + 1) * 128], lhsT=AT[pr, h, :], rhs=hi[pr, h * 128:(h + 1) * 128], start=False, stop=True),
                         r=[(ATk, c), hik], w=[("ps", bO)])
                S.act(lambda e, bO=bO, pr=pr: e.copy(obuf[pr].rearrange("p h v -> p (h v)"), k.ps[bO][0:64, :]), r=[("ps", bO)], w=[(obk, c)])

        def Post(s):
            obuf, obk = R_obuf.at(s)
            t2, t2k = R_t2.at(s)
            S.dve(lambda e: e.memset(ss[:], 0.0), w=["ss"])
            for h in range(4):
                S.act(lambda e, h=h: e.activation(junk[:], obuf[:, h, :], AF.Square, accum_out=ss[:, h:h + 1]), r=[(obk, 0), (obk, 1)], w=["ss", "junk"])
            S.act(lambda e: e.activation(ss[:], ss[:], AF.Ln, scale=1.0 / 128, bias=EPS), r=["ss"], w=["ss"])
            S.act(lambda e: e.activation(ss[:], ss[:], AF.Exp, scale=-0.5), r=["ss"], w=["ss"])
            S.dve(lambda e: e.tensor_tensor(out=ob[:], in0=obuf[:], in1=ss[:, 0:4].unsqueeze(2).to_broadcast([128, 4, 128]), op=ALU.mult), r=[(obk, 0), (obk, 1), "ss"], w=["ob"])
            S.dve(lambda e: e.tensor_tensor(out=obb[:], in0=ob[:].rearrange("p h v -> p (h v)"), in1=t2[:], op=ALU.mult), r=["ob", t2k], w=["obb"])
            so = s - first_own
            sti = (so // 4) % 2
            b = nbank(k)
            pb = k.ps[b][:].bitcast(BF16)
            for h in range(4):
                S.pe(lambda e, h=h, pb=pb: e.transpose(pb[:, h * 128:(h + 1) * 128], obb[:, h * 128:(h + 1) * 128], k.identb[:]),
                     r=["obb", "identb"], w=[("ps", b)])
            sub = so % 4
            S.act(lambda e, pb=pb: e.copy(obst[sti][:, :, sub * 128:(sub + 1) * 128], pb[:, 0:512].rearrange("p (h t) -> p h t", t=128)),
                  r=[("ps", b)], w=[("obst", sti)])
            if sub == 3:
                tb = (so // 4) * 512
                S.dma(k.OBT[:, :, tb:tb + 512].rearrange("j p t -> p j t"), obst[sti][:], r=[("obst", sti)], w=["OBT"], q="pool")

        ok = lambda s: 0 <= s < nsteps
        own_ = lambda s: first_own <= s < nsteps
        for i in range(-3, nsteps + 2):
            if ok(i):
                Ch_pe1(i)
                Ch_rec(i)
            if own_(i - 1):
                Ch_o(i - 1)
            if own_(i - 2):
                Post(i - 2)
            if ok(i + 1):
                E_c(i + 1)
            if ok(i + 2):
                E_b(i + 2)
            if ok(i + 3):
                E_a(i + 3)
        k.bank_excl = set()


def phase_d(k):
    nc = k.nc
    S = k.S
    with ExitStack() as st:
        sb = lambda n, s, d: st.enter_context(nc.sbuf_tensor(n, list(s), d))
        k.NW = 4
        k.wctr = 0
        k.wring = [sb("dwring%d" % i, [128, 8, 512], BF16) for i in range(k.NW)]
        xtok = sb("dxtok", [128, 4, D], F32)
        xTs = [sb("dxT%d" % i, [128, 8, T], F32) for i in range(2)]
        xnA = sb("dxnA", [128, 8, T], BF16)
        xnB = sb("dxnB", [128, 8, T], BF16)
        k.sq = sb("dsq", [128, 8, T], BF16)
        k.aT = sb("daT", [128, 22, T], BF16)
        yT = sb("dyT", [128, 8, T], F32)
        k.silu = [sb("dsilu%d" % i, [128, T], F32) for i in range(2)]
        sgt = [sb("dsgt%d" % i, [128, T], F32) for i in range(2)]
        rs = sb("drs", [128, T], F32)
        oaT = sb("doaT", [128, 4, T], BF16)
        obT = sb("dobT", [128, 4, T], BF16)
        sga = sb("dsga", [128, 8, T], BF16)
        sgb = sb("dsgb", [128, 8, T], BF16)
        ptoks = [sb("dptok%d" % i, [128, 4, 256], F32) for i in range(2)]
        pT = sb("dpT", [128, 2, T], BF16)
        pw = sb("dpw", [128, 2, 512], BF16)
        pg = sb("dpg", [128, 8, 512], BF16)
        NT = NOWN // T

        def chain_thunks(xT, xk, gtile, gcol, nxt):
            th = stats_thunks(k, yT, "yT", rs, "rs")

            def upd(kc):
                S.dve(lambda e: e.tensor_tensor(out=yT[:, kc, :], in0=yT[:, kc, :], in1=rs[:], op=ALU.mult), r=[("yT", kc), "rs"], w=[("yT", kc)])
                S.dve(lambda e: e.scalar_tensor_tensor(out=xT[:, kc, :], in0=yT[:, kc, :], scalar=gtile[:, gcol * 8 + kc:gcol * 8 + kc + 1], in1=xT[:, kc, :],
                                                       op0=ALU.mult, op1=ALU.add), r=[("yT", kc), (xk, kc), "gT", "gH"], w=[(xk, kc)])
            for kc in range(8):
                th.append(lambda kc=kc: upd(kc))
            if nxt is not None and nxt[0] == "norm":
                _, gcol2, dst, dstkey = nxt
                th += stats_thunks(k, xT, xk, rs, "rs")
                for kc in range(8):
                    th.append(lambda kc=kc: S.dve(lambda e: e.scalar_tensor_tensor(out=dst[:, kc, :], in0=xT[:, kc, :], scalar=k.gT[:, gcol2 * 8 + kc:gcol2 * 8 + kc + 1], in1=rs[:],
                                                                                    op0=ALU.mult, op1=ALU.mult), r=[(xk, kc), "rs", "gT"], w=[(dstkey, kc)]))
            elif nxt is not None and nxt[0] == "cast":
                _, dst, dstkey = nxt
                for kc in range(8):
                    th.append(lambda kc=kc: S.act(lambda e: e.copy(dst[:, kc, :], xT[:, kc, :]), r=[(xk, kc)], w=[(dstkey, kc)]))
            return th

        def P1_loads(ti):
            xT = xTs[ti % 2]
            xk = "dxT%d" % (ti % 2)
            to = ti * T
            S.dma(xT[:], k.H1T[:, :, to:to + T], w=[(xk, kc) for kc in range(8)])
            S.dma(oaT[:], k.OAT[:, :, to:to + T].rearrange("j p t -> p j t"), w=["oaT"])
            S.dma(obT[:], k.OBT[:, :, to:to + T].rearrange("j p t -> p j t"), w=["obT"])
            S.dma(sga[:], k.SGA[:, :, to:to + T].rearrange("j p t -> p j t"), w=["sga"])
            S.dma(sgb[:], k.SGB[:, :, to:to + T].rearrange("j p t -> p j t"), w=["sgb"])
            S.dma(ptoks[ti % 2][:], k.pc[to:to + T, :].rearrange("(s p) d -> p s d", p=128), w=[("ptok", ti % 2)])

        def P1(ti):
            for half in range(2):
                wa, wak = load_w(k, "w_branch_a", 0, 4, half * 512, 512)
                wb_, wbk = load_w(k, "w_branch_b", 0, 4, half * 512, 512)
                for jj in range(4):
                    j = half * 4 + jj
                    ba = nbank(k)
                    bb = nbank(k)
                    for kc in range(4):
                        S.pe(lambda e, kc=kc, jj=jj, ba=ba, wa=wa: e.matmul(k.ps[ba][:], lhsT=wa[:, kc, jj * 128:(jj + 1) * 128], rhs=oaT[:, kc, :], start=(kc == 0), stop=(kc == 3)),
                             r=[wak, "oaT"], w=[("ps", ba)])
                    for kc in range(4):
                        S.pe(lambda e, kc=kc, jj=jj, bb=bb, wb_=wb_: e.matmul(k.ps[bb][:], lhsT=wb_[:, kc, jj * 128:(jj + 1) * 128], rhs=obT[:, kc, :], start=(kc == 0), stop=(kc == 3)),
                             r=[wbk, "obT"], w=[("ps", bb)])
                    S.dve(lambda e, j=j, ba=ba: e.tensor_tensor(out=sgt[0][:], in0=k.ps[ba][:], in1=sga[:, j, :], op=ALU.mult), r=[("ps", ba), "sga"], w=[("sgt", 0)])
                    S.dve(lambda e, j=j, bb=bb: e.tensor_tensor(out=sgt[1][:], in0=k.ps[bb][:], in1=sgb[:, j, :], op=ALU.mult), r=[("ps", bb), "sgb"], w=[("sgt", 1)])
                    S.pool(lambda e, j=j: e.tensor_tensor(out=xnB[:, j, :], in0=sgt[0][:], in1=sgt[1][:], op=ALU.add), r=[("sgt", 0), ("sgt", 1)], w=[("xnB", j)])
            xnk = [("xnB", kc) for kc in range(8)]
            for half in range(2):
                wo, wok = load_w(k, "w_out", 0, 8, half * 512, 512)
                for jj in range(4):
                    j = half * 4 + jj
                    b = nbank(k)
                    for kc in range(8):
                        S.pe(lambda e, kc=kc, jj=jj, b=b, wo=wo: e.matmul(k.ps[b][:], lhsT=wo[:, kc, jj * 128:(jj + 1) * 128], rhs=xnB[:, kc, :], start=(kc == 0), stop=(kc == 7)),
                             r=[wok] + xnk, w=[("ps", b)])
                    S.act(lambda e, j=j, b=b: e.copy(yT[:, j, :], k.ps[b][:]), r=[("ps", b)], w=[("yT", j)])

        def P2(ti):
            for kc in range(2):
                b = nbank(k)
                for s in range(4):
                    S.pe(lambda e, kc=kc, s=s, b=b: e.transpose(k.ps[b][:, s * 128:(s + 1) * 128], ptoks[ti % 2][:, s, kc * 128:(kc + 1) * 128], k.identf[:]),
                         r=[("ptok", ti % 2), "identf"], w=[("ps", b)])
                S.act(lambda e, kc=kc, b=b: e.copy(pT[:, kc, :], k.ps[b][:]), r=[("ps", b)], w=["pT"])
            xnk = [("xnB", kc) for kc in range(8)]
            for half in range(2):
                S.dma(pw[:], k.wb["w_ple"][0:256, half * 512:(half + 1) * 512].rearrange("(kc p) n -> p kc n", p=128), r=wkeys(k, "w_ple", 0, 2), w=["pw"])
                S.dma(pg[:], k.wb["w_ple_gate"][:, half * 512:(half + 1) * 512].rearrange("(kc p) n -> p kc n", p=128), r=wkeys(k, "w_ple_gate", 0, 8), w=["pg"])
                wp, wpk, wg, wgk = pw, "pw", pg, "pg"
                for jj in range(4):
                    j = half * 4 + jj
                    be = nbank(k)
                    bg = nbank(k)
                    for kc in range(2):
                        S.pe(lambda e, kc=kc, jj=jj, be=be, wp=wp: e.matmul(k.ps[be][:], lhsT=wp[:, kc, jj * 128:(jj + 1) * 128], rhs=pT[:, kc, :], start=(kc == 0), stop=(kc == 1)),
                             r=[wpk, "pT"], w=[("ps", be)])
                    for kc in range(8):
                        S.pe(lambda e, kc=kc, jj=jj, bg=bg, wg=wg: e.matmul(k.ps[bg][:], lhsT=wg[:, kc, jj * 128:(jj + 1) * 128], rhs=xnB[:, kc, :], start=(kc == 0), stop=(kc == 7)),
                             r=[wgk] + xnk, w=[("ps", bg)])
                    sl = j % 2
                    S.act(lambda e, bg=bg, sl=sl: e.activation(sgt[sl][:], k.ps[bg][:], AF.Sigmoid), r=[("ps", bg)], w=[("sgt", sl)])
                    S.dve(lambda e, be=be, sl=sl, j=j: e.tensor_tensor(out=yT[:, j, :], in0=sgt[sl][:], in1=k.ps[be][:], op=ALU.mult),
                          r=[("ps", be), ("sgt", sl)], w=[("yT", j)])

        def O_thunks(ti):
            xT = xTs[ti % 2]
            xk = "dxT%d" % (ti % 2)
            to = ti * T
            th = []

            def tr(s, g2):
                b = nbank(k)
                for kk in range(4):
                    kc = g2 * 4 + kk
                    S.pe(lambda e, kc=kc, kk=kk: e.transpose(k.ps[b][:, kk * 128:(kk + 1) * 128], xT[:, kc, s * 128:(s + 1) * 128], k.identf[:]),
                         r=[(xk, kc), "identf"], w=[("ps", b)])
                S.act(lambda e: e.copy(xtok[:, s, g2 * 512:(g2 + 1) * 512], k.ps[b][:]), r=[("ps", b)], w=["xtok"])
            for s in range(4):
                for g2 in range(2):
                    th.append(lambda s=s, g2=g2: tr(s, g2))
            th.append(lambda: S.dma(k.out[to:to + T, :].rearrange("(s p) d -> p s d", p=128), xtok[:], r=["xtok"], w=[("out", ti)], q="pool"))
            return th

        def tail_thunks(ti):
            xT = xTs[ti % 2]
            xk = "dxT%d" % (ti % 2)
            th = chain_thunks(xT, xk, k.gH, 5, ("cast", xnB, "xnB"))
            th.append(lambda: P2(ti))
            th += chain_thunks(xT, xk, k.gT, 6, None)
            th += O_thunks(ti)
            if ti + 2 < NT:
                th.append(lambda: P1_loads(ti + 2))
            return th

        def c1_thunks(ti):
            xT = xTs[ti % 2]
            xk = "dxT%d" % (ti % 2)
            return chain_thunks(xT, xk, k.gT, 3, ("norm", 4, xnA, "xnA"))

        P1_loads(0)
        P1(0)
        for f in c1_thunks(0):
            f()
        ffn_gu(k, "w_ffn2_gu", xnA, "xnA")
        for ti in range(NT):
            if ti + 1 < NT:
                if ti == 0:
                    P1_loads(1)
                P1(ti + 1)
                ffn_down(k, "w_ffn2_down", yT, "yT", hooks=c1_thunks(ti + 1))
                ffn_gu(k, "w_ffn2_gu", xnA, "xnA", hooks=tail_thunks(ti))
            else:
                ffn_down(k, "w_ffn2_down", yT, "yT")
                for f in tail_thunks(ti):
                    f()


def _rel_bucket_np(n):
    n = np.maximum(n, 0)
    nf = np.maximum(n, 1).astype(np.float32)
    large = 16 + (np.log(nf / np.float32(16)) / np.float32(np.log(128 / 16)) * np.float32(16)).astype(np.int32)
    large = np.minimum(large, 31)
    return np.where(n < 16, n, large)


def _consts():
    c = {"ident": np.eye(128, dtype=np.float32)}
    c["cJ"] = np.ascontiguousarray(np.eye(128, dtype=np.float32)[::-1])
    j = np.arange(768)
    d = j - 255
    ohp = np.zeros((32, 768), np.float32)
    pos = d >= 0
    bk = _rel_bucket_np(np.maximum(d, 0))
    ohp[bk[pos], j[pos]] += 1.0
    ohp[31, j[pos]] -= 1.0
    c["cOhp"] = ohp
    cm = np.zeros((8, 768), np.float32)
    cm[:, ~pos] = NEG
    c["cCm"] = cm
    ka = np.zeros((32, NTOK), np.float32)
    ka[np.arange(NTOK) // 256, np.arange(NTOK)] = 1.0
    c["cKaug"] = ka
    oo = np.zeros((128, 16, 32), np.float32)
    for i in range(16):
        oo[:, i, 16 + i] = 1.0
    c["cOwnoh"] = oo
    s_ = np.arange(128)
    t_ = np.arange(128)
    same = (s_[:, None] // 64) == (t_[None, :] // 64)
    m2 = (same & (s_[:, None] <= t_[None, :])).astype(np.float32) - (same & ((s_[:, None] % 64) <= 31)).astype(np.float32)
    c["cM2"] = m2.astype(np.float32)
    sel = np.zeros((128, 4), np.float32)
    sel[(s_ < 64) & (s_ % 64 <= 31), 0] = 1
    sel[(s_ < 64) & (s_ % 64 > 31), 1] = 1
    sel[(s_ >= 64) & (s_ % 64 <= 31), 2] = 1
    sel[(s_ >= 64) & (s_ % 64 > 31), 3] = 1
    c["cSel"] = sel
    c["cTri"] = ((s_[:, None] % 64) <= np.arange(64)[None, :]).astype(np.float32)
    return c


def _gmask(half):
    g = np.zeros((128, 16, 32), np.float32)
    for i in range(16):
        g[:, i, 16 + i:] = -1e30
    if half == 0:
        g[:, :, 0:16] = -1e30
    return g


def make_in_maps(inputs):
    x = np.asarray(inputs["x"], np.float32)
    p = np.asarray(inputs["p"], np.float32)
    common = {n: np.ascontiguousarray(np.asarray(inputs[n], np.float32)[0]) for n, a, b in WSPEC}
    g = np.asarray(inputs["norm_gains"], np.float32)[0]
    common["gainsT"] = np.ascontiguousarray(g.reshape(7, 8, 128).transpose(2, 0, 1).reshape(128, 56))
    common["hg_norm_w"] = np.ascontiguousarray(np.asarray(inputs["hg_norm_w"], np.float32).reshape(1, 128))
    common["lb_param"] = np.ascontiguousarray(np.asarray(inputs["lb_param"], np.float32))
    common["rel_table"] = np.ascontiguousarray(np.asarray(inputs["rel_table"], np.float32))
    common.update(_consts())
    maps = []
    for c in range(8):
        b, half = c // 2, c % 2
        m = dict(common)
        xc = np.zeros((NTOK, D), np.float32)
        if half == 1:
            xc[:NCTX] = x[b, :NCTX]
        xc[NCTX:] = x[b, half * NOWN:(half + 1) * NOWN]
        m["xc"] = xc
        m["gmask"] = _gmask(half)
        m["pc"] = np.ascontiguousarray(p[0, b, half * NOWN:(half + 1) * NOWN])
        maps.append(m)
    return maps


def kernel(**inputs):
    nc = build_nc(debug=("OAT", "OBT", "TVH"))
    maps = make_in_maps(inputs)
    res = run_bass_kernel_spmd(nc, maps, core_ids=list(range(8)))
    out = np.zeros((4, 8192, D), np.float32)
    for c in range(8):
        b, half = c // 2, c % 2
        out[b, half * NOWN:(half + 1) * NOWN] = res.results[c]["out"]
    return out
```

```python
import numpy as np
import concourse.bass as bass
import concourse.mybir as mybir
from concourse.bass_utils import run_bass_kernel_spmd
from contextlib import ExitStack

F32 = mybir.dt.float32
BF16 = mybir.dt.bfloat16
ALU = mybir.AluOpType
AF = mybir.ActivationFunctionType
AX = mybir.AxisListType

COMPUTE = ("pe", "act", "dve", "pool")
ENGS = ("pe", "act", "dve", "pool", "sp")
NDMASEM = 6

D = 1024
DFF = 2816
T = 512
NCTX = 4096
NOWN = 4096
NTOK = NCTX + NOWN
EPS = 1e-6
NEG = -30000.0


class _Op:
    __slots__ = ("eng", "fn", "deps", "dma", "sig", "ticket", "semi", "semval")

    def __init__(self, eng, fn, dma):
        self.eng = eng
        self.fn = fn
        self.dma = dma
        self.deps = []
        self.sig = False
        self.ticket = None
        self.semi = None
        self.semval = None


class Sched:
    def __init__(self, nc, stack):
        self.nc = nc
        self.pending = {e: [] for e in ENGS}
        self.lastw = {}
        self.readers = {}
        self.tick = {e: 0 for e in COMPUTE}
        self.sem = {e: stack.enter_context(nc.semaphore("s_" + e)) for e in COMPUTE}
        self.dsem = {}
        self.dcnt = {}
        self.dlast = {}
        for q in ("sp", "act", "pool"):
            self.dsem[q] = [stack.enter_context(nc.semaphore("d_%s%d" % (q, i))) for i in range(NDMASEM)]
            self.dcnt[q] = 0
            self.dlast[q] = [None] * NDMASEM
        self.waited = {}
        self.lastreal = {e: None for e in COMPUTE}

    def add(self, eng, fn, reads=(), writes=(), dma=False):
        op = _Op(eng, fn, dma)
        deps = []
        for k in reads:
            w = self.lastw.get(k)
            if w is not None:
                deps.append((w, "raw"))
        for k in writes:
            w = self.lastw.get(k)
            if w is not None:
                deps.append((w, "waw"))
            for r in self.readers.get(k, ()):
                deps.append((r, "war"))
        seen = set()
        for p, kind in deps:
            if p is op or id(p) in seen:
                continue
            if not p.dma and p.eng == eng and not dma:
                if eng == "pe" or kind != "raw":
                    continue
            seen.add(id(p))
            op.deps.append(p)
            if not p.dma:
                p.sig = True
        if dma:
            q = eng
            i = self.dcnt[q] % NDMASEM
            self.dcnt[q] += 1
            prev = self.dlast[q][i]
            if prev is not None and id(prev) not in seen:
                op.deps.append(prev)
            op.semi = (q, i)
            op.semval = 16 * ((self.dcnt[q] - 1) // NDMASEM + 1)
            self.dlast[q][i] = op
        else:
            self.lastreal[eng] = op
        for k in reads:
            self.readers.setdefault(k, []).append(op)
        for k in writes:
            self.lastw[k] = op
            self.readers[k] = []
        self.pending[eng].append(op)
        return op

    def pe(self, fn, r=(), w=()):
        return self.add("pe", fn, r, w)

    def act(self, fn, r=(), w=()):
        return self.add("act", fn, r, w)

    def dve(self, fn, r=(), w=()):
        return self.add("dve", fn, r, w)

    def pool(self, fn, r=(), w=()):
        return self.add("pool", fn, r, w)

    def dma(self, out, in_, r=(), w=(), q="sp", **kw):
        return self.add(q, lambda e: e.dma_start(out=out, in_=in_, **kw), r, w, dma=True)

    def barrier(self):
        deps = []
        for e in COMPUTE:
            p = self.lastreal[e]
            if p is not None:
                p.sig = True
                deps.append(p)
        for q in self.dlast:
            for p in self.dlast[q]:
                if p is not None:
                    deps.append(p)
        for e in ENGS:
            w = _Op(e, None, False)
            w.deps = [p for p in deps if (p.dma or p.eng != e)]
            self.pending[e].append(w)
        self.lastw = {}
        self.readers = {}

    def flush(self):
        nc = self.nc
        for e in COMPUTE:
            for op in self.pending[e]:
                if op.sig and op.ticket is None:
                    self.tick[e] += 1
                    op.ticket = self.tick[e]
        pend = self.pending
        self.pending = {e: [] for e in ENGS}
        sched = self

        def emit(engname, engine):
            for op in pend[engname]:
                for p in op.deps:
                    if p.dma:
                        s = sched.dsem[p.semi[0]][p.semi[1]]
                        key = (engname, "d", p.semi)
                        val = p.semval
                    else:
                        assert p.ticket is not None
                        s = sched.sem[p.eng]
                        key = (engname, "c", p.eng)
                        val = p.ticket
                    if sched.waited.get(key, 0) >= val:
                        continue
                    sched.waited[key] = val
                    engine.wait_ge(s, val)
                if op.fn is None:
                    continue
                ins = op.fn(engine)
                if op.dma:
                    ins.then_inc(sched.dsem[op.semi[0]][op.semi[1]], 16)
                elif op.sig:
                    ins.then_inc(sched.sem[op.eng], 1)

        with nc.Block() as block:
            @block.tensor
            def _(e):
                emit("pe", e)

            @block.scalar
            def _(e):
                emit("act", e)

            @block.vector
            def _(e):
                emit("dve", e)

            @block.gpsimd
            def _(e):
                emit("pool", e)

            @block.sync
            def _(e):
                emit("sp", e)


WSPEC = [
    ("w_ffn1_gu", D, 2 * DFF), ("w_ffn1_down", DFF, D), ("w_in", D, 5632),
    ("w_branch_a", 512, D), ("w_branch_b", 512, D), ("w_out", D, D),
    ("w_ffn2_gu", D, 2 * DFF), ("w_ffn2_down", DFF, D), ("w_ple", 256, D), ("w_ple_gate", D, D),
]


class K:
    pass


def build_nc(debug=(), phases="ABCD"):
    nc = bass.Bass("TRN2", target_bir_lowering=False)
    k = K()
    k.nc = nc
    din = lambda n, s, d=F32: nc.dram_tensor(n, list(s), d, kind="ExternalInput").ap()

    def dscr(n, s, d):
        kind = "ExternalOutput" if n in debug else "Internal"
        return nc.dram_tensor(n, list(s), d, kind=kind).ap()

    k.xc = din("xc", [NTOK, D])
    k.pc = din("pc", [NOWN, 256])
    k.w = {n: din(n, [a, b]) for n, a, b in WSPEC}
    k.gainsT = din("gainsT", [128, 56])
    k.hgw = din("hg_norm_w", [1, 128])
    k.lbp = din("lb_param", [2, 512])
    k.rel = din("rel_table", [32, 8])
    k.ident = din("ident", [128, 128])
    k.out = nc.dram_tensor("out", [NOWN, D], F32, kind="ExternalOutput").ap()

    k.wb = {n: dscr("wb_" + n, [a, b], BF16) for n, a, b in WSPEC}
    k.H1T = dscr("H1T", [128, 8, NOWN], F32)
    k.QT = dscr("QT", [4, 128, NOWN], BF16)
    k.KT = dscr("KT", [4, 128, NTOK], BF16)
    k.V = dscr("V", [NTOK, 520], BF16)
    k.HF = dscr("HF", [NTOK, 512], F32)
    k.HI = dscr("HI", [NTOK, 512], BF16)
    k.HQ = dscr("HQ", [NOWN, 512], F32)
    k.HG = dscr("HG", [NOWN, 512], F32)
    k.SGA = dscr("SGA", [8, 128, NOWN], BF16)
    k.SGB = dscr("SGB", [8, 128, NOWN], BF16)
    k.OAT = dscr("OAT", [4, 128, NOWN], BF16)
    k.OBT = dscr("OBT", [4, 128, NOWN], BF16)
    k.TVH = dscr("TVH", [8, 768], BF16)
    k.TVL = dscr("TVL", [8, 768], BF16)
    k.cJ = din("cJ", [128, 128])
    k.cOhp = din("cOhp", [32, 768])
    k.cCm = din("cCm", [8, 768])
    k.cKaug = din("cKaug", [32, NTOK])
    k.gmask = din("gmask", [128, 16, 32])
    k.cOwnoh = din("cOwnoh", [128, 16, 32])
    k.cM2 = din("cM2", [128, 128])
    k.cSel = din("cSel", [128, 4])
    k.cTri = din("cTri", [128, 64])

    with ExitStack() as st0:
        S = Sched(nc, st0)
        k.S = S
        k.ps = [st0.enter_context(nc.psum_tensor("ps%d" % i, [128, 512], F32)) for i in range(8)]
        k.bankctr = 0
        k.bank_excl = set()
        sb0 = lambda n, s, d: st0.enter_context(nc.sbuf_tensor(n, list(s), d))
        k.identf = sb0("identf", [128, 128], F32)
        k.identb = sb0("identb", [128, 128], BF16)
        k.onesb = sb0("onesb", [128, 128], BF16)
        k.gT = sb0("gT", [128, 56], F32)
        k.gH = sb0("gH", [128, 56], F32)
        S.dma(k.identf[:], k.ident, w=["identf"])
        S.dma(k.gT[:], k.gainsT, w=["gT"])
        S.dve(lambda e: e.tensor_copy(k.identb[:], k.identf[:]), r=["identf"], w=["identb"])
        S.pool(lambda e: e.memset(k.onesb[:], 1.0), w=["onesb"])
        S.dve(lambda e: e.tensor_scalar_mul(k.gH[:], k.gT[:], 0.5), r=["gT"], w=["gH"])
        for n, a, b in WSPEC:
            if n == "w_ffn1_gu":
                for (n0, nc_) in [(c * 512, 512) for c in range(5)] + [(2560, 256)]:
                    for off in (0, DFF):
                        S.dma(k.wb[n][:, off + n0:off + n0 + nc_], k.w[n][:, off + n0:off + n0 + nc_], w=[("wbc", n, off + n0)], q="pool")
                continue
            step = 256
            for r0 in range(0, a, step):
                r1 = min(a, r0 + step)
                S.dma(k.wb[n][r0:r1, :], k.w[n][r0:r1, :], w=[("wb", n, r0)], q="pool")
        k.wrows = {n: a for n, a, b in WSPEC}
        if "A" in phases:
            phase_a(k)
            S.barrier()
            S.flush()
        if "B" in phases:
            phase_b(k)
            S.barrier()
            S.flush()
        if "C" in phases:
            phase_c(k)
            S.barrier()
            S.flush()
        if "D" in phases:
            phase_d(k)
        S.barrier()
        S.flush()
    return nc


def wkeys(k, n, kc0, nkc, n0=None):
    if n == "w_ffn1_gu":
        return [("wbc", n, n0)]
    r0 = kc0 * 128
    r1 = (kc0 + nkc) * 128
    return [("wb", n, r) for r in range(0, k.wrows[n], 256) if r < r1 and r + 256 > r0]


def load_w(k, n, kc0, nkc, n0, ncols):
    S = k.S
    slot = k.wctr % k.NW
    k.wctr += 1
    wt = k.wring[slot]
    src = k.wb[n][kc0 * 128:(kc0 + nkc) * 128, n0:n0 + ncols].rearrange("(kc p) n -> p kc n", p=128)
    S.dma(wt[:, 0:nkc, 0:ncols], src, r=wkeys(k, n, kc0, nkc, n0), w=[("w", slot)])
    return wt, ("w", slot)


def rms_stats(k, src, srckey, rs, rskey):
    S = k.S
    for kc in range(8):
        S.act(lambda e, kc=kc: e.activation(k.sq[:, kc, :], src[:, kc, :], AF.Square), r=[srckey], w=[("sq", kc)])
    b = nbank(k)
    for kc in range(8):
        S.pe(lambda e, kc=kc, b=b: e.matmul(k.ps[b][:], lhsT=k.onesb[:], rhs=k.sq[:, kc, :], start=(kc == 0), stop=(kc == 7)),
             r=[("sq", kc), "onesb"], w=[("ps", b)])
    S.act(lambda e, b=b: e.activation(rs[:], k.ps[b][:], AF.Sqrt, scale=1.0 / D, bias=EPS), r=[("ps", b)], w=[rskey])
    S.dve(lambda e: e.reciprocal(rs[:], rs[:]), r=[rskey], w=[rskey])


def norm_apply(k, src, srckey, rs, rskey, gcol, dst, dstkey):
    S = k.S
    for kc in range(8):
        S.dve(lambda e, kc=kc: e.scalar_tensor_tensor(out=dst[:, kc, :], in0=src[:, kc, :], scalar=k.gT[:, gcol * 8 + kc:gcol * 8 + kc + 1],
                                                       in1=rs[:], op0=ALU.mult, op1=ALU.mult),
              r=[srckey, rskey, "gT"], w=[(dstkey, kc)])


def nbank(k):
    while True:
        b = k.bankctr % 8
        k.bankctr += 1
        if b not in k.bank_excl:
            return b


def run_hooks(hooks, n):
    for _ in range(n):
        if hooks:
            hooks.pop(0)()


def stats_thunks(k, src, srckey, rs, rskey):
    S = k.S
    th = []
    for kc in range(8):
        th.append(lambda kc=kc: S.act(lambda e: e.activation(k.sq[:, kc, :], src[:, kc, :], AF.Square), r=[(srckey, kc)], w=[("sq", kc)]))

    def mm():
        b = nbank(k)
        for kc in range(8):
            S.pe(lambda e, kc=kc: e.matmul(k.ps[b][:], lhsT=k.onesb[:], rhs=k.sq[:, kc, :], start=(kc == 0), stop=(kc == 7)),
                 r=[("sq", kc), "onesb"], w=[("ps", b)])
        S.act(lambda e: e.activation(rs[:], k.ps[b][:], AF.Sqrt, scale=1.0 / D, bias=EPS), r=[("ps", b)], w=[rskey])
        S.dve(lambda e: e.reciprocal(rs[:], rs[:]), r=[rskey], w=[rskey])
    th.append(mm)
    return th


def ffn_gu(k, wgu, xn, xnkey, hooks=None):
    S = k.S
    hooks = hooks if hooks is not None else []
    per = -(-len(hooks) // 18) if hooks else 0
    chunks = [(c * 512, 512) for c in range(5)] + [(2560, 256)]
    for (n0, nc_) in chunks:
        wg, wgk = load_w(k, wgu, 0, 8, n0, nc_)
        wu, wuk = load_w(k, wgu, 0, 8, DFF + n0, nc_)
        for j in range(nc_ // 128):
            jb = (n0 // 128) + j
            bg = nbank(k)
            bu = nbank(k)
            for kc in range(8):
                S.pe(lambda e, kc=kc, j=j, bg=bg, wg=wg: e.matmul(k.ps[bg][:], lhsT=wg[:, kc, j * 128:(j + 1) * 128], rhs=xn[:, kc, :], start=(kc == 0), stop=(kc == 7)),
                     r=[wgk, (xnkey, kc)], w=[("ps", bg)])
            for kc in range(8):
                S.pe(lambda e, kc=kc, j=j, bu=bu, wu=wu: e.matmul(k.ps[bu][:], lhsT=wu[:, kc, j * 128:(j + 1) * 128], rhs=xn[:, kc, :], start=(kc == 0), stop=(kc == 7)),
                     r=[wuk, (xnkey, kc)], w=[("ps", bu)])
            sl = jb % 2
            S.act(lambda e, bg=bg, sl=sl: e.activation(k.silu[sl][:], k.ps[bg][:], AF.Silu), r=[("ps", bg)], w=[("silu", sl)])
            S.dve(lambda e, bu=bu, sl=sl, jb=jb: e.tensor_tensor(out=k.aT[:, jb, :], in0=k.silu[sl][:], in1=k.ps[bu][:], op=ALU.mult),
                  r=[("ps", bu), ("silu", sl)], w=[("aT", jb)])
            run_hooks(hooks, per)
    run_hooks(hooks, len(hooks))


def ffn_down(k, wdown, yT, ykey, hooks=None, coarse=False):
    S = k.S
    hooks = hooks if hooks is not None else []
    per = -(-len(hooks) // 5) if hooks else 0
    for half in range(2):
        banks = [nbank(k) for _ in range(4)]
        k.bank_excl = set(banks)
        groups = [(0, 8), (8, 8), (16, 6)]
        for (kc0, nkc) in groups:
            wd, wdk = load_w(k, wdown, kc0, nkc, half * 512, 512)
            for j in range(4):
                for kk in range(nkc):
                    kc = kc0 + kk
                    S.pe(lambda e, j=j, kk=kk, kc=kc, wd=wd, b=banks[j]: e.matmul(k.ps[b][:], lhsT=wd[:, kk, j * 128:(j + 1) * 128], rhs=k.aT[:, kc, :],
                                                                                 start=(kc == 0), stop=(kc == 21)),
                         r=[wdk, ("aT", kc)], w=[("ps", banks[j])])
            if not (half == 1 and kc0 == 16):
                run_hooks(hooks, per)
        k.bank_excl = set()
        for j in range(4):
            S.act(lambda e, j=j, b=banks[j], half=half: e.copy(yT[:, half * 4 + j, :], k.ps[b][:]), r=[("ps", banks[j])], w=[ykey if coarse else (ykey, half * 4 + j)])
    run_hooks(hooks, len(hooks))


def ffn(k, wgu, wdown, xn, xnkey, yT, ykey):
    ffn_gu(k, wgu, xn, xnkey)
    ffn_down(k, wdown, yT, ykey, coarse=True)


def phase_a(k):
    nc = k.nc
    S = k.S
    with ExitStack() as st:
        sb = lambda n, s, d: st.enter_context(nc.sbuf_tensor(n, list(s), d))
        k.NW = 5
        k.wctr = 0
        k.wring = [sb("wring%d" % i, [128, 8, 512], BF16) for i in range(k.NW)]
        xtok = sb("xtok", [128, 4, D], F32)
        xTs = [sb("xT%d" % i, [128, 8, T], F32) for i in range(2)]
        xnA = sb("xnA", [128, 8, T], BF16)
        xnB = sb("xnB", [128, 8, T], BF16)
        k.sq = sb("sq", [128, 8, T], BF16)
        k.aT = sb("aT", [128, 22, T], BF16)
        yT = sb("yT", [128, 8, T], F32)
        k.silu = [sb("silu%d" % i, [128, T], F32) for i in range(2)]
        rs = sb("rs", [128, T], F32)
        fmst = [sb("fmst%d" % i, [128, 4, T], BF16) for i in range(1)]
        tmst_b = [sb("tmstb%d" % i, [128, 4, 512], BF16) for i in range(2)]
        tmst_f = [sb("tmstf%d" % i, [128, 4, 512], F32) for i in range(1)]
        vst = [sb("vst%d" % i, [128, 4, 520], BF16) for i in range(2)]
        for i in range(2):
            S.pool(lambda e, i=i: e.memset(vst[i][:], 1.0), w=[("vst", i)])
        cn = {"fm": 0, "tmb": 0, "tmf": 0, "v": 0}
        NT = NTOK // T

        def front_thunks(ti):
            xT = xTs[ti % 2]
            xk = "xT%d" % (ti % 2)
            t0 = ti * T
            th = []
            th.append(lambda: S.dma(xtok[:], k.xc[t0:t0 + T, :].rearrange("(s p) d -> p s d", p=128), w=["xtok"]))

            def tr(kc):
                b = nbank(k)
                for s in range(4):
                    S.pe(lambda e, s=s: e.transpose(k.ps[b][:, s * 128:(s + 1) * 128], xtok[:, s, kc * 128:(kc + 1) * 128], k.identf[:]),
                         r=["xtok", "identf"], w=[("ps", b)])
                S.act(lambda e: e.copy(xT[:, kc, :], k.ps[b][:]), r=[("ps", b)], w=[(xk, kc)])
            for kc in range(8):
                th.append(lambda kc=kc: tr(kc))
            th += stats_thunks(k, xT, xk, rs, "rs")
            for kc in range(8):
                th.append(lambda kc=kc: S.dve(lambda e: e.scalar_tensor_tensor(out=xnA[:, kc, :], in0=xT[:, kc, :], scalar=k.gT[:, kc:kc + 1], in1=rs[:],
                                                                                op0=ALU.mult, op1=ALU.mult), r=[(xk, kc), "rs", "gT"], w=[("xnA", kc)]))
            return th

        def mid_thunks(ti):
            xT = xTs[ti % 2]
            xk = "xT%d" % (ti % 2)
            own = ti >= NCTX // T
            to = ti * T - NCTX
            th = stats_thunks(k, yT, "yT", rs, "rs")

            def upd(kc):
                S.dve(lambda e: e.tensor_tensor(out=yT[:, kc, :], in0=yT[:, kc, :], in1=rs[:], op=ALU.mult), r=[("yT", kc), "rs"], w=[("yT", kc)])
                S.dve(lambda e: e.scalar_tensor_tensor(out=xT[:, kc, :], in0=yT[:, kc, :], scalar=k.gH[:, 8 + kc:8 + kc + 1], in1=xT[:, kc, :],
                                                       op0=ALU.mult, op1=ALU.add), r=[("yT", kc), (xk, kc), "gH"], w=[(xk, kc)])
            for kc in range(8):
                th.append(lambda kc=kc: upd(kc))
            if own:
                th.append(lambda: S.dma(k.H1T[:, :, to:to + T], xT[:], r=[(xk, kc) for kc in range(8)], w=[("H1T", ti)], q="pool"))
            th += stats_thunks(k, xT, xk, rs, "rs")
            for kc in range(8):
                th.append(lambda kc=kc: S.dve(lambda e: e.scalar_tensor_tensor(out=xnB[:, kc, :], in0=xT[:, kc, :], scalar=k.gT[:, 16 + kc:16 + kc + 1], in1=rs[:],
                                                                                op0=ALU.mult, op1=ALU.mult), r=[(xk, kc), "rs", "gT"], w=[("xnB", kc)]))
            return th

        def win(ti):
            t0 = ti * T
            own = ti >= NCTX // T
            to = t0 - NCTX
            xn = xnB
            xnk = [("xnB", kc) for kc in range(8)]
            fm_groups = [("k", 512, k.KT, t0)]
            if own:
                fm_groups = [("q", 0, k.QT, to), ("k", 512, k.KT, t0), ("ga", 3584, k.SGA, to), ("ga", 4096, k.SGA, to),
                             ("gb", 4608, k.SGB, to), ("gb", 5120, k.SGB, to)]
            for (kind, n0, dst, tt) in fm_groups:
                wt, wk = load_w(k, "w_in", 0, 8, n0, 512)
                st_ = fmst[cn["fm"] % len(fmst)]
                stk = ("fmst", cn["fm"] % len(fmst))
                cn["fm"] += 1
                for j in range(4):
                    b = nbank(k)
                    for kc in range(8):
                        S.pe(lambda e, kc=kc, j=j, b=b, wt=wt: e.matmul(k.ps[b][:], lhsT=wt[:, kc, j * 128:(j + 1) * 128], rhs=xn[:, kc, :], start=(kc == 0), stop=(kc == 7)),
                             r=[wk] + xnk, w=[("ps", b)])
                    if kind == "q":
                        S.act(lambda e, j=j, b=b, st_=st_: e.mul(st_[:, j, :], k.ps[b][:], 0.125), r=[("ps", b)], w=[stk])
                    elif kind == "k":
                        S.act(lambda e, j=j, b=b, st_=st_: e.copy(st_[:, j, :], k.ps[b][:]), r=[("ps", b)], w=[stk])
                    else:
                        S.act(lambda e, j=j, b=b, st_=st_: e.activation(st_[:, j, :], k.ps[b][:], AF.Sigmoid), r=[("ps", b)], w=[stk])
                blk0 = {0: 0, 512: 0, 3584: 0, 4096: 4, 4608: 0, 5120: 4}[n0]
                S.dma(dst[blk0:blk0 + 4, :, tt:tt + T].rearrange("j p t -> p j t"), st_[:], r=[stk], w=[(kind, n0, ti)], q="pool")
            tm_groups = [("v", 1024, k.V, t0, True), ("hf", 2048, k.HF, t0, False), ("hi", 2560, k.HI, t0, True)]
            if own:
                tm_groups += [("hq", 1536, k.HQ, to, False), ("hg", 3072, k.HG, to, False)]
            for (kind, n0, dst, tt, isb) in tm_groups:
                wt, wk = load_w(k, "w_in", 0, 8, n0, 512)
                if kind == "v":
                    st_ = vst[cn["v"] % 2]
                    stk = ("vst", cn["v"] % 2)
                    cn["v"] += 1
                elif isb:
                    st_ = tmst_b[cn["tmb"] % len(tmst_b)]
                    stk = ("tmstb", cn["tmb"] % len(tmst_b))
                    cn["tmb"] += 1
                else:
                    st_ = tmst_f[cn["tmf"] % len(tmst_f)]
                    stk = ("tmstf", cn["tmf"] % len(tmst_f))
                    cn["tmf"] += 1
                for s in range(4):
                    b = nbank(k)
                    for kc in range(8):
                        S.pe(lambda e, kc=kc, s=s, b=b, wt=wt: e.matmul(k.ps[b][:], lhsT=xn[:, kc, s * 128:(s + 1) * 128], rhs=wt[:, kc, :], start=(kc == 0), stop=(kc == 7)),
                             r=[wk] + xnk, w=[("ps", b)])
                    if kind == "v":
                        S.act(lambda e, s=s, b=b, st_=st_: e.copy(st_[:, s, :].rearrange("p (h c) -> p h c", c=65)[:, :, 0:64],
                                                                  k.ps[b][:].rearrange("p (h c) -> p h c", c=64)), r=[("ps", b)], w=[stk])
                    else:
                        S.act(lambda e, s=s, b=b, st_=st_: e.copy(st_[:, s, :], k.ps[b][:]), r=[("ps", b)], w=[stk])
                S.dma(dst[tt:tt + T, :].rearrange("(s p) c -> p s c", p=128), st_[:], r=[stk], w=[(kind, ti)], q="pool")

        for f in front_thunks(0):
            f()
        ffn_gu(k, "w_ffn1_gu", xnA, "xnA")
        for ti in range(NT):
            fr = front_thunks(ti + 1) if ti + 1 < NT else []
            ffn_down(k, "w_ffn1_down", yT, "yT", hooks=fr)
            md = mid_thunks(ti)
            if ti + 1 < NT:
                ffn_gu(k, "w_ffn1_gu", xnA, "xnA", hooks=md)
            else:
                for f in md:
                    f()
            win(ti)


def phase_b(k):
    nc = k.nc
    S = k.S
    with ExitStack() as st:
        sb = lambda n, s, d: st.enter_context(nc.sbuf_tensor(n, list(s), d))
        mctr = [0]

        def mb():
            mctr[0] += 1
            return 6 + (mctr[0] % 2)

        VA = sb("b_VA", [128, 64, 520], BF16)
        KA = [sb("b_KA%d" % i, [96, NTOK], BF16) for i in range(2)]
        QA = [sb("b_QA%d" % i, [96, NOWN], BF16) for i in range(2)]
        sqb = sb("b_sqb", [64, NTOK], BF16)
        qsq = sb("b_qsq", [64, NOWN], BF16)
        Jb = sb("b_Jb", [128, 128], BF16)
        Jf = sb("b_Jf", [128, 128], F32)
        onesf = sb("b_onesf", [128, 64], F32)
        relt = sb("b_relt", [32, 8], F32)
        ohp = sb("b_ohp", [32, 768], F32)
        cmt = sb("b_cm", [8, 768], F32)
        tv = sb("b_tv", [8, 768], F32)
        tvh = sb("b_tvh", [8, 768], BF16)
        tvhf = sb("b_tvhf", [8, 768], F32)
        tvl = sb("b_tvl", [8, 768], BF16)
        gmask = sb("b_gmask", [128, 16, 32], F32)
        ownoh = sb("b_ownoh", [128, 16, 32], F32)
        Bt = [[sb("b_Bt%d_%d" % (i, j), [128, 256], BF16) for j in range(8)] for i in range(2)]
        kmTs = [sb("b_kmT%d" % i, [64, 32], BF16) for i in range(2)]
        kmf = sb("b_kmf", [64, 32], F32)
        kmx = sb("b_kmx", [128, 16], F32)
        kmax2 = sb("b_kmax2", [128, 1], F32)
        gm = sb("b_gm", [128, 32], F32)
        mx8 = sb("b_mx8", [128, 8], F32)
        vv = sb("b_vv", [128, 32], F32)
        sel = sb("b_sel", [128, 32], F32)
        mmall = sb("b_mmall", [128, 32], F32)
        mmball = sb("b_mmball", [128, 32], BF16)
        a01alls = [sb("b_a01all%d" % i, [128, 32], F32) for i in range(2)]
        mm = sb("b_mm", [128, 1], F32)
        mmb = sb("b_mmb", [128, 1], BF16)
        a01 = sb("b_a01", [128, 1], F32)
        rbb = sb("b_rbb", [128, 32], BF16)
        rbbs = [sb("b_rbbs%d" % i, [128, 32], BF16) for i in range(2)]
        PT = [sb("b_PT%d" % i, [128, 512], BF16) for i in range(4)]
        osb = [sb("b_osb%d" % i, [65, 256], F32) for i in range(2)]
        rden = [sb("b_rden%d" % i, [65, 256], F32) for i in range(2)]
        ost = [sb("b_ost%d" % i, [64, 256], BF16) for i in range(2)]

        Vr = k.V.rearrange("(a p) c -> p a c", p=128)
        for g8 in range(8):
            S.dma(VA[:, g8 * 8:(g8 + 1) * 8, :], Vr[:, g8 * 8:(g8 + 1) * 8, :], w=[("VA", g8)])
        S.dma(Jf[:], k.cJ, w=["Jf"])
        S.dve(lambda e: e.tensor_copy(Jb[:], Jf[:]), r=["Jf"], w=["Jb"])
        S.pool(lambda e: e.memset(onesf[:], 1.0), w=["onesf"])
        S.dma(relt[:], k.rel, w=["relt"])
        S.dma(ohp[:], k.cOhp, w=["ohp"])
        S.dma(cmt[:], k.cCm, w=["cmt"])
        S.dma(gmask[:], k.gmask, w=["gmask"])
        S.dma(ownoh[:], k.cOwnoh, w=["ownoh"])
        for i in range(2):
            S.dma(KA[i][64:96, :], k.cKaug, w=[("KAaug", i)], q="pool")
        for (c0, c1) in ((0, 512), (512, 768)):
            b = mb()
            S.pe(lambda e, b=b, c0=c0, c1=c1: e.matmul(k.ps[b][0:8, 0:c1 - c0], lhsT=relt[:], rhs=ohp[:, c0:c1], start=True, stop=True), r=["relt", "ohp"], w=[("ps", b)])
            S.dve(lambda e, b=b, c0=c0, c1=c1: e.tensor_tensor(out=tv[:, c0:c1], in0=k.ps[b][0:8, 0:c1 - c0], in1=cmt[:, c0:c1], op=ALU.add), r=[("ps", b), "cmt"], w=["tv"])
        S.dve(lambda e: e.tensor_copy(tvh[:], tv[:]), r=["tv"], w=["tvh"])
        S.dve(lambda e: e.tensor_copy(tvhf[:], tvh[:]), r=["tvh"], w=["tvhf"])
        S.dve(lambda e: e.tensor_tensor(out=tvl[:], in0=tv[:], in1=tvhf[:], op=ALU.subtract), r=["tv", "tvhf"], w=["tvl"])
        S.dma(k.TVH, tvh[:], r=["tvh"], w=["TVH"], q="pool")
        S.dma(k.TVL, tvl[:], r=["tvl"], w=["TVL"], q="pool")

        ctr = {"s": 0, "o": 0, "p": 0}

        def hv(h):
            hb = h % 2
            return h // 2, (h % 2) * 64, hb, KA[hb], QA[hb], ("KA", hb), ("QA", hb), Bt[hb]

        def setup(h):
            hp, po, hb, ka, qa, kak, qak, bt = hv(h)
            S.dma(ka[0:64, :], k.KT[hp, po:po + 64, :], w=[kak])
            S.dma(qa[0:64, :], k.QT[hp, po:po + 64, :], w=[qak])
            for a in range(2):
                for (idx, off) in ((0, 128 - 128 * a), (1, 384 - 128 * a)):
                    for (lo, src) in ((0, k.TVH), (1, k.TVL)):
                        ap_ = bass.AP(tensor=src.tensor, offset=src[h:h + 1, off:off + 1].offset, ap=[[1, 128], [1, 256]])
                        S.dma(bt[idx * 4 + a * 2 + lo][:], ap_, r=["TVH", "TVL"], w=[("Bt", hb, idx * 4 + a * 2 + lo)])
            S.dve(lambda e, ka=ka: e.tensor_reduce(out=kmf[:], in_=ka[0:64, :].rearrange("p (n c) -> p n c", c=256), axis=AX.X, op=ALU.add), r=[kak], w=["kmf"])
            S.dve(lambda e: e.tensor_scalar_mul(kmTs[hb][:], kmf[:], 1.0 / 256), r=["kmf"], w=[("kmT", hb)])
            S.pool(lambda e, ka=ka: e.tensor_tensor(out=sqb[:], in0=ka[0:64, :], in1=ka[0:64, :], op=ALU.mult), r=[kak], w=["sqb"])
            S.pool(lambda e, qa=qa: e.tensor_tensor(out=qsq[:], in0=qa[0:64, :], in1=qa[0:64, :], op=ALU.mult), r=[qak], w=["qsq"])
            for c in range(16):
                b = mb()
                S.pe(lambda e, b=b, c=c: e.matmul(k.ps[b][:], lhsT=k.onesb[0:64, :], rhs=sqb[:, c * 512:(c + 1) * 512], start=True, stop=True), r=["sqb", "onesb"], w=[("ps", b)])
                S.dve(lambda e, b=b, c=c: e.tensor_reduce(out=kmx[:, c:c + 1], in_=k.ps[b][:], axis=AX.X, op=ALU.max), r=[("ps", b)], w=["kmx"])
            S.dve(lambda e: e.tensor_reduce(out=kmax2[:], in_=kmx[:], axis=AX.X, op=ALU.max), r=["kmx"], w=["kmax2"])
            bq = mb()
            for qt_ in range(32):
                S.pe(lambda e, bq=bq, qt_=qt_: e.matmul(k.ps[bq][:, qt_:qt_ + 1], lhsT=qsq[:, qt_ * 128:(qt_ + 1) * 128], rhs=k.onesb[0:64, 0:1], start=True, stop=True),
                     r=["qsq", "onesb"], w=[("ps", bq)])
            S.dve(lambda e, bq=bq: e.tensor_scalar_mul(mmall[:], k.ps[bq][:, 0:32], kmax2[:, 0:1]), r=[("ps", bq), "kmax2"], w=["mmall"])
            S.act(lambda e: e.activation(mmall[:], mmall[:], AF.Ln), r=["mmall"], w=["mmall"])
            S.act(lambda e: e.activation(mmball[:], mmall[:], AF.Exp, scale=0.5), r=["mmall"], w=["mmball"])
            S.dve(lambda e: e.tensor_scalar(a01alls[hb][:], mmball[:], -1.0, -NEG, op0=ALU.mult, op1=ALU.add), r=["mmball"], w=[("a01all", hb)])

        def gate_parts(h, i):
            hp, po, hb, ka, qa, kak, qak, bt = hv(h)
            qc = slice(i * 256, (i + 1) * 256)
            bT = 6 + (i % 2)
            pbT = k.ps[bT][:].bitcast(BF16)
            bg = 7 - (i % 2)

            def g_mm(qt_):
                qcols = slice(i * 256 + qt_ * 128, i * 256 + (qt_ + 1) * 128)
                S.pe(lambda e: e.matmul(k.ps[bg][:, 0:32], lhsT=qa[0:64, qcols], rhs=kmTs[hb][:], start=True, stop=True), r=[qak, ("kmT", hb)], w=[("ps", bg)])
                S.dve(lambda e: e.tensor_tensor(out=gm[:], in0=k.ps[bg][:, 0:32], in1=gmask[:, i, :], op=ALU.add), r=[("ps", bg), "gmask"], w=["gm"])
                S.dve(lambda e: e.max(out=mx8[:], in_=gm[:]), r=["gm"], w=["mx8"])
                S.dve(lambda e: e.tensor_single_scalar(vv[:], gm[:], -1e29, op=ALU.is_gt), r=["gm"], w=["vv"])
                S.dve(lambda e: e.scalar_tensor_tensor(out=sel[:], in0=gm[:], scalar=mx8[:, 2:3], in1=vv[:], op0=ALU.is_ge, op1=ALU.mult), r=["gm", "mx8", "vv"], w=["sel"])
                S.dve(lambda e: e.tensor_tensor(out=sel[:], in0=sel[:], in1=ownoh[:, i, :], op=ALU.add), r=["sel", "ownoh"], w=["sel"])
                qti = i * 2 + qt_
                S.dve(lambda e: e.tensor_scalar(rbbs[qt_][:], sel[:], a01alls[hb][:, qti:qti + 1], NEG, op0=ALU.mult, op1=ALU.add), r=["sel", ("a01all", hb)], w=[("rbb", qt_)])

            def g_tr(qt_):
                S.pe(lambda e: e.transpose(pbT[0:32, qt_ * 128:(qt_ + 1) * 128], rbbs[qt_][:], k.identb[:]), r=[("rbb", qt_), "identb"], w=[("ps", bT)])

            def g_copy():
                S.dve(lambda e: e.tensor_copy(qa[64:96, qc], pbT[0:32, 0:256]), r=[("ps", bT)], w=[("QAaug", hb, i)])

            def p0():
                g_mm(0)

            def p1():
                g_tr(0)
                g_mm(1)

            def p2():
                g_tr(1)
                g_copy()
            return [p0, p1, p2]

        def main(h, i, hooks):
            hp, po, hb, ka, qa, kak, qak, bt = hv(h)
            I = 16 + i
            qc = slice(i * 256, (i + 1) * 256)
            bO = 4 + (ctr["o"] % 2)
            ctr["o"] += 1

            def rec_S(n):
                bS = ctr["s"] % 3
                ctr["s"] += 1
                special = n >= I - 1
                for a in range(2):
                    kcols = slice(n * 256 + a * 128, n * 256 + (a + 1) * 128)
                    S.pe(lambda e, bS=bS, a=a, kcols=kcols, special=special: e.matmul(
                        k.ps[bS][:, a * 256:(a + 1) * 256], lhsT=ka[0:96, kcols], rhs=qa[0:96, qc], start=True, stop=(not special)),
                        r=[kak, qak, ("KAaug", hb), ("QAaug", hb, i)], w=[("ps", bS)])
                    if special:
                        idx = 0 if n == I else 1
                        for lo in range(2):
                            S.pe(lambda e, bS=bS, a=a, idx=idx, lo=lo: e.matmul(
                                k.ps[bS][:, a * 256:(a + 1) * 256], lhsT=Jb[:], rhs=bt[idx * 4 + a * 2 + lo][:], start=False, stop=(lo == 1)),
                                r=["Jb", ("Bt", hb, idx * 4 + a * 2 + lo)], w=[("ps", bS)])
                return bS

            def rec_PV(n, bS):
                pi = ctr["p"] % 4
                ctr["p"] += 1
                S.act(lambda e, bS=bS, pi=pi: e.activation(PT[pi][:], k.ps[bS][:], AF.Exp), r=[("ps", bS)], w=[("PT", pi)])
                for a in range(2):
                    ktile = n * 2 + a
                    S.pe(lambda e, a=a, ktile=ktile, pi=pi, first=(n == 0 and a == 0), last=(n == I and a == 1): e.matmul(
                        k.ps[bO][0:65, 0:256], lhsT=VA[:, ktile, h * 65:(h + 1) * 65], rhs=PT[pi][:, a * 256:(a + 1) * 256], start=first, stop=last),
                        r=[("VA", ktile // 8), ("PT", pi)], w=[("ps", bO)])

            DEPTH = 2
            banks = {}
            for n in range(min(DEPTH, I + 1)):
                banks[n] = rec_S(n)
            for n in range(I + 1):
                if n + DEPTH <= I:
                    banks[n + DEPTH] = rec_S(n + DEPTH)
                rec_PV(n, banks.pop(n))
                for fn in hooks.get(n, ()):
                    fn()
            oi = ctr["o"] % 2
            S.dve(lambda e: e.tensor_copy(osb[oi][:], k.ps[bO][0:65, 0:256]), r=[("ps", bO)], w=[("osb", oi)])
            S.dve(lambda e: e.reciprocal(rden[oi][64:65, :], osb[oi][64:65, :]), r=[("osb", oi)], w=[("rden", oi)])

            def fin():
                bB = 3
                S.pe(lambda e: e.matmul(k.ps[bB][0:64, 0:256], lhsT=onesf[64:65, :], rhs=rden[oi][64:65, :], start=True, stop=True), r=["onesf", ("rden", oi)], w=[("ps", bB)])
                S.dve(lambda e: e.tensor_tensor(out=ost[oi][:], in0=osb[oi][0:64, :], in1=k.ps[bB][0:64, 0:256], op=ALU.mult), r=[("osb", oi), ("ps", bB)], w=[("ost", oi)])
                S.dma(k.OAT[hp, po:po + 64, qc], ost[oi][:], r=[("ost", oi)], w=["OAT"], q="pool")
            return fin

        blocks = [(h, i) for h in range(8) for i in range(16)]
        setup(0)
        for p in gate_parts(0, 0):
            p()
        pending_fin = None
        for bi, (h, i) in enumerate(blocks):
            hooks = {}
            if pending_fin is not None:
                hooks.setdefault(3, []).append(pending_fin)
            if i == 10 and h + 1 < 8:
                hooks.setdefault(1, []).append(lambda h2=h + 1: setup(h2))
            if bi + 1 < len(blocks):
                h2, i2 = blocks[bi + 1]
                parts = gate_parts(h2, i2)
                if h2 != h:
                    steps = (14, 19, 24)
                else:
                    steps = (1, 6, 11)
                for st_, p in zip(steps, parts):
                    hooks.setdefault(st_, []).append(p)
            pending_fin = main(h, i, hooks)
        pending_fin()


def phase_c(k):
    nc = k.nc
    S = k.S
    LN_HALF = -0.6931471805599453
    with ExitStack() as st:
        sb = lambda n, s, d: st.enter_context(nc.sbuf_tensor(n, list(s), d))

        class Ring:
            def __init__(self, name, n, shape, dt):
                self.name = name
                self.t = [sb("c_%s%d" % (name, i), shape, dt) for i in range(n)]

            def at(self, s):
                i = s % len(self.t)
                return self.t[i], (self.name, i)

        lbp2 = sb("c_lbp2", [128, 2, 512], F32)
        lb = sb("c_lb", [128, 512], F32)
        c0 = sb("c_c0", [128, 512], F32)
        c1 = sb("c_c1", [128, 512], F32)
        hgwb = sb("c_hgwb", [128, 512], F32)
        M2 = sb("c_M2", [128, 128], F32)
        Sel = sb("c_Sel", [128, 4], F32)
        tri = sb("c_tri", [128, 64], F32)
        state = sb("c_state", [128, 4, 128], F32)
        stfs = [sb("c_stf%d" % i, [128, 4, 128], F32) for i in range(2)]
        tmp = sb("c_tmp", [128, 4, 128], F32)
        R_hf = Ring("hf", 2, [128, 512], F32)
        R_hi = Ring("hi", 6, [128, 512], BF16)
        R_hq = Ring("hq", 4, [128, 512], F32)
        R_hg = Ring("hg", 4, [128, 512], F32)
        R_thq = Ring("thq", 3, [128, 512], F32)
        R_thg = Ring("thg", 3, [128, 512], F32)
        R_f = Ring("f", 2, [128, 512], F32)
        R_logf = Ring("logf", 2, [128, 512], F32)
        R_kin = Ring("kin", 2, [128, 512], F32)
        R_kt = Ring("kt", 2, [128, 512], BF16)
        R_qTs = Ring("qTs", 3, [128, 4, 128], BF16)
        R_kTs = Ring("kTs", 3, [128, 4, 128], BF16)
        R_dec = Ring("dec", 2, [128, 16], F32)
        R_t2 = Ring("t2", 4, [128, 512], F32)
        R_AT = Ring("AT", 2, [128, 4, 64], BF16)
        R_spb = [Ring("spb%d" % c, 2, [128, 4, 128], BF16) for c in range(2)]
        R_obuf = Ring("obuf", 2, [128, 4, 128], F32)
        thf = sb("c_thf", [128, 512], F32)
        Em = sb("c_Em", [128, 512], F32)
        Ep = sb("c_Ep", [128, 512], F32)
        q1 = sb("c_q1", [128, 512], F32)
        qt = sb("c_qt", [128, 512], BF16)
        junk = sb("c_junk", [128, 128], F32)
        ss = sb("c_ss", [128, 4], F32)
        ob = sb("c_ob", [128, 4, 128], F32)
        obb = sb("c_obb", [128, 512], BF16)
        obst = [sb("c_obst%d" % i, [128, 4, 512], BF16) for i in range(2)]

        S.dma(lbp2[:, 0, :], k.lbp[0:1, :].partition_broadcast(128), w=[("lbp2", 0)])
        S.dma(lbp2[:, 1, :], k.lbp[1:2, :].partition_broadcast(128), w=[("lbp2", 1)])
        for h in range(4):
            S.dma(hgwb[:, h * 128:(h + 1) * 128], k.hgw[0:1, :].partition_broadcast(128), w=[("hgwb", h)])
        S.dma(M2[:], k.cM2, w=["M2"])
        S.dma(Sel[:], k.cSel, w=["Sel"])
        S.dma(tri[:], k.cTri, w=["tri"])
        S.dve(lambda e: e.tensor_tensor(out=lb[:], in0=lbp2[:, 1, :], in1=lbp2[:, 0, :], op=ALU.subtract), r=[("lbp2", 0), ("lbp2", 1)], w=["lb"])
        S.act(lambda e: e.activation(lb[:], lb[:], AF.Exp), r=["lb"], w=["lb"])
        S.dve(lambda e: e.tensor_scalar_add(lb[:], lb[:], 1.0), r=["lb"], w=["lb"])
        S.dve(lambda e: e.reciprocal(lb[:], lb[:]), r=["lb"], w=["lb"])
        S.dve(lambda e: e.tensor_scalar(c1[:], lb[:], -0.5, 0.5, op0=ALU.mult, op1=ALU.add), r=["lb"], w=["c1"])
        S.dve(lambda e: e.tensor_tensor(out=c0[:], in0=lb[:], in1=c1[:], op=ALU.add), r=["lb", "c1"], w=["c0"])
        S.dve(lambda e: e.tensor_scalar_mul(hgwb[:], hgwb[:], 0.5), r=[("hgwb", h) for h in range(4)], w=["hgwb"])
        S.pool(lambda e: e.memset(state[:], 0.0), w=["state"])

        nsteps = NTOK // 128
        first_own = NCTX // 128
        prs = [slice(0, 64), slice(64, 128)]
        k.bank_excl = {0, 1, 2}
        ctxs = {}
        chs = {}

        def E_a(s):
            own = s >= first_own
            t0 = s * 128
            to = t0 - NCTX
            hf, hfk = R_hf.at(s)
            hi, hik = R_hi.at(s)
            f_, fk = R_f.at(s)
            S.dma(hf[:], k.HF[t0:t0 + 128, :], w=[hfk])
            S.dma(hi[:], k.HI[t0:t0 + 128, :], w=[hik])
            S.act(lambda e: e.activation(thf[:], hf[:], AF.Tanh, scale=0.5), r=[hfk], w=["thf"])
            if own:
                hq, hqk = R_hq.at(s)
                hg, hgk = R_hg.at(s)
                thq, thqk = R_thq.at(s)
                thg, thgk = R_thg.at(s)
                S.dma(hq[:], k.HQ[to:to + 128, :], w=[hqk])
                S.dma(hg[:], k.HG[to:to + 128, :], w=[hgk])
                S.act(lambda e: e.activation(thq[:], hq[:], AF.Tanh, scale=0.5), r=[hqk], w=[thqk])
                S.act(lambda e: e.activation(thg[:], hg[:], AF.Tanh, scale=0.5), r=[hgk], w=[thgk])
            S.dve(lambda e: e.tensor_tensor(out=f_[:], in0=thf[:], in1=c1[:], op=ALU.mult), r=["thf", "c1"], w=[fk])
            S.dve(lambda e: e.tensor_tensor(out=f_[:], in0=f_[:], in1=c0[:], op=ALU.add), r=[fk, "c0"], w=[fk])

        def E_b(s):
            f_, fk = R_f.at(s)
            logf, lk = R_logf.at(s)
            kin, kk = R_kin.at(s)
            S.act(lambda e: e.activation(logf[:], f_[:], AF.Ln), r=[fk], w=[lk])
            S.act(lambda e: e.activation(kin[:], f_[:], AF.Identity, scale=-1.0, bias=1.0), r=[fk], w=[kk])
            bbp = s % 2
            S.pe(lambda e: e.matmul(k.ps[bbp][:], lhsT=M2[:], rhs=logf[:], start=True, stop=True), r=["M2", lk], w=[("ps", bbp)])
            bdec = 2
            co = (s % 2) * 16
            for h in range(4):
                S.pe(lambda e, h=h: e.matmul(k.ps[bdec][:, co + h * 4:co + (h + 1) * 4], lhsT=logf[:, h * 128:(h + 1) * 128], rhs=Sel[:], start=True, stop=True),
                     r=[lk, "Sel"], w=[("psdec", s % 2)])
            ctxs[s] = (bbp, bdec, co)

        def E_c(s):
            own = s >= first_own
            bbp, bdec, co = ctxs.pop(s)
            kin, kk = R_kin.at(s)
            dec, dk = R_dec.at(s)
            kt, ktk = R_kt.at(s)
            S.act(lambda e: e.activation(Em[:], k.ps[bbp][:], AF.Exp, scale=-1.0), r=[("ps", bbp)], w=["Em"])
            if own:
                S.act(lambda e: e.activation(Ep[:], k.ps[bbp][:], AF.Exp, bias=LN_HALF), r=[("ps", bbp)], w=["Ep"])
            S.act(lambda e: e.activation(dec[:], k.ps[bdec][:, co:co + 16], AF.Exp), r=[("psdec", s % 2)], w=[dk])
            S.dve(lambda e: e.tensor_tensor(out=kt[:], in0=kin[:], in1=Em[:], op=ALU.mult), r=[kk, "Em"], w=[ktk])
            if own:
                hq, hqk = R_hq.at(s)
                hg, hgk = R_hg.at(s)
                thq, thqk = R_thq.at(s)
                thg, thgk = R_thg.at(s)
                t2, t2k = R_t2.at(s)
                qTs, qTk = R_qTs.at(s)
                kTs, kTk = R_kTs.at(s)
                S.dve(lambda e: e.scalar_tensor_tensor(out=q1[:], in0=thq[:], scalar=1.0, in1=hq[:], op0=ALU.add, op1=ALU.mult), r=[thqk, hqk], w=["q1"])
                S.dve(lambda e: e.tensor_tensor(out=qt[:], in0=q1[:], in1=Ep[:], op=ALU.mult), r=["q1", "Ep"], w=["qt"])
                S.dve(lambda e: e.scalar_tensor_tensor(out=t2[:], in0=thg[:], scalar=1.0, in1=hg[:], op0=ALU.add, op1=ALU.mult), r=[thgk, hgk], w=[t2k])
                S.dve(lambda e: e.tensor_tensor(out=t2[:], in0=t2[:], in1=hgwb[:], op=ALU.mult), r=[t2k, "hgwb"], w=[t2k])
                for (src, srck, dst, dstk) in ((qt, "qt", qTs, qTk), (kt, ktk, kTs, kTk)):
                    b = nbank(k)
                    pb = k.ps[b][:].bitcast(BF16)
                    for h in range(4):
                        S.pe(lambda e, h=h, pb=pb, src=src: e.transpose(pb[:, h * 128:(h + 1) * 128], src[:, h * 128:(h + 1) * 128], k.identb[:]),
                             r=[srck, "identb"], w=[("ps", b)])
                    S.act(lambda e, pb=pb, dst=dst: e.copy(dst[:].rearrange("p h t -> p (h t)"), pb[:, 0:512]), r=[("ps", b)], w=[dstk])

        def Ch_pe1(s):
            own = s >= first_own
            kt, ktk = R_kt.at(s)
            hi, hik = R_hi.at(s)
            bKs = []
            for c in range(2):
                pr = prs[c]
                bK = nbank(k)
                bKs.append(bK)
                for h in range(4):
                    S.pe(lambda e, h=h, bK=bK, pr=pr: e.matmul(k.ps[bK][:, h * 128:(h + 1) * 128], lhsT=kt[pr, h * 128:(h + 1) * 128], rhs=hi[pr, h * 128:(h + 1) * 128], start=True, stop=True),
                         r=[ktk, hik], w=[("ps", bK)])
            chs[s] = bKs
            if own:
                qTs, qTk = R_qTs.at(s)
                kTs, kTk = R_kTs.at(s)
                AT, ATk = R_AT.at(s)
                for c in range(2):
                    pr = prs[c]
                    bA = nbank(k)
                    for h in range(4):
                        S.pe(lambda e, h=h, bA=bA, pr=pr: e.matmul(k.ps[bA][0:64, h * 64:(h + 1) * 64], lhsT=kTs[:, h, pr], rhs=qTs[:, h, pr], start=True, stop=True),
                             r=[kTk, qTk], w=[("ps", bA)])
                    S.dve(lambda e, bA=bA, pr=pr: e.tensor_tensor(out=AT[pr], in0=k.ps[bA][0:64, 0:256].rearrange("p (h t) -> p h t", t=64),
                                                                  in1=tri[0:64, :].unsqueeze(1).to_broadcast([64, 4, 64]), op=ALU.mult),
                          r=[("ps", bA), "tri"], w=[(ATk, c)])

        def Ch_rec(s):
            own = s >= first_own
            dec, dk = R_dec.at(s)
            bKs = chs.pop(s)
            for c in range(2):
                dmid = dec[:, 2 * c:16:4].unsqueeze(2).to_broadcast([128, 4, 128])
                dlast = dec[:, 2 * c + 1:16:4].unsqueeze(2).to_broadcast([128, 4, 128])
                bK = bKs[c]
                S.dve(lambda e, dmid=dmid, c=c: e.tensor_tensor(out=stfs[c][:], in0=state[:], in1=dmid, op=ALU.mult), r=["state", dk], w=[("stf", c)])
                if own:
                    spb, spk = R_spb[c].at(s)
                    S.act(lambda e, c=c, spb=spb: e.copy(spb[:], stfs[c][:]), r=[("stf", c)], w=[spk])
                S.dve(lambda e, bK=bK, c=c: e.tensor_tensor(out=tmp[:], in0=k.ps[bK][:].rearrange("p (h v) -> p h v", v=128), in1=stfs[c][:], op=ALU.add), r=[("ps", bK), ("stf", c)], w=["tmp"])
                S.dve(lambda e, dlast=dlast: e.tensor_tensor(out=state[:], in0=tmp[:], in1=dlast, op=ALU.mult), r=["tmp", dk], w=["state"])

        def Ch_o(s):
            qTs, qTk = R_qTs.at(s)
            hi, hik = R_hi.at(s)
            AT, ATk = R_AT.at(s)
            obuf, obk = R_obuf.at(s)
            for c in range(2):
                pr = prs[c]
                spb, spk = R_spb[c].at(s)
                bO = nbank(k)
                for h in range(4):
                    S.pe(lambda e, h=h, bO=bO, pr=pr, spb=spb: e.matmul(k.ps[bO][0:64, h * 128:(h + 1) * 128], lhsT=qTs[:, h, pr], rhs=spb[:, h, :], start=True, stop=False),
                         r=[qTk, spk], w=[("ps", bO)])
                    S.pe(lambda e, h=h, bO=bO, pr=pr: e.matmul(k.ps[bO][0:64, h * 128:(h + 1) * 128], lhsT=AT[pr, h, :], rhs=hi[pr, h * 128:(h + 1) * 128], start=False, stop=True),
                         r=[(ATk, c), hik], w=[("ps", bO)])
                S.act(lambda e, bO=bO, pr=pr: e.copy(obuf[pr].rearrange("p h v -> p (h v)"), k.ps[bO][0:64, :]), r=[("ps", bO)], w=[(obk, c)])

        def Post(s):
            obuf, obk = R_obuf.at(s)
            t2, t2k = R_t2.at(s)
            S.dve(lambda e: e.memset(ss[:], 0.0), w=["ss"])
            for h in range(4):
                S.act(lambda e, h=h: e.activation(junk[:], obuf[:, h, :], AF.Square, accum_out=ss[:, h:h + 1]), r=[(obk, 0), (obk, 1)], w=["ss", "junk"])
            S.act(lambda e: e.activation(ss[:], ss[:], AF.Ln, scale=1.0 / 128, bias=EPS), r=["ss"], w=["ss"])
            S.act(lambda e: e.activation(ss[:], ss[:], AF.Exp, scale=-0.5), r=["ss"], w=["ss"])
            S.dve(lambda e: e.tensor_tensor(out=ob[:], in0=obuf[:], in1=ss[:, 0:4].unsqueeze(2).to_broadcast([128, 4, 128]), op=ALU.mult), r=[(obk, 0), (obk, 1), "ss"], w=["ob"])
            S.dve(lambda e: e.tensor_tensor(out=obb[:], in0=ob[:].rearrange("p h v -> p (h v)"), in1=t2[:], op=ALU.mult), r=["ob", t2k], w=["obb"])
            so = s - first_own
            sti = (so // 4) % 2
            b = nbank(k)
            pb = k.ps[b][:].bitcast(BF16)
            for h in range(4):
                S.pe(lambda e, h=h, pb=pb: e.transpose(pb[:, h * 128:(h + 1) * 128], obb[:, h * 128:(h + 1) * 128], k.identb[:]),
                     r=["obb", "identb"], w=[("ps", b)])
            sub = so % 4
            S.act(lambda e, pb=pb: e.copy(obst[sti][:, :, sub * 128:(sub + 1) * 128], pb[:, 0:512].rearrange("p (h t) -> p h t", t=128)),
                  r=[("ps", b)], w=[("obst", sti)])
            if sub == 3:
                tb = (so // 4) * 512
                S.dma(k.OBT[:, :, tb:tb + 512].rearrange("j p t -> p j t"), obst[sti][:], r=[("obst", sti)], w=["OBT"], q="pool")

        ok = lambda s: 0 <= s < nsteps
        own_ = lambda s: first_own <= s < nsteps
        for i in range(-3, nsteps + 2):
            if ok(i):
                Ch_pe1(i)
                Ch_rec(i)
            if own_(i - 1):
                Ch_o(i - 1)
            if own_(i - 2):
                Post(i - 2)
            if ok(i + 1):
                E_c(i + 1)
            if ok(i + 2):
                E_b(i + 2)
            if ok(i + 3):
                E_a(i + 3)
        k.bank_excl = set()


def phase_d(k):
    nc = k.nc
    S = k.S
    with ExitStack() as st:
        sb = lambda n, s, d: st.enter_context(nc.sbuf_tensor(n, list(s), d))
        k.NW = 4
        k.wctr = 0
        k.wring = [sb("dwring%d" % i, [128, 8, 512], BF16) for i in range(k.NW)]
        xtok = sb("dxtok", [128, 4, D], F32)
        xTs = [sb("dxT%d" % i, [128, 8, T], F32) for i in range(2)]
        xnA = sb("dxnA", [128, 8, T], BF16)
        xnB = sb("dxnB", [128, 8, T], BF16)
        k.sq = sb("dsq", [128, 8, T], BF16)
        k.aT = sb("daT", [128, 22, T], BF16)
        yT = sb("dyT", [128, 8, T], F32)
        k.silu = [sb("dsilu%d" % i, [128, T], F32) for i in range(2)]
        sgt = [sb("dsgt%d" % i, [128, T], F32) for i in range(2)]
        rs = sb("drs", [128, T], F32)
        oaT = sb("doaT", [128, 4, T], BF16)
        obT = sb("dobT", [128, 4, T], BF16)
        sga = sb("dsga", [128, 8, T], BF16)
        sgb = sb("dsgb", [128, 8, T], BF16)
        ptoks = [sb("dptok%d" % i, [128, 4, 256], F32) for i in range(2)]
        pT = sb("dpT", [128, 2, T], BF16)
        pw = sb("dpw", [128, 2, 512], BF16)
        pg = sb("dpg", [128, 8, 512], BF16)
        NT = NOWN // T

        def chain_thunks(xT, xk, gtile, gcol, nxt):
            th = stats_thunks(k, yT, "yT", rs, "rs")

            def upd(kc):
                S.dve(lambda e: e.tensor_tensor(out=yT[:, kc, :], in0=yT[:, kc, :], in1=rs[:], op=ALU.mult), r=[("yT", kc), "rs"], w=[("yT", kc)])
                S.dve(lambda e: e.scalar_tensor_tensor(out=xT[:, kc, :], in0=yT[:, kc, :], scalar=gtile[:, gcol * 8 + kc:gcol * 8 + kc + 1], in1=xT[:, kc, :],
                                                       op0=ALU.mult, op1=ALU.add), r=[("yT", kc), (xk, kc), "gT", "gH"], w=[(xk, kc)])
            for kc in range(8):
                th.append(lambda kc=kc: upd(kc))
            if nxt is not None and nxt[0] == "norm":
                _, gcol2, dst, dstkey = nxt
                th += stats_thunks(k, xT, xk, rs, "rs")
                for kc in range(8):
                    th.append(lambda kc=kc: S.dve(lambda e: e.scalar_tensor_tensor(out=dst[:, kc, :], in0=xT[:, kc, :], scalar=k.gT[:, gcol2 * 8 + kc:gcol2 * 8 + kc + 1], in1=rs[:],
                                                                                    op0=ALU.mult, op1=ALU.mult), r=[(xk, kc), "rs", "gT"], w=[(dstkey, kc)]))
            elif nxt is not None and nxt[0] == "cast":
                _, dst, dstkey = nxt
                for kc in range(8):
                    th.append(lambda kc=kc: S.act(lambda e: e.copy(dst[:, kc, :], xT[:, kc, :]), r=[(xk, kc)], w=[(dstkey, kc)]))
            return th

        def P1_loads(ti):
            xT = xTs[ti % 2]
            xk = "dxT%d" % (ti % 2)
            to = ti * T
            S.dma(xT[:], k.H1T[:, :, to:to + T], w=[(xk, kc) for kc in range(8)])
            S.dma(oaT[:], k.OAT[:, :, to:to + T].rearrange("j p t -> p j t"), w=["oaT"])
            S.dma(obT[:], k.OBT[:, :, to:to + T].rearrange("j p t -> p j t"), w=["obT"])
            S.dma(sga[:], k.SGA[:, :, to:to + T].rearrange("j p t -> p j t"), w=["sga"])
            S.dma(sgb[:], k.SGB[:, :, to:to + T].rearrange("j p t -> p j t"), w=["sgb"])
            S.dma(ptoks[ti % 2][:], k.pc[to:to + T, :].rearrange("(s p) d -> p s d", p=128), w=[("ptok", ti % 2)])

        def P1(ti):
            for half in range(2):
                wa, wak = load_w(k, "w_branch_a", 0, 4, half * 512, 512)
                wb_, wbk = load_w(k, "w_branch_b", 0, 4, half * 512, 512)
                for jj in range(4):
                    j = half * 4 + jj
                    ba = nbank(k)
                    bb = nbank(k)
                    for kc in range(4):
                        S.pe(lambda e, kc=kc, jj=jj, ba=ba, wa=wa: e.matmul(k.ps[ba][:], lhsT=wa[:, kc, jj * 128:(jj + 1) * 128], rhs=oaT[:, kc, :], start=(kc == 0), stop=(kc == 3)),
                             r=[wak, "oaT"], w=[("ps", ba)])
                    for kc in range(4):
                        S.pe(lambda e, kc=kc, jj=jj, bb=bb, wb_=wb_: e.matmul(k.ps[bb][:], lhsT=wb_[:, kc, jj * 128:(jj + 1) * 128], rhs=obT[:, kc, :], start=(kc == 0), stop=(kc == 3)),
                             r=[wbk, "obT"], w=[("ps", bb)])
                    S.dve(lambda e, j=j, ba=ba: e.tensor_tensor(out=sgt[0][:], in0=k.ps[ba][:], in1=sga[:, j, :], op=ALU.mult), r=[("ps", ba), "sga"], w=[("sgt", 0)])
                    S.dve(lambda e, j=j, bb=bb: e.tensor_tensor(out=sgt[1][:], in0=k.ps[bb][:], in1=sgb[:, j, :], op=ALU.mult), r=[("ps", bb), "sgb"], w=[("sgt", 1)])
                    S.pool(lambda e, j=j: e.tensor_tensor(out=xnB[:, j, :], in0=sgt[0][:], in1=sgt[1][:], op=ALU.add), r=[("sgt", 0), ("sgt", 1)], w=[("xnB", j)])
            xnk = [("xnB", kc) for kc in range(8)]
            for half in range(2):
                wo, wok = load_w(k, "w_out", 0, 8, half * 512, 512)
                for jj in range(4):
                    j = half * 4 + jj
                    b = nbank(k)
                    for kc in range(8):
                        S.pe(lambda e, kc=kc, jj=jj, b=b, wo=wo: e.matmul(k.ps[b][:], lhsT=wo[:, kc, jj * 128:(jj + 1) * 128], rhs=xnB[:, kc, :], start=(kc == 0), stop=(kc == 7)),
                             r=[wok] + xnk, w=[("ps", b)])
                    S.act(lambda e, j=j, b=b: e.copy(yT[:, j, :], k.ps[b][:]), r=[("ps", b)], w=[("yT", j)])

        def P2(ti, halves=(0, 1)):
            for kc in (range(2) if 0 in halves else ()):
                b = nbank(k)
                for s in range(4):
                    S.pe(lambda e, kc=kc, s=s, b=b: e.transpose(k.ps[b][:, s * 128:(s + 1) * 128], ptoks[ti % 2][:, s, kc * 128:(kc + 1) * 128], k.identf[:]),
                         r=[("ptok", ti % 2), "identf"], w=[("ps", b)])
                S.act(lambda e, kc=kc, b=b: e.copy(pT[:, kc, :], k.ps[b][:]), r=[("ps", b)], w=["pT"])
            xnk = [("xnB", kc) for kc in range(8)]
            for half in halves:
                S.dma(pw[:], k.wb["w_ple"][0:256, half * 512:(half + 1) * 512].rearrange("(kc p) n -> p kc n", p=128), r=wkeys(k, "w_ple", 0, 2), w=["pw"])
                S.dma(pg[:], k.wb["w_ple_gate"][:, half * 512:(half + 1) * 512].rearrange("(kc p) n -> p kc n", p=128), r=wkeys(k, "w_ple_gate", 0, 8), w=["pg"])
                wp, wpk, wg, wgk = pw, "pw", pg, "pg"
                for jj in range(4):
                    j = half * 4 + jj
                    be = nbank(k)
                    bg = nbank(k)
                    for kc in range(2):
                        S.pe(lambda e, kc=kc, jj=jj, be=be, wp=wp: e.matmul(k.ps[be][:], lhsT=wp[:, kc, jj * 128:(jj + 1) * 128], rhs=pT[:, kc, :], start=(kc == 0), stop=(kc == 1)),
                             r=[wpk, "pT"], w=[("ps", be)])
                    for kc in range(8):
                        S.pe(lambda e, kc=kc, jj=jj, bg=bg, wg=wg: e.matmul(k.ps[bg][:], lhsT=wg[:, kc, jj * 128:(jj + 1) * 128], rhs=xnB[:, kc, :], start=(kc == 0), stop=(kc == 7)),
                             r=[wgk] + xnk, w=[("ps", bg)])
                    sl = j % 2
                    S.act(lambda e, bg=bg, sl=sl: e.activation(sgt[sl][:], k.ps[bg][:], AF.Sigmoid), r=[("ps", bg)], w=[("sgt", sl)])
                    S.dve(lambda e, be=be, sl=sl, j=j: e.tensor_tensor(out=yT[:, j, :], in0=sgt[sl][:], in1=k.ps[be][:], op=ALU.mult),
                          r=[("ps", be), ("sgt", sl)], w=[("yT", j)])

        def O_thunks(ti):
            xT = xTs[ti % 2]
            xk = "dxT%d" % (ti % 2)
            to = ti * T
            th = []

            def tr(s, g2):
                b = nbank(k)
                for kk in range(4):
                    kc = g2 * 4 + kk
                    S.pe(lambda e, kc=kc, kk=kk: e.transpose(k.ps[b][:, kk * 128:(kk + 1) * 128], xT[:, kc, s * 128:(s + 1) * 128], k.identf[:]),
                         r=[(xk, kc), "identf"], w=[("ps", b)])
                S.act(lambda e: e.copy(xtok[:, s, g2 * 512:(g2 + 1) * 512], k.ps[b][:]), r=[("ps", b)], w=["xtok"])
            for s in range(4):
                for g2 in range(2):
                    th.append(lambda s=s, g2=g2: tr(s, g2))
            th.append(lambda: S.dma(k.out[to:to + T, :].rearrange("(s p) d -> p s d", p=128), xtok[:], r=["xtok"], w=[("out", ti)], q="pool"))
            return th

        def tail_thunks(ti):
            xT = xTs[ti % 2]
            xk = "dxT%d" % (ti % 2)
            th = chain_thunks(xT, xk, k.gH, 5, ("cast", xnB, "xnB"))
            th.append(lambda: P2(ti, (0,)))
            th.append(lambda: P2(ti, (1,)))
            th += chain_thunks(xT, xk, k.gT, 6, None)
            th += O_thunks(ti)
            if ti + 2 < NT:
                th.append(lambda: P1_loads(ti + 2))
            return th

        def c1_thunks(ti):
            xT = xTs[ti % 2]
            xk = "dxT%d" % (ti % 2)
            return chain_thunks(xT, xk, k.gT, 3, ("norm", 4, xnA, "xnA"))

        P1_loads(0)
        P1(0)
        for f in c1_thunks(0):
            f()
        ffn_gu(k, "w_ffn2_gu", xnA, "xnA")
        for ti in range(NT):
            if ti + 1 < NT:
                if ti == 0:
                    P1_loads(1)
                P1(ti + 1)
                ffn_down(k, "w_ffn2_down", yT, "yT", hooks=c1_thunks(ti + 1))
                ffn_gu(k, "w_ffn2_gu", xnA, "xnA", hooks=tail_thunks(ti))
            else:
                ffn_down(k, "w_ffn2_down", yT, "yT")
                for f in tail_thunks(ti):
                    f()


def _rel_bucket_np(n):
    n = np.maximum(n, 0)
    nf = np.maximum(n, 1).astype(np.float32)
    large = 16 + (np.log(nf / np.float32(16)) / np.float32(np.log(128 / 16)) * np.float32(16)).astype(np.int32)
    large = np.minimum(large, 31)
    return np.where(n < 16, n, large)


def _consts():
    c = {"ident": np.eye(128, dtype=np.float32)}
    c["cJ"] = np.ascontiguousarray(np.eye(128, dtype=np.float32)[::-1])
    j = np.arange(768)
    d = j - 255
    ohp = np.zeros((32, 768), np.float32)
    pos = d >= 0
    bk = _rel_bucket_np(np.maximum(d, 0))
    ohp[bk[pos], j[pos]] += 1.0
    ohp[31, j[pos]] -= 1.0
    c["cOhp"] = ohp
    cm = np.zeros((8, 768), np.float32)
    cm[:, ~pos] = NEG
    c["cCm"] = cm
    ka = np.zeros((32, NTOK), np.float32)
    ka[np.arange(NTOK) // 256, np.arange(NTOK)] = 1.0
    c["cKaug"] = ka
    oo = np.zeros((128, 16, 32), np.float32)
    for i in range(16):
        oo[:, i, 16 + i] = 1.0
    c["cOwnoh"] = oo
    s_ = np.arange(128)
    t_ = np.arange(128)
    same = (s_[:, None] // 64) == (t_[None, :] // 64)
    m2 = (same & (s_[:, None] <= t_[None, :])).astype(np.float32) - (same & ((s_[:, None] % 64) <= 31)).astype(np.float32)
    c["cM2"] = m2.astype(np.float32)
    sel = np.zeros((128, 4), np.float32)
    sel[(s_ < 64) & (s_ % 64 <= 31), 0] = 1
    sel[(s_ < 64) & (s_ % 64 > 31), 1] = 1
    sel[(s_ >= 64) & (s_ % 64 <= 31), 2] = 1
    sel[(s_ >= 64) & (s_ % 64 > 31), 3] = 1
    c["cSel"] = sel
    c["cTri"] = ((s_[:, None] % 64) <= np.arange(64)[None, :]).astype(np.float32)
    return c


def _gmask(half):
    g = np.zeros((128, 16, 32), np.float32)
    for i in range(16):
        g[:, i, 16 + i:] = -1e30
    if half == 0:
        g[:, :, 0:16] = -1e30
    return g


def make_in_maps(inputs):
    x = np.asarray(inputs["x"], np.float32)
    p = np.asarray(inputs["p"], np.float32)
    common = {n: np.ascontiguousarray(np.asarray(inputs[n], np.float32)[0]) for n, a, b in WSPEC}
    g = np.asarray(inputs["norm_gains"], np.float32)[0]
    common["gainsT"] = np.ascontiguousarray(g.reshape(7, 8, 128).transpose(2, 0, 1).reshape(128, 56))
    common["hg_norm_w"] = np.ascontiguousarray(np.asarray(inputs["hg_norm_w"], np.float32).reshape(1, 128))
    common["lb_param"] = np.ascontiguousarray(np.asarray(inputs["lb_param"], np.float32))
    common["rel_table"] = np.ascontiguousarray(np.asarray(inputs["rel_table"], np.float32))
    common.update(_consts())
    maps = []
    for c in range(8):
        b, half = c // 2, c % 2
        m = dict(common)
        xc = np.zeros((NTOK, D), np.float32)
        if half == 1:
            xc[:NCTX] = x[b, :NCTX]
        xc[NCTX:] = x[b, half * NOWN:(half + 1) * NOWN]
        m["xc"] = xc
        m["gmask"] = _gmask(half)
        m["pc"] = np.ascontiguousarray(p[0, b, half * NOWN:(half + 1) * NOWN])
        maps.append(m)
    return maps


def kernel(**inputs):
    nc = build_nc(debug=("OAT", "OBT", "TVH"))
    maps = make_in_maps(inputs)
    res = run_bass_kernel_spmd(nc, maps, core_ids=list(range(8)))
    out = np.zeros((4, 8192, D), np.float32)
    for c in range(8):
        b, half = c // 2, c % 2
        out[b, half * NOWN:(half + 1) * NOWN] = res.results[c]["out"]
    return out
```
